# Optimizing a Trainium2 kernel written in Bass

```python
import math
import jax, jax.numpy as jnp
from jax import lax
import numpy as np

D_MODEL = 1024
BATCH = 2
SEQ = 8192
DEPTH = 2

REL_BUCKETS = 32
REL_MAX_EXACT = 16
REL_MAX_DIST = 128
ATTN_HEADS = 8

GLA_HEADS = 4
GLA_DK = 64
GLA_DV = 128
GLA_RANK = 16
GLA_TAU = 16.0
GLA_CHUNK = 64

SWA_HEADS = ATTN_HEADS
SWA_KV_HEADS = 2
SWA_DH = 64
SWA_WINDOW = 128

MOBA_HEADS = ATTN_HEADS
MOBA_KV_HEADS = 2
MOBA_DH = 128
MOBA_BLOCK = 256
MOBA_TOPK = 3
MOBA_Q_CHUNK = 32

EPS = 1e-6

A_WIDTH = GLA_HEADS * GLA_DV
B_WIDTH = SWA_HEADS * SWA_DH
C_WIDTH = MOBA_HEADS * MOBA_DH
EVEN_WIDTH = A_WIDTH + B_WIDTH
EVEN_SPLITS = (GLA_HEADS * GLA_DK, GLA_HEADS * GLA_DK, A_WIDTH, GLA_RANK, A_WIDTH,
               B_WIDTH, SWA_KV_HEADS * SWA_DH, SWA_KV_HEADS * SWA_DH, B_WIDTH)
EVEN_COLS = 2832
ODD_SPLITS = (C_WIDTH, MOBA_KV_HEADS * MOBA_DH, MOBA_KV_HEADS * MOBA_DH, C_WIDTH)
ODD_COLS = 2560
N_EVEN = (DEPTH + 1) // 2
N_ODD = DEPTH // 2

kernel_name = "hybrid_gla_swa_moba_gated"


def rmsnorm(x, g):
    xf = x.astype(jnp.float32)
    y = xf * lax.rsqrt(jnp.mean(xf * xf, axis=-1, keepdims=True) + EPS)
    return (y * g.astype(jnp.float32)).astype(x.dtype)


def split_cols(h, sizes):
    idx = np.cumsum(np.array(sizes))[:-1].tolist()
    return jnp.split(h, idx, axis=-1)


def t5_bucket(dist):
    n = jnp.maximum(dist, 0)
    nf = jnp.maximum(n, 1).astype(jnp.float32)
    large = REL_MAX_EXACT + (jnp.log(nf / REL_MAX_EXACT) / math.log(REL_MAX_DIST / REL_MAX_EXACT)
                             * (REL_BUCKETS - REL_MAX_EXACT)).astype(jnp.int32)
    large = jnp.minimum(large, REL_BUCKETS - 1)
    return jnp.where(n < REL_MAX_EXACT, n, large)


def gla_chunked(q, k, v, log_a):
    B_, S_, H_, DK = q.shape
    DV = v.shape[-1]
    N = S_ // GLA_CHUNK

    def chunks(t):
        return t.astype(jnp.float32).reshape(B_, N, GLA_CHUNK, H_, t.shape[-1]).transpose(0, 3, 1, 2, 4)

    qc, kc, vc, gc = chunks(q), chunks(k), chunks(v), chunks(log_a)
    b = jnp.cumsum(gc, axis=3)
    b_last = b[:, :, :, -1:, :]
    q_e = qc * jnp.exp(b) * (DK ** -0.5)
    k_e = kc * jnp.exp(-b)
    causal = jnp.tril(jnp.ones((GLA_CHUNK, GLA_CHUNK), dtype=bool))
    att = jnp.where(causal, jnp.einsum('bhncd,bhnsd->bhncs', q_e, k_e), 0.0)
    o_intra = jnp.einsum('bhncs,bhnse->bhnce', att, vc)
    kv_chunk = jnp.einsum('bhncd,bhnce->bhnde', kc * jnp.exp(b_last - b), vc)
    decay = jnp.exp(b_last[:, :, :, 0, :])

    def step(state, inp):
        dec, kv = inp
        return dec[..., None] * state + kv, state

    s0 = jnp.zeros((B_, H_, DK, DV), jnp.float32)
    _, s_before = lax.scan(step, s0, (jnp.moveaxis(decay, 2, 0), jnp.moveaxis(kv_chunk, 2, 0)))
    s_before = jnp.moveaxis(s_before, 0, 2)
    o = o_intra + jnp.einsum('bhncd,bhnde->bhnce', q_e, s_before)
    return o.transpose(0, 2, 3, 1, 4).reshape(B_, S_, H_, DV).astype(v.dtype)


def swa_attention(q, k, v, sinks, rel_bias):
    B_, S_, H_, DH = q.shape
    KVH = k.shape[2]
    G = H_ // KVH
    W = SWA_WINDOW
    NB = S_ // W
    qb = q.reshape(B_, NB, W, KVH, G, DH).transpose(0, 3, 4, 1, 2, 5)

    def band(t):
        tb = t.reshape(B_, NB, W, KVH, DH).transpose(0, 3, 1, 2, 4)
        prev = jnp.pad(tb[:, :, :-1], ((0, 0), (0, 0), (1, 0), (0, 0), (0, 0)))
        return jnp.concatenate([prev, tb], axis=3)

    kb, vb = band(k), band(v)
    qi = jnp.arange(W)[:, None]
    kj = jnp.arange(2 * W)[None, :]
    dist = qi + W - kj
    in_win = (dist >= 0) & (dist < W)
    blk = jnp.arange(NB)[:, None, None]
    mask = in_win[None] & ((blk > 0) | (kj[None] >= W))
    bias = rel_bias[t5_bucket(dist)].astype(jnp.float32)
    bias = bias.transpose(2, 0, 1).reshape(KVH, G, 1, W, 2 * W)
    logits = jnp.einsum('bkgnqd,bknsd->bkgnqs', qb, kb).astype(jnp.float32) * (DH ** -0.5) + bias
    logits = jnp.where(mask, logits, -jnp.inf)
    sink = jnp.broadcast_to(sinks.astype(jnp.float32).reshape(KVH, G, 1, 1, 1), logits.shape[:-1] + (1,))
    probs = jax.nn.softmax(jnp.concatenate([logits, sink], axis=-1), axis=-1)[..., :-1]
    o = jnp.einsum('bkgnqs,bknsd->bkgnqd', probs.astype(v.dtype), vb)
    return o.transpose(0, 3, 4, 1, 2, 5).reshape(B_, S_, H_ * DH)


def moba_attention(q, k, v, rel_bias):
    B_, S_, H_, DH = q.shape
    G = H_ // k.shape[2]
    S_pad = -(-S_ // MOBA_BLOCK) * MOBA_BLOCK
    pad = ((0, 0), (0, S_pad - S_), (0, 0), (0, 0))
    qh = jnp.pad(q, pad).transpose(0, 2, 1, 3)
    kh = jnp.repeat(jnp.pad(k, pad), G, axis=2).transpose(0, 2, 1, 3)
    vh = jnp.repeat(jnp.pad(v, pad), G, axis=2).transpose(0, 2, 1, 3)
    NBLK = S_pad // MOBA_BLOCK
    kblk = kh.reshape(B_, H_, NBLK, MOBA_BLOCK, DH)
    vblk = vh.reshape(B_, H_, NBLK, MOBA_BLOCK, DH)
    kmean = kblk.mean(axis=3)
    k_sel = min(MOBA_TOPK, NBLK)
    bi = jnp.arange(B_)[:, None, None, None]
    hi = jnp.arange(H_)[None, :, None, None]
    offs = jnp.arange(MOBA_BLOCK)
    blk_ids = jnp.arange(NBLK)
    scale = DH ** -0.5

    def one_chunk(c):
        start = c * MOBA_Q_CHUNK
        qc = lax.dynamic_slice_in_dim(qh, start, MOBA_Q_CHUNK, axis=2)
        tpos = start + jnp.arange(MOBA_Q_CHUNK)
        own = start // MOBA_BLOCK
        gate = jnp.einsum('bhqd,bhnd->bhqn', qc, kmean).astype(jnp.float32)
        gate = jnp.where(blk_ids < own, gate, -jnp.inf)
        _, idx = lax.top_k(gate, k_sel)
        kg = kblk[bi, hi, idx].reshape(B_, H_, MOBA_Q_CHUNK, k_sel * MOBA_BLOCK, DH)
        vg = vblk[bi, hi, idx].reshape(B_, H_, MOBA_Q_CHUNK, k_sel * MOBA_BLOCK, DH)
        kpos = (idx[..., None] * MOBA_BLOCK + offs).reshape(B_, H_, MOBA_Q_CHUNK, k_sel * MOBA_BLOCK)
        valid = jnp.broadcast_to((idx < own)[..., None], idx.shape + (MOBA_BLOCK,)).reshape(kpos.shape)
        bias_sel = rel_bias[t5_bucket(tpos[None, None, :, None] - kpos), hi].astype(jnp.float32)
        logit_sel = jnp.einsum('bhqd,bhqsd->bhqs', qc, kg).astype(jnp.float32) * scale + bias_sel
        logit_sel = jnp.where(valid, logit_sel, -jnp.inf)
        ko = lax.dynamic_slice_in_dim(kh, own * MOBA_BLOCK, MOBA_BLOCK, axis=2)
        vo = lax.dynamic_slice_in_dim(vh, own * MOBA_BLOCK, MOBA_BLOCK, axis=2)
        dist_own = tpos[:, None] - (own * MOBA_BLOCK + offs)[None, :]
        bias_own = rel_bias[t5_bucket(dist_own)].astype(jnp.float32).transpose(2, 0, 1)
        logit_own = jnp.einsum('bhqd,bhsd->bhqs', qc, ko).astype(jnp.float32) * scale + bias_own
        logit_own = jnp.where(dist_own >= 0, logit_own, -jnp.inf)
        probs = jax.nn.softmax(jnp.concatenate([logit_sel, logit_own], axis=-1), axis=-1)
        p_sel = probs[..., :k_sel * MOBA_BLOCK]
        p_own = probs[..., k_sel * MOBA_BLOCK:]
        o = jnp.einsum('bhqs,bhqsd->bhqd', p_sel, vg) + jnp.einsum('bhqs,bhsd->bhqd', p_own, vo)
        return o.astype(v.dtype)

    out = lax.map(one_chunk, jnp.arange(S_pad // MOBA_Q_CHUNK))
    out = out.transpose(1, 0, 3, 2, 4).reshape(B_, S_pad, H_ * DH)
    return out[:, :S_]


def even_layer(hn, w_in, gla_w_up, gla_b_up, gla_gain, sinks, w_out, rel_bias):
    B_, S_, _ = hn.shape
    proj = hn @ w_in
    aq, ak, av, a_down, a_gate, bq, bk, bv, b_gate = split_cols(proj, EVEN_SPLITS)
    log_a = jax.nn.log_sigmoid((a_down @ gla_w_up + gla_b_up).astype(jnp.float32)) / GLA_TAU
    oa = gla_chunked(aq.reshape(B_, S_, GLA_HEADS, GLA_DK), ak.reshape(B_, S_, GLA_HEADS, GLA_DK),
                     av.reshape(B_, S_, GLA_HEADS, GLA_DV), log_a.reshape(B_, S_, GLA_HEADS, GLA_DK))
    oa = rmsnorm(oa, gla_gain).reshape(B_, S_, A_WIDTH) * jax.nn.silu(a_gate)
    ob = swa_attention(bq.reshape(B_, S_, SWA_HEADS, SWA_DH), bk.reshape(B_, S_, SWA_KV_HEADS, SWA_DH),
                       bv.reshape(B_, S_, SWA_KV_HEADS, SWA_DH), sinks, rel_bias) * jax.nn.silu(b_gate)
    return jnp.concatenate([oa, ob], axis=-1) @ w_out


def odd_layer(hn, w_in, w_out, rel_bias):
    B_, S_, _ = hn.shape
    cq, ck, cv, c_gate = split_cols(hn @ w_in, ODD_SPLITS)
    oc = moba_attention(cq.reshape(B_, S_, MOBA_HEADS, MOBA_DH), ck.reshape(B_, S_, MOBA_KV_HEADS, MOBA_DH),
                        cv.reshape(B_, S_, MOBA_KV_HEADS, MOBA_DH), rel_bias)
    return (oc * jax.nn.silu(c_gate)) @ w_out


def setup_inputs(seed: int = 0) -> dict:
    key = jax.random.key(seed)
    ks = jax.random.split(key, 12)
    f32 = jnp.float32
    x = jax.random.normal(ks[0], (BATCH, SEQ, D_MODEL), f32)
    norm_gain = 1.0 + 0.02 * jax.random.normal(ks[1], (DEPTH, D_MODEL), f32)
    final_gain = 1.0 + 0.02 * jax.random.normal(ks[2], (D_MODEL,), f32)
    rel_bias = 0.2 * jax.random.normal(ks[3], (REL_BUCKETS, ATTN_HEADS), f32)
    w_in_even = jax.random.normal(ks[4], (N_EVEN, D_MODEL, EVEN_COLS), f32) * D_MODEL ** -0.5
    gla_w_up = jax.random.normal(ks[5], (N_EVEN, GLA_RANK, GLA_HEADS * GLA_DK), f32) * GLA_RANK ** -0.5
    gla_b_up = 0.1 * jax.random.normal(ks[6], (N_EVEN, GLA_HEADS * GLA_DK), f32)
    gla_norm_gain = 1.0 + 0.02 * jax.random.normal(ks[7], (N_EVEN, GLA_DV), f32)
    swa_sinks = 0.5 * jax.random.normal(ks[8], (N_EVEN, SWA_HEADS), f32)
    w_out_even = jax.random.normal(ks[9], (N_EVEN, EVEN_WIDTH, D_MODEL), f32) * EVEN_WIDTH ** -0.5
    w_in_odd = jax.random.normal(ks[10], (N_ODD, D_MODEL, ODD_COLS), f32) * D_MODEL ** -0.5
    w_out_odd = jax.random.normal(ks[11], (N_ODD, C_WIDTH, D_MODEL), f32) * C_WIDTH ** -0.5
    return {"x": x, "norm_gain": norm_gain, "final_gain": final_gain, "rel_bias": rel_bias,
            "w_in_even": w_in_even, "gla_w_up": gla_w_up, "gla_b_up": gla_b_up,
            "gla_norm_gain": gla_norm_gain, "swa_sinks": swa_sinks, "w_out_even": w_out_even,
            "w_in_odd": w_in_odd, "w_out_odd": w_out_odd}


def reference(x, norm_gain, final_gain, rel_bias, w_in_even, gla_w_up, gla_b_up, gla_norm_gain,
              swa_sinks, w_out_even, w_in_odd, w_out_odd):
    h = x
    for layer in range(DEPTH):
        hn = rmsnorm(h, norm_gain[layer])
        if layer % 2 == 0:
            i = layer // 2
            h = h + even_layer(hn, w_in_even[i], gla_w_up[i], gla_b_up[i], gla_norm_gain[i],
                               swa_sinks[i], w_out_even[i], rel_bias)
        else:
            i = layer // 2
            h = h + odd_layer(hn, w_in_odd[i], w_out_odd[i], rel_bias)
    return rmsnorm(h, final_gain)
```

```python
import contextlib
import math
import numpy as np
import ml_dtypes
import concourse.bass as bass
import concourse.mybir as mybir
from concourse.bass_utils import run_bass_kernel_spmd

F32 = mybir.dt.float32
BF16 = mybir.dt.bfloat16
AF = mybir.ActivationFunctionType
ALU = mybir.AluOpType
AX = mybir.AxisListType

NCORES = 8
SEG = 2048
NT = SEG // 128
EPS = 1e-6
CH = 30000
NDMA = 40
NDMA_SW = 12


class Sched:
    CE = ("pe", "act", "dve", "pool")

    def __init__(self, nc):
        self.nc = nc
        self.ops = {e: [] for e in self.CE + ("sp",)}
        self.cnt = {e: 0 for e in self.CE}
        self.res = {}
        self.clk = {}
        self.know = {e: {} for e in self.CE + ("sp",)}
        self.ndma = 0
        self.ndma_q = {}
        self.slots = set()
        self.final = []
        self.dma_toks = []
        self.pfx = ""
        self.excl = set()
        self.bank = {}

    def _need(self, eng, tok, waits):
        if tok is None:
            return
        k = self.know[eng]
        key = (tok[0], tok[1])
        if k.get(key, 0) >= tok[2]:
            return
        waits.append(tok)
        for kk, v in self.clk[tok].items():
            if k.get(kk, 0) < v:
                k[kk] = v

    def _dep(self, eng, tok, waits):
        if tok is None:
            return
        if tok[0] == "c" and tok[1] == "pe" and eng == "pe":
            return
        self._need(eng, tok, waits)

    def op(self, eng, fn, r=(), w=(), dma=False, final=False):
        waits = []
        for key in list(r) + list(w):
            if key in self.excl:
                for f, t in self.bank.setdefault(key, {}).items():
                    if f != eng:
                        self._need(eng, t, waits)
        for key in r:
            ent = self.res.get(key)
            if ent is not None:
                self._dep(eng, ent[0], waits)
        for key in w:
            ent = self.res.get(key)
            if ent is not None:
                self._dep(eng, ent[0], waits)
                for t in ent[1]:
                    self._dep(eng, t, waits)
        if dma:
            nslot, base = (NDMA, 0) if eng == "sp" else (NDMA_SW, 100)
            n = self.ndma_q.get(eng, 0)
            self.ndma_q[eng] = n + 1
            slot = base + n % nslot
            val = 16 * (n // nslot + 1)
            if val > 16:
                self._need(eng, ("d", slot, val - 16), waits)
            self.ndma += 1
            self.slots.add(slot)
            tok = ("d", slot, val)
            self.dma_toks.append(tok)
        else:
            self.cnt[eng] += 1
            tok = ("c", eng, self.cnt[eng])
        clk = dict(self.know[eng])
        clk[(tok[0], tok[1])] = tok[2]
        self.clk[tok] = clk
        self.ops[eng].append((waits, fn, tok))
        for key in r:
            ent = self.res.setdefault(key, [None, []])
            ent[1].append(tok)
        for key in w:
            self.res[key] = [tok, []]
        for key in list(r) + list(w):
            if key in self.excl:
                self.bank[key][eng] = tok
        if final:
            self.final.append(tok)
        return tok

    def cc(self, fn):
        self.cnt["pool"] += 1
        tok = ("c", "pool", self.cnt["pool"])
        clk = dict(self.know["pool"])
        clk[("c", "pool")] = tok[2]
        self.clk[tok] = clk
        self.ops["pool"].append(([], fn, tok))
        w = []
        self._need("pool", tok, w)
        self.ops["pool"].append((w, None, None))
        return tok

    def barrier(self):
        toks = [("c", e, self.cnt[e]) for e in self.CE if self.cnt[e] > 0]
        last = {}
        for t in self.dma_toks:
            last[t[1]] = t
        toks += list(last.values())
        for eng in self.CE + ("sp",):
            waits = []
            for t in toks:
                self._need(eng, t, waits)
            if waits:
                self.ops[eng].append((waits, None, None))
        self.dma_toks = []

    def emit(self):
        nc = self.nc
        fw = []
        for t in self.final:
            self._need("sp", t, fw)
        with contextlib.ExitStack() as es:
            sems = {}
            for e in self.CE:
                n = (self.cnt[e] + CH - 1) // CH
                for i in range(max(n, 1)):
                    sems[(e, i)] = es.enter_context(nc.semaphore(f"{self.pfx}s_{e}{i}"))
            for i in sorted(self.slots):
                sems[("d", i)] = es.enter_context(nc.semaphore(f"{self.pfx}s_d{i}"))
            block = es.enter_context(nc.Block())

            def semval(tok):
                if tok[0] == "c":
                    n = tok[2] - 1
                    return sems[(tok[1], n // CH)], n % CH + 1
                return sems[("d", tok[1])], tok[2]

            def run(ename, eng):
                for waits, fn, tok in self.ops[ename]:
                    for t in waits:
                        s, v = semval(t)
                        eng.wait_ge(s, v)
                    if fn is None:
                        continue
                    ins = fn(eng)
                    s, v = semval(tok)
                    ins.then_inc(s, 1 if tok[0] == "c" else 16)
                if ename == "sp":
                    for t in fw:
                        s, v = semval(t)
                        eng.wait_ge(s, v)

            @block.tensor
            def _(e):
                run("pe", e)

            @block.scalar
            def _(e):
                run("act", e)

            @block.vector
            def _(e):
                run("dve", e)

            @block.gpsimd
            def _(e):
                run("pool", e)

            @block.sync
            def _(e):
                run("sp", e)


def _bucket(d):
    n = np.maximum(d, 0)
    nf = np.maximum(n, 1).astype(np.float32)
    large = 16 + (np.log(nf / np.float32(16)) / np.float32(math.log(8.0)) * np.float32(16)).astype(np.int32)
    large = np.minimum(large, 31)
    return np.where(n < 16, n, large)


def _consts():
    s = np.arange(128)
    same = (s[:, None] // 64) == (s[None, :] // 64)
    tri = (same & (s[:, None] <= s[None, :])).astype(np.float32)
    blk = same.astype(np.float32)
    cind = ((s[:, None] // 64) == np.arange(2)[None, :]).astype(np.float32)
    maskbd = np.tile(tri, (1, 4)).astype(ml_dtypes.bfloat16)
    onehot = np.zeros((32, 1024), np.float32)
    m = np.arange(255)
    for typ in range(4):
        d = m - 127 if typ in (0, 2) else m + 1
        if typ in (0, 1):
            valid = (d >= 0) & (d < 128)
        else:
            valid = d >= 0
        b = _bucket(d)
        for mm in range(255):
            if valid[mm]:
                onehot[b[mm], typ * 256 + mm] = 1.0
    return {
        "c_ident": np.eye(128).astype(ml_dtypes.bfloat16),
        "c_tri": tri, "c_blk": blk, "c_cind": cind, "c_maskbd": maskbd,
        "c_onehot": onehot,
        "c_ones1": np.ones((1, 128), np.float32),
    }


def _core_tables(seg):
    own = 8 * seg + np.arange(NT) // 2
    n = np.arange(32)
    valid = (n[None, :] < own[:, None]).astype(np.float32)
    addm = np.where(valid > 0, 0.0, -1e30).astype(np.float32)
    prev = (n[None, :] == (own[:, None] - 1)).astype(np.float32)
    rep = lambda a: np.ascontiguousarray(np.broadcast_to(a.reshape(1, -1), (128, a.size))).astype(np.float32)
    flags = np.zeros(16, np.float32)
    flags[0] = 1.0 if seg > 0 else 0.0
    for sp in range(4):
        flags[4 + sp] = 1.0 if sp < seg else 0.0
        flags[8 + sp] = 1.0 if sp == seg - 1 else 0.0
    return {"t_valid": rep(valid), "t_addm": rep(addm), "t_prev": rep(prev), "t_flags": rep(flags)}


class Ctx:
    pass


_UID = [0]


def _mk(nc, es, dcache=None):
    C = Ctx()
    C.nc = nc
    C.S = Sched(nc)
    _UID[0] += 1
    C.pfx = f"u{_UID[0]}_"
    C.S.pfx = C.pfx
    dcache = {} if dcache is None else dcache
    C.stack = [es]
    C.sb = lambda n, s, d=F32: C.stack[-1].enter_context(nc.sbuf_tensor(C.pfx + "S_" + n, list(s), d))

    @contextlib.contextmanager
    def scope():
        with contextlib.ExitStack() as es2:
            C.stack.append(es2)
            try:
                yield
            finally:
                C.S.barrier()
                C.stack.pop()
    C.scope = scope
    C.ps = lambda n, s, d=F32: C.stack[-1].enter_context(nc.psum_tensor(C.pfx + "P_" + n, list(s), d))
    def din(n, s, d=F32):
        if n not in dcache:
            dcache[n] = nc.dram_tensor(n, list(s), d, kind="ExternalInput").ap()
        return dcache[n]
    C.din = din
    C.dout = lambda n, s, d=F32: nc.dram_tensor(n, list(s), d, kind="ExternalOutput").ap()
    return C


def _load(C, dst, src, key, eng="sp"):
    return C.S.op(eng, lambda e: e.dma_start(out=dst, in_=src), w=[key], dma=True)


def _load_consts(C, names):
    shapes = {"c_ident": ([128, 128], BF16), "c_tri": ([128, 128], F32), "c_blk": ([128, 128], F32),
              "c_cind": ([128, 2], F32), "c_maskbd": ([128, 512], BF16), "c_onehot": ([32, 1024], F32),
              "c_ones1": ([1, 128], F32)}
    out = {}
    for n in names:
        shp, dt = shapes[n]
        d = C.din(n, shp, dt)
        t = C.sb("sb_" + n, shp, dt)
        _load(C, t[:], d, n)
        out[n] = t
    return out


def _load_weights(C, io):
    S = C.S
    W = {"wb": C.sb("wE", [128, 8, 2832], BF16), "wq": C.sb("wq", [128, 8, 512], BF16),
         "wo": C.sb("wO", [128, 8, 1024], BF16), "w1": C.sb("w1", [128, 8, 2560], BF16),
         "g0": C.sb("g0rep", [128, 1024]), "g1": C.sb("g1rep", [128, 1024])}
    _load(C, W["g0"][:], io["gain0_rep"], "grep")
    _load(C, W["g1"][:], io["gain1_rep"], "grep")
    first = [("wE", k) for k in range(8)]

    def ld(dst, srcap, key):
        S.op("pool", lambda e: e.dma_start(out=dst, in_=srcap), r=([] if key in first else first), w=[key], dma=True)
    we = io["w_in_even"]
    for k in range(8):
        ld(W["wb"][:, k, 256:1040], we[k * 128:(k + 1) * 128, 256:1040], ("wE", k))
    for k in range(8):
        ld(W["wb"][:, k, 0:256], we[k * 128:(k + 1) * 128, 0:256], ("wE0", k))
        ld(W["wb"][:, k, 1040:1552], we[k * 128:(k + 1) * 128, 1040:1552], ("wE1", k))
    for k in range(8):
        ld(W["wb"][:, k, 1552:2832], we[k * 128:(k + 1) * 128, 1552:2832], ("wE2", k))
        for a in range(4):
            ld(W["wq"][:, k, a * 128:(a + 1) * 128].rearrange("p (g d) -> p g d", g=2),
               we[k * 128:(k + 1) * 128, 1552:2064].rearrange("p (g a d) -> p g a d", g=2, a=4)[:, :, a, :], ("wq", k, a))
    for k in range(8):
        ld(W["wo"][:, k, :], io["w_out_even"][k * 128:(k + 1) * 128, :], ("wO", k))
    for k in range(8):
        ld(W["w1"][:, k, :], io["w_in_odd"][k * 128:(k + 1) * 128, :], ("w1", k))
    return W


def _norm_transpose(C, src, srckey, bufs, ident, tag, grep=None):
    S = C.S
    junk, ss, rstd, xn, pT, xnT = (bufs[k] for k in ("junk", "ss", "rstd", "xn", "pT", "xnT"))
    S.op("act", lambda e: e.activation(out=junk[:], in_=src, func=AF.Square, accum_out=ss[:]),
         r=[srckey], w=[tag + "junk", tag + "ss"])
    S.op("act", lambda e: e.activation(out=ss[:], in_=ss[:], func=AF.Ln, scale=1.0 / 1024, bias=EPS),
         r=[tag + "ss"], w=[tag + "ss"])
    S.op("act", lambda e: e.activation(out=rstd[:], in_=ss[:], func=AF.Exp, scale=-0.5), r=[tag + "ss"], w=[tag + "rstd"])
    if grep is None:
        S.op("dve", lambda e: e.tensor_scalar(out=xn[:], in0=src, scalar1=rstd[:], scalar2=None, op0=ALU.mult),
             r=[srckey, tag + "rstd"], w=[tag + "xn"])
    else:
        S.op("dve", lambda e: e.scalar_tensor_tensor(out=xn[:], in0=src, scalar=rstd[:], in1=grep[:], op0=ALU.mult, op1=ALU.mult),
             r=[srckey, tag + "rstd", "grep"], w=[tag + "xn"])
    for k in range(8):
        S.op("pe", lambda e, k=k: e.transpose(out=pT[:, k, :], in_=xn[:, k * 128:(k + 1) * 128], identity=ident[:]),
             r=[tag + "xn", "c_ident"], w=["pTbank"])
    S.op("act", lambda e: e.copy(out=xnT[:], in_=pT[:]), r=["pTbank"], w=[tag + "xnT"])


def _silu(C, out, pin, pkey, tmp, tkey, okey):
    S = C.S
    S.op("act", lambda e: e.activation(out=tmp, in_=pin, func=AF.Exp, scale=-1.0), r=[pkey], w=[tkey])
    S.op("act", lambda e: e.activation(out=tmp, in_=tmp, func=AF.Ln, bias=1.0), r=[tkey], w=[tkey])
    S.op("act", lambda e: e.activation(out=tmp, in_=tmp, func=AF.Exp, scale=-1.0), r=[tkey], w=[tkey])
    S.op("dve", lambda e: e.tensor_tensor(out=out, in0=pin, in1=tmp, op=ALU.mult), r=[pkey, tkey], w=[okey])


def _mm8(C, out, lhs_fn, rhs_fn, rkeys, wkey):
    for k in range(8):
        C.S.op("pe", lambda e, k=k: e.matmul(out, lhsT=lhs_fn(k), rhs=rhs_fn(k), start=(k == 0), stop=(k == 7)),
               r=rkeys(k), w=[wkey])


class PsPool:
    def __init__(self, C, n, name):
        self.t = [C.ps(f"{name}{i}", [128, 512], F32) for i in range(n)]
        self.free_list = list(range(n))
        self.name = name
        for i in range(n):
            C.S.excl.add((name, i))
        C.S.excl.add("pTbank")
        C.S.excl.add("pT2")

    def alloc(self):
        assert self.free_list, "PSUM pool exhausted (program-order liveness bug)"
        j = self.free_list.pop(0)
        return self.t[j], (self.name, j)

    def free(self, key):
        assert key[1] not in self.free_list
        self.free_list.append(key[1])


def _gla_tile(C, G, ti, pA, pAkey, vsrc, vkey, adT_key, full):
    S = C.S
    K = G["K"]
    pp = G["pp"]
    sfx = G.get("sfx", "")
    kx = lambda n: n + sfx
    pz, pzk = pp.alloc()
    S.op("pe", lambda e: e.matmul(pz[:, 0:256], lhsT=G["adT"][:], rhs=G["wup"][:], start=True, stop=False),
         r=[adT_key, "wup"], w=[pzk])
    S.op("pe", lambda e: e.matmul(pz[:, 0:256], lhsT=K["c_ones1"][:], rhs=G["bup"][:], start=False, stop=True),
         r=["c_ones1", "bup"], w=[pzk])
    el = G["el"]
    S.op("act", lambda e: e.activation(out=el[:], in_=pz[:, 0:256], func=AF.Exp, scale=-1.0), r=[pzk], w=[kx("el")])
    pp.free(pzk)
    S.op("act", lambda e: e.activation(out=el[:], in_=el[:], func=AF.Ln, bias=1.0), r=[kx("el")], w=[kx("el")])
    yield
    pc, pck = pp.alloc()
    S.op("pe", lambda e: e.matmul(pc[:, 0:256], lhsT=K["c_tri"][:], rhs=el[:], start=True, stop=True),
         r=["c_tri", kx("el")], w=[pck])
    S.op("pe", lambda e: e.matmul(pc[:, 256:512], lhsT=K["c_blk"][:], rhs=el[:], start=True, stop=True),
         r=["c_blk", kx("el")], w=[pck])
    pd, pdk = pp.alloc()
    for h in range(4):
        S.op("pe", lambda e, h=h: e.matmul(pd[0:64, 2 * h:2 * h + 2], lhsT=el[:, h * 64:(h + 1) * 64], rhs=K["c_cind"][:],
                                           start=True, stop=True), r=[kx("el"), "c_cind"], w=[pdk])
    dec = G["dec"]
    S.op("act", lambda e: e.activation(out=dec[:], in_=pd[0:64, 0:8], func=AF.Exp, scale=-1.0 / 16), r=[pdk], w=[kx("dec")])
    if not full:
        S.op("dve", lambda e: e.tensor_tensor(out=G["lacc"][:], in0=G["lacc"][:], in1=pd[0:64, 0:8], op=ALU.add),
             r=[pdk, "lacc"], w=["lacc"])
    pp.free(pdk)
    yield
    enb, ebl, edec = G["enb"], G["ebl"], G["edec"]
    S.op("act", lambda e: e.activation(out=enb[:], in_=pc[:, 0:256], func=AF.Exp, scale=1.0 / 16), r=[pck], w=[kx("enb")])
    S.op("act", lambda e: e.activation(out=ebl[:], in_=pc[:, 256:512], func=AF.Exp, scale=-1.0 / 16), r=[pck], w=[kx("ebl")])
    if full:
        eb = G["eb"]
        S.op("act", lambda e: e.activation(out=eb[:], in_=pc[:, 0:256], func=AF.Exp, scale=-1.0 / 16), r=[pck], w=[kx("eb")])
    pp.free(pck)
    yield
    S.op("dve", lambda e: e.tensor_tensor(out=edec[:], in0=ebl[:], in1=enb[:], op=ALU.mult), r=[kx("enb"), kx("ebl")], w=[kx("edec")])
    kdec = G["kdec"]
    S.op("dve", lambda e: e.tensor_tensor(out=kdec[:], in0=pA[:, 256:512], in1=edec[:], op=ALU.mult),
         r=[pAkey, kx("edec")], w=[kx("kdec")])
    if full:
        qke = G["qke"]
        S.op("dve", lambda e: e.scalar_tensor_tensor(out=qke[:, 0:256], in0=pA[:, 0:256], scalar=0.125, in1=eb[:],
                                                     op0=ALU.mult, op1=ALU.mult), r=[pAkey, kx("eb")], w=[kx("qke")])
        S.op("dve", lambda e: e.tensor_tensor(out=qke[:, 256:512], in0=pA[:, 256:512], in1=enb[:], op=ALU.mult),
             r=[pAkey, kx("enb")], w=[kx("qke")])
        pT2, qkT = G["pT2"], G["qkT"]
        for j in range(8):
            S.op("pe", lambda e, j=j: e.transpose(out=pT2[0:64, j, :], in_=qke[:, j * 64:(j + 1) * 64], identity=K["c_ident"][:]),
                 r=[kx("qke"), "c_ident"], w=["pT2"])
        S.op("act", lambda e: e.copy(out=qkT[:], in_=pT2[0:64, :, :]), r=["pT2"], w=[kx("qkT")])
        yield
        patt, pattk = pp.alloc()
        for h in range(4):
            S.op("pe", lambda e, h=h: e.matmul(patt[:, h * 128:(h + 1) * 128], lhsT=qkT[:, 4 + h, :], rhs=qkT[:, h, :],
                                               start=True, stop=True), r=[kx("qkT")], w=[pattk])
        attm = G["attm"]
        S.op("dve", lambda e: e.tensor_tensor(out=attm[:], in0=patt[:], in1=K["c_maskbd"][:], op=ALU.mult),
             r=[pattk, "c_maskbd"], w=[kx("attm")])
        pp.free(pattk)
        yield
    S32 = G["S32"]

    def kv_mm(c):
        pkv, pkvk = pp.alloc()
        for h in range(4):
            S.op("pe", lambda e, h=h: e.matmul(pkv[0:64, h * 128:(h + 1) * 128],
                                               lhsT=kdec[c * 64:(c + 1) * 64, h * 64:(h + 1) * 64],
                                               rhs=vsrc[c * 64:(c + 1) * 64, h * 128:(h + 1) * 128],
                                               start=True, stop=True), r=[kx("kdec"), vkey], w=[pkvk])
        return pkv, pkvk

    def upd(c, pkv, pkvk):
        g = 2 * ti + c
        for h in range(4):
            S.op("dve", lambda e, h=h: e.scalar_tensor_tensor(
                out=S32[:, h * 128:(h + 1) * 128], in0=S32[:, h * 128:(h + 1) * 128],
                scalar=dec[:, 2 * h + c:2 * h + c + 1], in1=pkv[0:64, h * 128:(h + 1) * 128],
                op0=ALU.mult, op1=ALU.add), r=["S32", kx("dec"), pkvk], w=["S32"])
        pp.free(pkvk)
        if full:
            nb = G["Sbf"][(g + 1) % 2]
            S.op("pool", lambda e: e.tensor_copy(out=nb[:], in_=S32[:]), r=["S32"], w=[("Sbf", (g + 1) % 2)])

    k0 = kv_mm(0)
    upd(0, *k0)
    yield
    k1 = kv_mm(1)
    if full:
        yield
    po = pok = None
    if full:
        po, pok = pp.alloc()
        for h in range(4):
            S.op("pe", lambda e, h=h: e.matmul(po[:, h * 128:(h + 1) * 128], lhsT=attm[:, h * 128:(h + 1) * 128],
                                               rhs=vsrc[:, h * 128:(h + 1) * 128], start=True, stop=False),
                 r=[kx("attm"), vkey], w=[pok])
            for c in range(2):
                g = 2 * ti + c
                sb_ = G["Sbf"][g % 2]
                S.op("pe", lambda e, h=h, c=c, sb_=sb_: e.matmul(po[c * 64:(c + 1) * 64, h * 128:(h + 1) * 128],
                                                                 lhsT=qkT[:, h, c * 64:(c + 1) * 64],
                                                                 rhs=sb_[:, h * 128:(h + 1) * 128],
                                                                 start=False, stop=True),
                     r=[kx("qkT"), ("Sbf", g % 2)], w=[pok])
    if full:
        yield
    upd(1, *k1)
    return po, pok


def _gla_bufs(C, K, pp, full):
    G = {"K": K, "pp": pp}
    G["adT"] = C.sb("g_adT", [16, 128])
    G["wup"] = C.sb("g_wup", [16, 256])
    G["bup"] = C.sb("g_bup", [1, 256])
    G["el"] = C.sb("g_el", [128, 256])
    G["dec"] = C.sb("g_dec", [64, 8])
    G["enb"] = C.sb("g_enb", [128, 256])
    G["ebl"] = C.sb("g_ebl", [128, 256])
    G["edec"] = C.sb("g_edec", [128, 256])
    G["kdec"] = C.sb("g_kdec", [128, 256], BF16)
    G["S32"] = C.sb("g_S32", [64, 512])
    if full:
        G["eb"] = C.sb("g_eb", [128, 256])
        G["qke"] = C.sb("g_qke", [128, 512], BF16)
        G["qkT"] = C.sb("g_qkT", [64, 8, 128], BF16)
        G["attm"] = C.sb("g_attm", [128, 512], BF16)
        G["Sbf"] = [C.sb("g_Sbf0", [64, 512], BF16), C.sb("g_Sbf1", [64, 512], BF16)]
    else:
        G["lacc"] = C.sb("g_lacc", [64, 8])
    return G


def phase_a(C, io, W, T):
    S = C.S
    K = _load_consts(C, ["c_ident", "c_tri", "c_blk", "c_cind", "c_ones1", "c_onehot"])
    wb = W["wb"]
    pp = PsPool(C, 6, "pp")
    xt = [C.sb(f"xt{i}", [128, 1024]) for i in range(3)]
    _load(C, xt[0][:], io["x"][0:128, :], ("xt", 0))
    ecrep, Dc = T["ecrep"], T["Dc"]
    _load(C, ecrep[:], io["rb31_rep"], "ecrep")
    S.op("act", lambda e: e.activation(out=ecrep[:], in_=ecrep[:], func=AF.Exp), r=["ecrep"], w=["ecrep"])

    def per_head(h, t32, tk):
        S.op("dve", lambda e: e.tensor_scalar(out=Dc[:, h, :], in0=t32[:, 3, :], scalar1=ecrep[:, h:h + 1], scalar2=None,
                                              op0=ALU.subtract), r=[tk, "ecrep"], w=["Dc"])
    tables = _bias_tables(C, K, io, pp, T["EB"], per_head)
    G0 = _gla_bufs(C, K, pp, False)
    _load(C, G0["wup"][:], io["gla_w_up"], "wup")
    _load(C, G0["bup"][:], io["gla_b_up"], "bup")
    S.op("dve", lambda e: e.memset(G0["S32"][:], 0.0), w=["S32"])
    S.op("dve", lambda e: e.memset(G0["lacc"][:], 0.0), w=["lacc"])
    G1 = dict(G0)
    for n, shp, dt in [("adT", [16, 128], F32), ("el", [128, 256], F32), ("dec", [64, 8], F32), ("enb", [128, 256], F32),
                       ("ebl", [128, 256], F32), ("edec", [128, 256], F32), ("kdec", [128, 256], BF16)]:
        G1[n] = C.sb("g1_" + n, shp, dt)
    G0["sfx"], G1["sfx"] = "_0", "_1"
    G = G0
    Gs = [G0, G1]
    pT = C.ps("pT", [128, 8, 128], BF16)
    nbs = [{"junk": C.sb(f"junk{p}", [128, 1024], BF16), "ss": C.sb(f"ss{p}", [128, 1]), "rstd": C.sb(f"rstd{p}", [128, 1]),
            "xn": C.sb(f"xn{p}", [128, 1024], BF16), "pT": pT, "xnT": C.sb(f"xnT{p}", [128, 8, 128], BF16)} for p in range(2)]
    vbfs = [C.sb(f"vbf{p}", [128, 512], BF16) for p in range(2)]

    def A(ti):
        par = ti % 2
        nb, vbf, Gp = nbs[par], vbfs[par], Gs[par]
        tag = f"a{par}"
        if ti + 1 < NT:
            _load(C, xt[(ti + 1) % 3][:], io["x"][(ti + 1) * 128:(ti + 2) * 128, :], ("xt", (ti + 1) % 3))
        _norm_transpose(C, xt[ti % 3][:], ("xt", ti % 3), nb, K["c_ident"], tag, grep=W["g0"])
        yield
        xnT = nb["xnT"]
        rk = lambda k: [tag + "xnT", ("wE", k)]
        pA, pAk = pp.alloc()
        _mm8(C, pA[:, 256:512], lambda k: xnT[:, k, :], lambda k: wb[:, k, 256:512], rk, pAk)
        yield
        pV, pVk = pp.alloc()
        _mm8(C, pV[:, 0:512], lambda k: xnT[:, k, :], lambda k: wb[:, k, 512:1024], rk, pVk)
        S.op("act", lambda e: e.copy(out=vbf[:], in_=pV[:, 0:512]), r=[pVk], w=["vbf" + tag])
        pp.free(pVk)
        yield
        pM, pMk = pp.alloc()
        _mm8(C, pM[0:16, 0:128], lambda k: wb[:, k, 1024:1040], lambda k: xnT[:, k, :], rk, pMk)
        S.op("dve", lambda e: e.tensor_copy(out=Gp["adT"][:], in_=pM[0:16, 0:128]), r=[pMk], w=["adT" + tag])
        pp.free(pMk)
        yield
        yield from _gla_tile(C, Gp, ti, pA, pAk, vbf, "vbf" + tag, "adT" + tag, False)
        pp.free(pAk)

    active, nxt = [A(0), tables], 1
    while active:
        for g in list(active):
            try:
                next(g)
            except StopIteration:
                active.remove(g)
        if sum(1 for g in active if g is not tables) < 2 and nxt < NT:
            active.append(A(nxt))
            nxt += 1
    S.op("sp", lambda e: e.dma_start(out=io["o_summ"][:, 0:512], in_=G["S32"][:]), r=["S32"], dma=True, final=True)
    S.op("sp", lambda e: e.dma_start(out=io["o_summ"][:, 512:520], in_=G["lacc"][:]), r=["lacc"], dma=True, final=True)


def _bcast_mid(ap, n):
    pat = [list(p) for p in ap.ap]
    return bass.AP(ap.tensor, ap.offset, [pat[0], [0, n]] + pat[1:])


def _bias_tables(C, K, io, pp, EB, per_head=None):
    S = C.S
    rb = C.sb("rb", [32, 8])
    _load(C, rb[:], io["rel_bias"], "rb")
    S.op("act", lambda e: e.activation(out=rb[:], in_=rb[:], func=AF.Exp), r=["rb"], w=["rb"])
    lh = [C.sb(f"lh{i}", [32, 128]) for i in range(2)]
    frep = [C.sb(f"frep{i}", [128, 1024]) for i in range(2)]
    T32 = [C.sb(f"T32_{i}", [128, 4, 128]) for i in range(2)]
    zt = io["zscr"]

    def produce(h):
        l = lh[h % 2]
        lk = ("lh", h % 2)
        S.op("dve", lambda e: e.tensor_copy(out=l[:], in_=rb[:, h:h + 1].to_broadcast([32, 128])), r=["rb"], w=[lk])
        fr = frep[h % 2]
        fk = ("frep", h % 2)
        for half in range(2):
            pf, pfk = pp.alloc()
            S.op("pe", lambda e, half=half, pf=pf: e.matmul(pf[:, 0:512], lhsT=l[:], rhs=K["c_onehot"][:, half * 512:(half + 1) * 512],
                                                            start=True, stop=True), r=[lk, "c_onehot"], w=[pfk])
            S.op("act", lambda e, half=half, pf=pf: e.copy(out=fr[:, half * 512:(half + 1) * 512], in_=pf[:, 0:512]), r=[pfk], w=[fk])
            pp.free(pfk)
        S.op("sp", lambda e: e.dma_start(out=zt[h], in_=fr[:]), r=[fk], w=[("z", h)], dma=True)

    def consume(h):
        src_ap = bass.AP(zt.tensor, zt.offset + h * 128 * 1024 + 127, [[1023, 128], [256, 4], [1, 128]])
        t32 = T32[h % 2]
        tk = ("T32", h % 2)
        S.op("sp", lambda e: e.dma_start(out=t32[:], in_=src_ap), r=[("z", h)], w=[tk], dma=True)
        S.op("dve", lambda e: e.tensor_copy(out=EB[:, :, h, :], in_=t32[:]), r=[tk], w=["EB"])
        if per_head is not None:
            per_head(h, t32, tk)

    for h in range(8):
        produce(h)
        yield
        if h >= 1:
            consume(h - 1)
            yield
    consume(7)
    yield


def phase_b(C, io, W, T):
    S = C.S
    K = _load_consts(C, ["c_ident", "c_tri", "c_blk", "c_cind", "c_ones1", "c_maskbd"])
    ident = K["c_ident"]
    pp = PsPool(C, 6, "pp")
    flags = C.sb("flags", [128, 16])
    _load(C, flags[:], io["t_flags"], "flags")
    wb, wo, w1, wq = W["wb"], W["wo"], W["w1"], W["wq"]
    EB = T["EB"]
    EBfirst = C.sb("EBfirst", [128, 8, 128], BF16)
    sinkexp = C.sb("sinkexp", [128, 8])
    ggain = C.sb("ggain", [128, 512])
    G = _gla_bufs(C, K, pp, True)
    G["pT2"] = C.ps("pT2", [128, 8, 128], BF16)
    S32 = G["S32"]
    if True:
        S.op("dve", lambda e: e.tensor_scalar(out=EBfirst[:], in0=EB[:, 1, :, :], scalar1=flags[:, 0:1], scalar2=None, op0=ALU.mult),
             r=["EB", "flags"], w=["EBfirst"])
        _load(C, sinkexp[:], io["sinks_rep"], "sinkexp")
        S.op("act", lambda e: e.activation(out=sinkexp[:], in_=sinkexp[:], func=AF.Exp), r=["sinkexp"], w=["sinkexp"])
        _load(C, ggain[:], io["gla_gain_rep"], "ggain")
        _load(C, G["wup"][:], io["gla_w_up"], "wup")
        _load(C, G["bup"][:], io["gla_b_up"], "bup")
        S.op("dve", lambda e: e.memset(S32[:], 0.0), w=["S32"])
        lall = C.sb("lall", [64, 4, 8])
        h1t = C.sb("h1t", [128, 1024])
        s1 = h1t[0:64, 0:512]
        S.op("sp", lambda e: e.dma_start(out=lall[:], in_=io["summ_all"][:, :, 512:520].rearrange("s p c -> p s c")),
             r=["cc1"], w=["lall"], dma=True)
        S.op("act", lambda e: e.activation(out=lall[:], in_=lall[:], func=AF.Exp, scale=-1.0 / 16), r=["lall"], w=["lall"])
        aseg = C.sb("aseg", [64, 4, 4])
        la4 = lall[:].rearrange("p s (h c) -> p s h c", c=2)
        S.op("dve", lambda e: e.tensor_tensor(out=aseg[:], in0=la4[:, :, :, 0], in1=la4[:, :, :, 1], op=ALU.mult), r=["lall"], w=["aseg"])
        for sp in range(4):
            m = flags[0:64, 4 + sp:5 + sp]
            S.op("sp", lambda e, sp=sp: e.dma_start(out=s1[:], in_=io["summ_all"][sp][:, 0:512]), r=["cc1"], w=["s1"], dma=True)
            S.op("dve", lambda e, sp=sp, m=m: e.tensor_scalar(out=aseg[:, sp, :], in0=aseg[:, sp, :], scalar1=-1.0, scalar2=m,
                                                              op0=ALU.add, op1=ALU.mult), r=["aseg", "flags"], w=["aseg"])
            S.op("dve", lambda e, sp=sp: e.tensor_scalar(out=aseg[:, sp, :], in0=aseg[:, sp, :], scalar1=1.0, scalar2=None, op0=ALU.add),
                 r=["aseg"], w=["aseg"])
            S.op("dve", lambda e, m=m: e.tensor_scalar(out=s1[:], in0=s1[:], scalar1=m, scalar2=None, op0=ALU.mult),
                 r=["s1", "flags"], w=["s1"])
            for h in range(4):
                S.op("dve", lambda e, sp=sp, h=h: e.scalar_tensor_tensor(
                    out=S32[:, h * 128:(h + 1) * 128], in0=S32[:, h * 128:(h + 1) * 128], scalar=aseg[:, sp, h:h + 1],
                    in1=s1[:, h * 128:(h + 1) * 128], op0=ALU.mult, op1=ALU.add), r=["S32", "aseg", "s1"], w=["S32"])
        S.op("pool", lambda e: e.tensor_copy(out=G["Sbf"][0][:], in_=S32[:]), r=["S32"], w=[("Sbf", 0)])

    xt = [C.sb(f"xt{i}", [128, 1024]) for i in range(3)]
    nb = {"junk": C.sb("junk", [128, 1024], BF16), "ss": C.sb("ss", [128, 1]), "rstd": C.sb("rstd", [128, 1]),
          "xn": C.sb("xn", [128, 1024], BF16), "pT": C.ps("pT", [128, 8, 128], BF16), "xnT": C.sb("xnT", [128, 8, 128], BF16)}
    nb1 = dict(nb)
    nb1["junk"] = C.sb("junk1", [128, 1024], BF16)
    stmp1 = C.sb("stmp1", [128, 512])
    stmp2 = C.sb("stmp2", [128, 512])
    nb1["ss"] = C.sb("ss1", [128, 1]); nb1["rstd"] = C.sb("rstd1", [128, 1])
    nb1["xn"] = C.sb("xn1", [128, 1024], BF16); nb1["xnT"] = C.sb("xn1T", [128, 8, 128], BF16)
    vbf = C.sb("vbf", [128, 512], BF16)
    KT = [C.sb(f"KT{i}", [128, 128], BF16) for i in range(2)]
    VA = [C.sb(f"VA{i}", [128, 2, 66], BF16) for i in range(2)]
    for i in range(2):
        S.op("pool", lambda e, i=i: e.memset(VA[i][:], 1.0), w=[("VA", i)])
    QT = C.sb("QT", [128, 4, 128], BF16)
    sgB = C.sb("sgB", [128, 512])
    stmp = C.sb("stmp", [128, 512])
    sgA = C.sb("sgA", [128, 512])
    gsg = C.sb("gsg", [128, 512])
    Ee = C.sb("Ee", [128, 512], BF16)
    Pt = [C.sb(f"Pt{i}", [128, 512], BF16) for i in range(2)]
    den = C.sb("den", [128, 4]); rden = C.sb("rden", [128, 4])
    og = [C.sb(f"og{i}", [128, 1024], BF16) for i in range(2)]
    ogT = C.sb("ogT", [128, 8, 128], BF16)
    ssA = C.sb("ssA", [128, 4]); rsA = C.sb("rsA", [128, 4])
    junkA = C.sb("junkA", [128, 128], BF16)
    QT1 = C.sb("L1QT", [128, 8, 128], BF16)
    KT1 = C.sb("L1KT", [128, 2, 128], BF16)
    V1 = C.sb("L1V", [128, 2, 130], BF16)
    S.op("pool", lambda e: e.memset(V1[:], 1.0), w=["V1"])
    sg1 = C.sb("sg1", [128, 1024], BF16)
    kmacc = C.sb("kmacc", [128, 2, 8])
    kmt = C.sb("kmt", [128, 2])
    pT = nb["pT"]

    def xsrc(i):
        return io["x_halo"] if i == 0 else io["x"][(i - 1) * 128:i * 128, :]

    def X(i):
        ti = i - 1
        if i + 1 <= NT:
            _load(C, xt[(i + 1) % 3][:], xsrc(i + 1), ("xt", (i + 1) % 3))
        xk = ("xt", i % 3)
        xti = xt[i % 3]
        _norm_transpose(C, xti[:], xk, nb, ident, "n0", grep=W["g0"])
        yield
        xnT = nb["xnT"]
        rk = lambda k: ["n0xnT", ("wE", k), ("wE0", k), ("wE1", k), ("wE2", k)]
        ogc = og[ti % 2] if ti >= 0 else None
        oka, okb = ("og", ti % 2, "a"), ("og", ti % 2, "b")

        def chain_s():
            pM, pMk = pp.alloc()
            _mm8(C, pM[:, 0:128], lambda k: xnT[:, k, :], lambda k: wb[:, k, 2192:2320], rk, pMk)
            _mm8(C, pM[:, 128:256], lambda k: wb[:, k, 2064:2192], lambda k: xnT[:, k, :], rk, pMk)
            va, kt_ = VA[i % 2], KT[i % 2]
            S.op("dve", lambda e: e.tensor_copy(out=va[:, :, 0:64], in_=pM[:, 0:128].rearrange("p (k d) -> p k d", k=2)),
                 r=[pMk], w=[("VA", i % 2)])
            S.op("dve", lambda e: e.tensor_copy(out=kt_[:], in_=pM[:, 128:256]), r=[pMk], w=[("KT", i % 2)])
            pp.free(pMk)
            yield
            if ti < 0:
                return
            pQ, pQk = pp.alloc()
            for a in range(4):
                _mm8(C, pQ[:, a * 128:(a + 1) * 128], lambda k, a=a: wq[:, k, a * 128:(a + 1) * 128],
                     lambda k: xnT[:, k, :], lambda k, a=a: ["n0xnT", ("wq", k, a)], pQk)
            S.op("act", lambda e: e.activation(out=QT[:], in_=pQ[:, 0:512].rearrange("p (a t) -> p a t", a=4), func=AF.Copy, scale=0.125),
                 r=[pQk], w=["QT"])
            pp.free(pQk)
            yield
            pG, pGk = pp.alloc()
            _mm8(C, pG[:, 0:512], lambda k: xnT[:, k, :], lambda k: wb[:, k, 2320:2832], rk, pGk)
            _silu(C, sgB[:], pG[:, 0:512], pGk, stmp[:], "stmp", "sgB")
            pp.free(pGk)
            yield
            for kv in range(2):
                for kt in range(2):
                    ktile = KT[(i - 1 + kt) % 2]
                    pS, pSk = pp.alloc()
                    S.op("pe", lambda e, ktile=ktile, kv=kv, pS=pS: e.matmul(pS[:, 0:512], lhsT=ktile[kv * 64:(kv + 1) * 64, :],
                                                                            rhs=QT[kv * 64:(kv + 1) * 64, :, :], start=True, stop=True),
                         r=[("KT", (i - 1 + kt) % 2), "QT"], w=[pSk])
                    S.op("act", lambda e, pS=pS: e.activation(out=Ee[:], in_=pS[:, 0:512], func=AF.Exp), r=[pSk], w=["Ee"])
                    pp.free(pSk)
                    if kt == 0:
                        tab = EBfirst[:, kv * 4:(kv + 1) * 4, :] if ti == 0 else EB[:, 1, kv * 4:(kv + 1) * 4, :]
                        tk = "EBfirst" if ti == 0 else "EB"
                    else:
                        tab = EB[:, 0, kv * 4:(kv + 1) * 4, :]
                        tk = "EB"
                    S.op("dve", lambda e, kt=kt, tab=tab: e.tensor_tensor(out=Pt[kt][:].rearrange("p (a t) -> p a t", a=4),
                                                                          in0=Ee[:].rearrange("p (a t) -> p a t", a=4), in1=tab, op=ALU.mult),
                         r=["Ee", tk], w=[("Pt", kt)])
                    yield
                pO, pOk = pp.alloc()
                for a in range(4):
                    for kt in range(2):
                        vat = VA[(i - 1 + kt) % 2]
                        S.op("pe", lambda e, a=a, kt=kt, vat=vat, pO=pO, kv=kv: e.matmul(
                            pO[:, a * 65:(a + 1) * 65], lhsT=Pt[kt][:, a * 128:(a + 1) * 128], rhs=vat[:, kv, 0:65],
                            start=(kt == 0), stop=(kt == 1)), r=[("Pt", kt), ("VA", (i - 1 + kt) % 2)], w=[pOk])
                pO3 = pO[:, 0:260].rearrange("p (a d) -> p a d", a=4)
                S.op("dve", lambda e, pO3=pO3, kv=kv: e.tensor_tensor(out=den[:], in0=pO3[:, :, 64], in1=sinkexp[:, kv * 4:(kv + 1) * 4], op=ALU.add),
                     r=[pOk, "sinkexp"], w=["den"])
                S.op("dve", lambda e: e.reciprocal(out=rden[:], in_=den[:]), r=["den"], w=["rden"])
                for a in range(4):
                    h = kv * 4 + a
                    S.op("dve", lambda e, a=a, h=h, pO3=pO3: e.scalar_tensor_tensor(
                        out=ogc[:, 512 + h * 64:512 + (h + 1) * 64], in0=pO3[:, a, 0:64], scalar=rden[:, a:a + 1],
                        in1=sgB[:, h * 64:(h + 1) * 64], op0=ALU.mult, op1=ALU.mult), r=[pOk, "rden", "sgB"], w=[okb])
                pp.free(pOk)
                yield

        def chain_g():
            if ti < 0:
                return
            pD, pDk = pp.alloc()
            _mm8(C, pD[0:16, 0:128], lambda k: wb[:, k, 1024:1040], lambda k: xnT[:, k, :], rk, pDk)
            S.op("dve", lambda e: e.tensor_copy(out=G["adT"][:], in_=pD[0:16, 0:128]), r=[pDk], w=["adT"])
            pp.free(pDk)
            pV, pVk = pp.alloc()
            _mm8(C, pV[:, 0:512], lambda k: xnT[:, k, :], lambda k: wb[:, k, 512:1024], rk, pVk)
            S.op("act", lambda e: e.copy(out=vbf[:], in_=pV[:, 0:512]), r=[pVk], w=["vbf"])
            pp.free(pVk)
            yield
            pA, pAk = pp.alloc()
            _mm8(C, pA[:, 0:512], lambda k: xnT[:, k, :], lambda k: wb[:, k, 0:512], rk, pAk)
            yield
            gla = _gla_tile(C, G, ti, pA, pAk, vbf, "vbf", "adT", True)
            res = None
            step = 0
            while True:
                try:
                    next(gla)
                except StopIteration as stop:
                    res = stop.value
                    break
                step += 1
                if step == 2:
                    pG2, pG2k = pp.alloc()
                    _mm8(C, pG2[:, 0:512], lambda k: xnT[:, k, :], lambda k: wb[:, k, 1040:1552], rk, pG2k)
                    _silu(C, sgA[:], pG2[:, 0:512], pG2k, stmp2[:], "stmp2", "sgA")
                    pp.free(pG2k)
                    S.op("pool", lambda e: e.tensor_tensor(out=gsg[:], in0=sgA[:], in1=ggain[:], op=ALU.mult), r=["sgA", "ggain"], w=["gsg"])
                yield
            po, pok = res
            pp.free(pAk)
            for h in range(4):
                S.op("act", lambda e, h=h: e.activation(out=junkA[:], in_=po[:, h * 128:(h + 1) * 128], func=AF.Square,
                                                        accum_out=ssA[:, h:h + 1]), r=[pok], w=["junkA", "ssA"])
            S.op("act", lambda e: e.activation(out=ssA[:], in_=ssA[:], func=AF.Ln, scale=1.0 / 128, bias=EPS), r=["ssA"], w=["ssA"])
            S.op("act", lambda e: e.activation(out=rsA[:], in_=ssA[:], func=AF.Exp, scale=-0.5), r=["ssA"], w=["rsA"])
            for h in range(4):
                S.op("dve", lambda e, h=h: e.scalar_tensor_tensor(
                    out=ogc[:, h * 128:(h + 1) * 128], in0=po[:, h * 128:(h + 1) * 128], scalar=rsA[:, h:h + 1],
                    in1=gsg[:, h * 128:(h + 1) * 128], op0=ALU.mult, op1=ALU.mult), r=[pok, "rsA", "gsg"], w=[oka])
            pp.free(pok)
            yield

        gens = [chain_g(), chain_s()]
        while gens:
            for g in list(gens):
                try:
                    next(g)
                except StopIteration:
                    gens.remove(g)
                yield

    def Y(ti):
        i = ti + 1
        xk = ("xt", i % 3)
        xti = xt[i % 3]
        ogc = og[ti % 2]
        oka, okb = ("og", ti % 2, "a"), ("og", ti % 2, "b")
        for k in range(8):
            S.op("pe", lambda e, k=k: e.transpose(out=pT[:, k, :], in_=ogc[:, k * 128:(k + 1) * 128], identity=ident[:]),
                 r=[oka, okb, "c_ident"], w=["pTbank"])
        S.op("act", lambda e: e.copy(out=ogT[:], in_=pT[:]), r=["pTbank"], w=["ogT"])
        yield
        for nbk in range(2):
            pH, pHk = pp.alloc()
            _mm8(C, pH[:, 0:512], lambda k: ogT[:, k, :], lambda k, nbk=nbk: wo[:, k, nbk * 512:(nbk + 1) * 512],
                 lambda k: ["ogT", ("wO", k)], pHk)
            S.op("dve", lambda e, nbk=nbk, pH=pH: e.tensor_tensor(out=h1t[:, nbk * 512:(nbk + 1) * 512], in0=pH[:, 0:512],
                                                                  in1=xti[:, nbk * 512:(nbk + 1) * 512], op=ALU.add),
                 r=[pHk, xk], w=["h1t"])
            pp.free(pHk)
            yield
        S.op("sp", lambda e: e.dma_start(out=io["o_h1"][ti * 128:(ti + 1) * 128, :], in_=h1t[:]), r=["h1t"], dma=True, final=True)
        _norm_transpose(C, h1t[:], "h1t", nb1, ident, "n1", grep=W["g1"])
        yield
        x1T = nb1["xnT"]
        rk1 = lambda k: ["n1xnT", ("w1", k)]
        for half in range(2):
            pq, pqk = pp.alloc()
            for a in range(4):
                h = half * 4 + a
                _mm8(C, pq[:, a * 128:(a + 1) * 128], lambda k, h=h: w1[:, k, h * 128:(h + 1) * 128], lambda k: x1T[:, k, :], rk1, pqk)
                yield
            S.op("act", lambda e, half=half, pq=pq: e.activation(out=QT1[:, half * 4:(half + 1) * 4, :],
                                                                 in_=pq[:, 0:512].rearrange("p (a t) -> p a t", a=4),
                                                                 func=AF.Copy, scale=128.0 ** -0.5), r=[pqk], w=["QT1"])
            pp.free(pqk)
        S.op("sp", lambda e: e.dma_start(out=io["o_qT"][ti], in_=QT1[:].rearrange("p h t -> p (h t)")), r=["QT1"], dma=True, final=True)
        pk, pkk = pp.alloc()
        for kv in range(2):
            _mm8(C, pk[:, kv * 128:(kv + 1) * 128], lambda k, kv=kv: w1[:, k, 1024 + kv * 128:1024 + (kv + 1) * 128],
                 lambda k: x1T[:, k, :], rk1, pkk)
        _mm8(C, pk[:, 256:512], lambda k: x1T[:, k, :], lambda k: w1[:, k, 1280:1536], rk1, pkk)
        S.op("dve", lambda e: e.tensor_copy(out=KT1[:], in_=pk[:, 0:256].rearrange("p (k t) -> p k t", k=2)), r=[pkk], w=["KT1"])
        S.op("dve", lambda e: e.tensor_copy(out=V1[:, :, 0:128], in_=pk[:, 256:512].rearrange("p (k d) -> p k d", k=2)),
             r=[pkk], w=["V1"])
        pp.free(pkk)
        yield
        S.op("sp", lambda e: e.dma_start(out=io["o_K"][:, :, ti * 128:(ti + 1) * 128], in_=KT1[:]), r=["KT1"], dma=True, final=True)
        S.op("sp", lambda e: e.dma_start(out=io["o_V"][ti * 128:(ti + 1) * 128, :].rearrange("p (k d) -> p k d", k=2), in_=V1[:, :, 0:128]),
             r=["V1"], dma=True, final=True)
        for half in range(2):
            pg, pgk = pp.alloc()
            _mm8(C, pg[:, 0:512], lambda k: x1T[:, k, :], lambda k, half=half: w1[:, k, 1536 + half * 512:1536 + (half + 1) * 512], rk1, pgk)
            _silu(C, sg1[:, half * 512:(half + 1) * 512], pg[:, 0:512], pgk, stmp1[:], "stmp1", "sg1")
            pp.free(pgk)
            yield
        S.op("sp", lambda e: e.dma_start(out=io["o_sg"][ti * 128:(ti + 1) * 128, :], in_=sg1[:]), r=["sg1"], dma=True, final=True)

    def interleave(*gens):
        gens = [g for g in gens if g is not None]
        while gens:
            for g in list(gens):
                try:
                    next(g)
                except StopIteration:
                    gens.remove(g)

    _load(C, xt[0][:], xsrc(0), ("xt", 0))
    interleave(X(0))
    interleave(X(1))
    def spread(y, x, every=2):
        if x is None:
            interleave(y)
            return
        ydone = xdone = False
        while not (ydone and xdone):
            for _ in range(every):
                if not xdone:
                    try:
                        next(x)
                    except StopIteration:
                        xdone = True
            if not ydone:
                try:
                    next(y)
                except StopIteration:
                    ydone = True

    for ti in range(NT):
        spread(Y(ti), X(ti + 2) if ti + 2 <= NT else None)


def phase_c(C, io, T, gather2):
    S = C.S
    K = _load_consts(C, ["c_ident"])
    ident = K["c_ident"]
    pp = PsPool(C, 7, "pp")
    flags = C.sb("flags", [128, 16])
    _load(C, flags[:], io["t_flags"], "flags")
    tval = C.sb("tval", [128, 512]); tadd = C.sb("tadd", [128, 512]); tprev = C.sb("tprev", [128, 512])
    _load(C, tval[:], io["t_valid"], "tval")
    _load(C, tadd[:], io["t_addm"], "tadd")
    _load(C, tprev[:], io["t_prev"], "tprev")
    fgain = C.sb("fgain", [128, 1024])
    _load(C, fgain[:], io["fgain_rep"], "fgain")
    ecrep, EB, Dc = T["ecrep"], T["EB"], T["Dc"]
    wo = C.sb("wO2", [128, 8, 1024], BF16)
    KTall = C.sb("KTall", [128, 2, 8192], BF16)
    Vall = C.sb("Vall", [128, 64, 260], BF16)
    KTloc = C.sb("KTloc", [128, 2, 2048], BF16)
    Vloc = C.sb("Vloc", [128, 16, 260], BF16)
    kmT = C.sb("kmT", [128, 2, 32], BF16)
    Kprev = C.sb("Kprev", [128, 2, 128], BF16)
    Vprev = C.sb("Vprev", [128, 2, 130], BF16)
    S.op("dve", lambda e: e.memset(Vall[:], 1.0), w=[("Vall", sp, kv) for sp in range(4) for kv in range(2)])
    S.op("dve", lambda e: e.memset(Vloc[:], 1.0), w=[("Vloc", 0), ("Vloc", 1)])
    _load(C, KTloc[:], io["k_loc"], "KTloc")
    gather2()
    for sp in range(4):
        S.op("sp", lambda e, sp=sp: e.dma_start(out=KTall[:, :, sp * 2048:(sp + 1) * 2048], in_=io["k_all"][sp]),
             r=["ccK"], w=[("KTall", sp)], dma=True)
    for sp in range(4):
        for kv in range(2):
            C.S.op("sp", lambda e, sp=sp, kv=kv: e.dma_start(out=Vall[:, sp * 16:(sp + 1) * 16, kv * 130:kv * 130 + 128],
                                                             in_=io["v_all"][sp][:, kv * 128:(kv + 1) * 128].rearrange("(t p) d -> p t d", p=128)),
                   r=["ccV"], w=[("Vall", sp, kv)], dma=True)
    for kv in range(2):
        C.S.op("sp", lambda e, kv=kv: e.dma_start(out=Vloc[:, :, kv * 130:kv * 130 + 128],
                                                  in_=io["v_loc"][:, kv * 128:(kv + 1) * 128].rearrange("(t p) d -> p t d", p=128)),
               w=[("Vloc", kv)], dma=True)
    if True:
        for k in range(8):
            S.op("pool", lambda e, k=k: e.dma_start(out=wo[:, k, :], in_=io["w_out_odd"][k * 128:(k + 1) * 128, :]), w=[("wO2", k)], dma=True)

        kmf = C.sb("kmf", [128, 2, 32])
        for sp in range(4):
            for kv in range(2):
                S.op("dve", lambda e, sp=sp, kv=kv: e.tensor_reduce(
                    out=kmf[:, kv, sp * 8:(sp + 1) * 8], in_=KTall[:, kv, sp * 2048:(sp + 1) * 2048].rearrange("p (b t) -> p b t", b=8),
                    axis=AX.X, op=ALU.add), r=[("KTall", sp)], w=[("kmf", sp, kv)])
        S.op("dve", lambda e: e.tensor_scalar(out=kmT[:], in0=kmf[:], scalar1=1.0 / 256, scalar2=None, op0=ALU.mult),
             r=[("kmf", sp, kv) for sp in range(4) for kv in range(2)], w=["kmT"])
        for sp in range(3):
            ksrc = KTall[:, :, (16 * sp + 15) * 128:(16 * sp + 16) * 128]
            vsrc = Vall[:, 16 * sp + 15, :].rearrange("p (k d) -> p k d", k=2)
            m = flags[:, 8 + sp:9 + sp]
            if sp == 0:
                S.op("dve", lambda e, ksrc=ksrc, m=m: e.tensor_scalar(out=Kprev[:], in0=ksrc, scalar1=m, scalar2=None, op0=ALU.mult),
                     r=[("KTall", sp), "flags"], w=["Kprev"])
                S.op("dve", lambda e, vsrc=vsrc, m=m: e.tensor_scalar(out=Vprev[:], in0=vsrc, scalar1=m, scalar2=None, op0=ALU.mult),
                     r=[("Vall", sp, 0), ("Vall", sp, 1), "flags"], w=["Vprev"])
            else:
                S.op("dve", lambda e, ksrc=ksrc, m=m: e.scalar_tensor_tensor(out=Kprev[:], in0=ksrc, scalar=m, in1=Kprev[:],
                                                                            op0=ALU.mult, op1=ALU.add), r=[("KTall", sp), "flags", "Kprev"], w=["Kprev"])
                S.op("dve", lambda e, vsrc=vsrc, m=m: e.scalar_tensor_tensor(out=Vprev[:], in0=vsrc, scalar=m, in1=Vprev[:],
                                                                            op0=ALU.mult, op1=ALU.add), r=[("Vall", sp, 0), ("Vall", sp, 1), "flags", "Vprev"], w=["Vprev"])

    QT1 = [C.sb(f"QT1_{i}", [128, 8, 128], BF16) for i in range(3)]
    sgt = [C.sb(f"sgt{i}", [128, 1024], BF16) for i in range(4)]
    h1t = [C.sb(f"h1t{i}", [128, 1024]) for i in range(4)]
    accs = [C.sb(f"acc{i}", [128, 8, 130]) for i in range(2)]
    Pt = [C.sb(f"Pt{i}", [128, 512], BF16) for i in range(4)]
    Ee = [C.sb(f"Ee{i}", [128, 512], BF16) for i in range(2)]
    gms = [C.sb(f"gm{i}", [128, 8, 32]) for i in range(2)]
    top8s = [C.sb(f"top8{i}", [128, 8, 8]) for i in range(2)]
    sels = [C.sb(f"sel{i}", [128, 8, 32]) for i in range(2)]
    selws = [C.sb(f"selw{i}", [128, 8, 32]) for i in range(2)]
    tmps = [C.sb(f"tmpsel{i}", [128, 8, 32]) for i in range(2)]
    selps = [C.sb(f"selp{i}", [128, 8]) for i in range(2)]
    rden = C.sb("rden", [128, 8])
    ogtmp = [C.sb(f"ogtmp{i}", [128, 128]) for i in range(2)]
    og = C.sb("og", [128, 1024], BF16); ogT = C.sb("ogT", [128, 8, 128], BF16)
    pT = C.ps("pT", [128, 8, 128], BF16)
    h2 = C.sb("h2", [128, 1024]); junk = C.sb("junk", [128, 1024], BF16)
    ss = C.sb("ss", [128, 1]); rstd = C.sb("rstd", [128, 1]); outt = C.sb("outt", [128, 1024])
    cnt = {"pt": 0, "ee": 0}

    def ld(t):
        _load(C, QT1[t % 3][:].rearrange("p h t -> p (h t)"), io["q_t"][t], ("QT1", t % 3))
        _load(C, sgt[t % 4][:], io["sg"][t * 128:(t + 1) * 128, :], ("sgt", t % 4))
        _load(C, h1t[t % 4][:], io["h1"][t * 128:(t + 1) * 128, :], ("h1t", t % 4))

    def tile(t):
        if t + 1 < NT:
            ld(t + 1)
        q = QT1[t % 3]
        qk = ("QT1", t % 3)
        p = t % 2
        acc, gm, top8, sel, selw, tmp, selp = accs[p], gms[p], top8s[p], sels[p], selws[p], tmps[p], selps[p]
        kq = lambda n: (n, p)
        pg, pgk = pp.alloc()
        for h in range(8):
            S.op("pe", lambda e, h=h: e.matmul(pg[:, h * 32:(h + 1) * 32], lhsT=q[:, h, :], rhs=kmT[:, h // 4, :], start=True, stop=True),
                 r=[qk, "kmT"], w=[pgk])
        S.op("dve", lambda e: e.tensor_tensor(out=gm[:], in0=pg[:, 0:256].rearrange("p (h n) -> p h n", h=8),
                                              in1=_bcast_mid(tadd[:, t * 32:(t + 1) * 32], 8), op=ALU.add), r=[pgk, "tadd"], w=[kq("gm")])
        pp.free(pgk)
        for h in range(8):
            S.op("dve", lambda e, h=h: e.max(out=top8[:, h, :], in_=gm[:, h, :]), r=[kq("gm")], w=[("top8", p, h)])
            S.op("dve", lambda e, h=h: e.tensor_scalar(out=sel[:, h, :], in0=gm[:, h, :], scalar1=top8[:, h, 2:3], scalar2=None, op0=ALU.is_ge),
                 r=[kq("gm"), ("top8", p, h)], w=[("sel", p, h)])
        selk = [("sel", p, h) for h in range(8)]
        S.op("dve", lambda e: e.tensor_tensor(out=sel[:], in0=sel[:], in1=_bcast_mid(tval[:, t * 32:(t + 1) * 32], 8), op=ALU.mult),
             r=selk + ["tval"], w=selk)
        S.op("dve", lambda e: e.tensor_tensor(out=tmp[:], in0=sel[:], in1=_bcast_mid(tprev[:, t * 32:(t + 1) * 32], 8), op=ALU.mult),
             r=selk + ["tprev"], w=[kq("tmpsel")])
        S.op("dve", lambda e: e.tensor_reduce(out=selp[:], in_=tmp[:], axis=AX.X, op=ALU.add), r=[kq("tmpsel")], w=[kq("selp")])
        for h in range(8):
            S.op("pool", lambda e, h=h: e.tensor_scalar(out=selw[:, h, :], in0=sel[:, h, :], scalar1=ecrep[:, h:h + 1], scalar2=None, op0=ALU.mult),
                 r=selk + ["ecrep"], w=[kq("selw")])
        S.op("pool", lambda e: e.memset(acc[:], 0.0), w=[("acc", p, h) for h in range(8)])
        yield "G"

        def stage1(u):
            keys, table = u["keys"], u["table"]
            kv = u["kv"]
            u["pt"] = []
            for j, (kap, kkey, _, _) in enumerate(keys):
                pS, pSk = pp.alloc()
                S.op("pe", lambda e, kap=kap, pS=pS: e.matmul(pS[:, 0:512], lhsT=kap, rhs=q[:, kv * 4:(kv + 1) * 4, :], start=True, stop=True),
                     r=[kkey, qk], w=[pSk])
                pi = cnt["pt"] % 4
                cnt["pt"] += 1
                if table is None:
                    S.op("act", lambda e, pS=pS, pi=pi: e.activation(out=Pt[pi][:], in_=pS[:, 0:512], func=AF.Exp), r=[pSk], w=[("Pt", pi)])
                else:
                    ei = cnt["ee"] % 2
                    cnt["ee"] += 1
                    S.op("act", lambda e, pS=pS, ei=ei: e.activation(out=Ee[ei][:], in_=pS[:, 0:512], func=AF.Exp), r=[pSk], w=[("Ee", ei)])
                    tab, tkey = table[j]
                    S.op("dve", lambda e, pi=pi, ei=ei, tab=tab: e.tensor_tensor(out=Pt[pi][:].rearrange("p (a t) -> p a t", a=4),
                                                                                in0=Ee[ei][:].rearrange("p (a t) -> p a t", a=4), in1=tab, op=ALU.mult),
                         r=[("Ee", ei), tkey], w=[("Pt", pi)])
                pp.free(pSk)
                u["pt"].append(pi)

        def stage2(u):
            kv, keys, wf = u["kv"], u["keys"], u["w"]
            banks = [pp.alloc(), pp.alloc()]
            for a in range(4):
                pO, pOk = banks[a // 2]
                col = (a % 2) * 129
                for j, (_, _, vap, vkey) in enumerate(keys):
                    pi = u["pt"][j]
                    S.op("pe", lambda e, pO=pO, col=col, pi=pi, vap=vap, a=a, j=j: e.matmul(
                        pO[:, col:col + 129], lhsT=Pt[pi][:, a * 128:(a + 1) * 128], rhs=vap[:, kv, 0:129],
                        start=(j == 0), stop=(j == len(keys) - 1)), r=[("Pt", pi), vkey], w=[pOk])
            for a in range(4):
                pO, pOk = banks[a // 2]
                col = (a % 2) * 129
                h = kv * 4 + a
                sc, sck = wf(h)
                S.op("dve", lambda e, pO=pO, col=col, h=h, sc=sc: e.scalar_tensor_tensor(
                    out=acc[:, h, 0:129], in0=pO[:, col:col + 129], scalar=sc, in1=acc[:, h, 0:129], op0=ALU.mult, op1=ALU.add),
                    r=[pOk, ("acc", p, h)] + sck, w=[("acc", p, h)])
            pp.free(banks[0][1])
            pp.free(banks[1][1])

        units = []
        for n in range(min(32, 24 + t // 2)):
            sp = n // 8
            for kv in range(2):
                keys = []
                for kt in range(2):
                    g = 2 * n + kt
                    keys.append((KTall[:, kv, g * 128:(g + 1) * 128], ("KTall", sp),
                                 Vall[:, g, :].rearrange("p (k d) -> p k d", k=2), ("Vall", sp, kv)))
                units.append({"kv": kv, "keys": keys, "table": None,
                              "w": (lambda h, n=n: (selw[:, h, n:n + 1], [kq("selw")]))})
        for kv in range(2):
            keys, table = [], []
            if t % 2 == 1:
                keys.append((KTloc[:, kv, (t - 1) * 128:t * 128], "KTloc", Vloc[:, t - 1, :].rearrange("p (k d) -> p k d", k=2), ("Vloc", kv)))
                table.append((EB[:, 3, kv * 4:(kv + 1) * 4, :], "EB"))
            keys.append((KTloc[:, kv, t * 128:(t + 1) * 128], "KTloc", Vloc[:, t, :].rearrange("p (k d) -> p k d", k=2), ("Vloc", kv)))
            table.append((EB[:, 2, kv * 4:(kv + 1) * 4, :], "EB"))
            units.append({"kv": kv, "keys": keys, "table": table, "w": (lambda h: (1.0, []))})
        if t % 2 == 0:
            for kv in range(2):
                if t == 0:
                    keys = [(Kprev[:, kv, :], "Kprev", Vprev, "Vprev")]
                else:
                    keys = [(KTloc[:, kv, (t - 1) * 128:t * 128], "KTloc", Vloc[:, t - 1, :].rearrange("p (k d) -> p k d", k=2), ("Vloc", kv))]
                units.append({"kv": kv, "keys": keys, "table": [(Dc[:, kv * 4:(kv + 1) * 4, :], "Dc")],
                              "w": (lambda h: (selp[:, h:h + 1], [kq("selp")]))})
        stage1(units[0])
        for ui in range(len(units)):
            if ui + 1 < len(units):
                stage1(units[ui + 1])
            stage2(units[ui])
            yield "U"
        yield "UE"

        S.op("dve", lambda e: e.reciprocal(out=rden[:], in_=acc[:, :, 128]), r=[("acc", p, h) for h in range(8)], w=["rden"])
        sg_ = sgt[t % 4]
        for h in range(8):
            tf = ogtmp[h % 2]
            S.op("pool", lambda e, h=h, tf=tf: e.tensor_scalar(out=tf[:], in0=acc[:, h, 0:128], scalar1=rden[:, h:h + 1], scalar2=None, op0=ALU.mult),
                 r=[("acc", p, h), "rden"], w=[("ogtmp", h % 2)])
            S.op("pool", lambda e, h=h, tf=tf: e.tensor_tensor(out=og[:, h * 128:(h + 1) * 128], in0=tf[:], in1=sg_[:, h * 128:(h + 1) * 128], op=ALU.mult),
                 r=[("ogtmp", h % 2), ("sgt", t % 4)], w=[("og", h)])
            if h % 2 == 1:
                yield "E"
        for k in range(8):
            S.op("pe", lambda e, k=k: e.transpose(out=pT[:, k, :], in_=og[:, k * 128:(k + 1) * 128], identity=ident[:]),
                 r=[("og", k), "c_ident"], w=["pTbank"])
        S.op("act", lambda e: e.copy(out=ogT[:], in_=pT[:]), r=["pTbank"], w=["ogT"])
        yield "E"
        hh = h1t[t % 4]
        for nbk in range(2):
            pH, pHk = pp.alloc()
            _mm8(C, pH[:, 0:512], lambda k: ogT[:, k, :], lambda k, nbk=nbk: wo[:, k, nbk * 512:(nbk + 1) * 512],
                 lambda k: ["ogT", ("wO2", k)], pHk)
            S.op("dve", lambda e, nbk=nbk, pH=pH: e.tensor_tensor(out=h2[:, nbk * 512:(nbk + 1) * 512], in0=pH[:, 0:512],
                                                                  in1=hh[:, nbk * 512:(nbk + 1) * 512], op=ALU.add),
                 r=[pHk, ("h1t", t % 4)], w=["h2"])
            pp.free(pHk)
            yield "E"
        S.op("act", lambda e: e.activation(out=junk[:], in_=h2[:], func=AF.Square, accum_out=ss[:]), r=["h2"], w=["junk", "ss"])
        S.op("act", lambda e: e.activation(out=ss[:], in_=ss[:], func=AF.Ln, scale=1.0 / 1024, bias=EPS), r=["ss"], w=["ss"])
        S.op("act", lambda e: e.activation(out=rstd[:], in_=ss[:], func=AF.Exp, scale=-0.5), r=["ss"], w=["rstd"])
        S.op("dve", lambda e: e.scalar_tensor_tensor(out=outt[:], in0=h2[:], scalar=rstd[:], in1=fgain[:], op0=ALU.mult, op1=ALU.mult),
             r=["h2", "rstd", "fgain"], w=["outt"])
        S.op("sp", lambda e, t=t: e.dma_start(out=io["out"][t * 128:(t + 1) * 128, :], in_=outt[:]), r=["outt"], dma=True, final=True)

    ld(0)
    gens = [tile(t) for t in range(NT)]
    done_gate = set()

    def run_until(g, marks):
        while True:
            try:
                r = next(g)
            except StopIteration:
                return None
            if r in marks:
                return r

    run_until(gens[0], ("G",))
    done_gate.add(0)
    for t in range(NT):
        side = []
        if t >= 1:
            side.append(gens[t - 1])
        ui = 0
        while True:
            r = next(gens[t])
            if r == "UE":
                break
            ui += 1
            if side:
                try:
                    next(side[0])
                except StopIteration:
                    side = []
            elif t + 1 < NT and (t + 1) not in done_gate:
                run_until(gens[t + 1], ("G",))
                done_gate.add(t + 1)
        for g in side:
            for _ in g:
                pass
        if t + 1 < NT and (t + 1) not in done_gate:
            run_until(gens[t + 1], ("G",))
            done_gate.add(t + 1)
    for _ in gens[NT - 1]:
        pass


RG = [[0, 1, 2, 3], [4, 5, 6, 7]]


def build_fused():
    nc = bass.Bass("TRN2", target_bir_lowering=False)
    dc = {}
    D = lambda n, s, d=F32: nc.dram_tensor(n, list(s), d).ap()
    DL = lambda n, s, d=F32: nc.dram_tensor(n, list(s), d, addr_space="Local").ap()
    summ, summ_all = D("i_summ", [64, 520]), DL("i_summ_all", [256, 520])
    h1, qT, sg = D("i_h1", [SEG, 1024]), D("i_qT", [NT, 128, 1024], BF16), D("i_sg", [SEG, 1024], BF16)
    Kl, Vl = D("i_K", [128, 2 * SEG], BF16), D("i_V", [SEG, 256], BF16)
    k_all, v_all = DL("i_k_all", [512, 2 * SEG], BF16), DL("i_v_all", [4 * SEG, 256], BF16)
    zscr = D("zscr", [8, 128, 1024])

    with contextlib.ExitStack() as es:
        C = _mk(nc, es, dc)
        S = C.S
        io = {"x": C.din("x", [SEG, 1024]), "x_halo": C.din("x_halo", [128, 1024]),
              "gain0_rep": C.din("gain0_rep", [128, 1024]), "gain1_rep": C.din("gain1_rep", [128, 1024]),
              "w_in_even": C.din("w_in_even", [1024, 2832]), "w_out_even": C.din("w_out_even", [1024, 1024]),
              "w_in_odd": C.din("w_in_odd", [1024, 2560]), "w_out_odd": C.din("w_out_odd", [1024, 1024]),
              "gla_w_up": C.din("gla_w_up", [16, 256]), "gla_b_up": C.din("gla_b_up", [1, 256]),
              "gla_gain_rep": C.din("gla_gain_rep", [128, 512]), "sinks_rep": C.din("sinks_rep", [128, 8]),
              "rel_bias": C.din("rel_bias", [32, 8]), "rb31_rep": C.din("rb31_rep", [128, 8]),
              "fgain_rep": C.din("fgain_rep", [128, 1024]), "t_flags": C.din("t_flags", [128, 16]),
              "t_valid": C.din("t_valid", [128, 512]), "t_addm": C.din("t_addm", [128, 512]), "t_prev": C.din("t_prev", [128, 512]),
              "zscr": zscr,
              "o_summ": summ, "summ_all": summ_all.rearrange("(s p) c -> s p c", s=4),
              "o_h1": h1, "o_qT": qT, "o_sg": sg, "o_K": Kl.rearrange("p (k t) -> p k t", k=2), "o_V": Vl,
              "q_t": qT, "sg": sg, "h1": h1,
              "k_all": k_all.rearrange("(s p) (k t) -> s p k t", s=4, k=2), "v_all": v_all.rearrange("(s t) d -> s t d", s=4),
              "k_loc": Kl.rearrange("p (k t) -> p k t", k=2), "v_loc": Vl,
              "out": nc.dram_tensor("out", [SEG, 1024], F32, kind="ExternalOutput").ap()}

        def gather(pairs):
            for a, b in pairs:
                tok = S.cc(lambda e, a=a, b=b: e.collective_compute("AllGather", ALU.bypass, replica_groups=RG, ins=[a], outs=[b]))
                S.res["cc1"] = [tok, []]

        C.pfx = "T_"
        T = {"EB": C.sb("EBall", [128, 4, 8, 128], BF16), "Dc": C.sb("Dcorr", [128, 8, 128], BF16), "ecrep": C.sb("ecrep", [128, 8])}
        C.pfx = "W_"
        with C.scope():
            W = _load_weights(C, io)
            C.pfx = "A_"
            with C.scope():
                phase_a(C, io, W, T)
            gather([(summ, summ_all)])
            C.pfx = "B_"
            with C.scope():
                phase_b(C, io, W, T)
        def gather2():
            for key, a, b in (("ccK", Kl, k_all), ("ccV", Vl, v_all)):
                tok = S.cc(lambda e, a=a, b=b: e.collective_compute("AllGather", ALU.bypass, replica_groups=RG, ins=[a], outs=[b]))
                S.res[key] = [tok, []]

        C.pfx = "C_"
        with C.scope():
            phase_c(C, io, T, gather2)
        S.emit()
    return nc


_PROGS = {}


def _prog(name, fn):
    if name not in _PROGS:
        _PROGS[name] = fn()
    return _PROGS[name]


def _rep(a, n=128):
    a = np.asarray(a, np.float32)
    return np.ascontiguousarray(np.broadcast_to(a.reshape(1, -1), (n, a.size)))


def kernel(x, norm_gain, final_gain, rel_bias, w_in_even, gla_w_up, gla_b_up, gla_norm_gain,
           swa_sinks, w_out_even, w_in_odd, w_out_odd):
    f = lambda a: np.ascontiguousarray(np.asarray(a, dtype=np.float32))
    x = f(x)
    cores = list(range(NCORES))
    cn = _consts()
    rb = f(rel_bias)
    shared = dict(cn, gain0_rep=_rep(f(norm_gain)[0]), gain1_rep=_rep(f(norm_gain)[1]), w_in_even=f(w_in_even)[0], w_out_even=f(w_out_even)[0], w_in_odd=f(w_in_odd)[0],
                  w_out_odd=f(w_out_odd)[0], gla_w_up=f(gla_w_up)[0], gla_b_up=f(gla_b_up)[0].reshape(1, 256),
                  gla_gain_rep=_rep(np.tile(f(gla_norm_gain)[0], 4)), sinks_rep=_rep(f(swa_sinks)[0]), rel_bias=rb,
                  rb31_rep=_rep(rb[31]), fgain_rep=_rep(f(final_gain)))
    in_maps = []
    for c in cores:
        b, s = c // 4, c % 4
        halo = x[b, s * SEG - 128:s * SEG] if s > 0 else np.zeros((128, 1024), np.float32)
        in_maps.append(dict(shared, x=np.ascontiguousarray(x[b, s * SEG:(s + 1) * SEG]), x_halo=np.ascontiguousarray(halo),
                            **_core_tables(s)))
    res = run_bass_kernel_spmd(_prog("fused", build_fused), in_maps, core_ids=cores).results
    out = np.empty((2, 4 * SEG, 1024), np.float32)
    for c in cores:
        out[c // 4, (c % 4) * SEG:(c % 4 + 1) * SEG] = np.asarray(res[c]["out"])
    return out
```

```python
import contextlib
import math
import numpy as np
import ml_dtypes
import concourse.bass as bass
import concourse.mybir as mybir
from concourse.bass_utils import run_bass_kernel_spmd

F32 = mybir.dt.float32
BF16 = mybir.dt.bfloat16
AF = mybir.ActivationFunctionType
ALU = mybir.AluOpType
AX = mybir.AxisListType

NCORES = 8
SEG = 2048
NT = SEG // 128
EPS = 1e-6
CH = 30000
NDMA = 40
NDMA_SW = 12


class Sched:
    CE = ("pe", "act", "dve", "pool")

    def __init__(self, nc):
        self.nc = nc
        self.ops = {e: [] for e in self.CE + ("sp",)}
        self.cnt = {e: 0 for e in self.CE}
        self.res = {}
        self.clk = {}
        self.know = {e: {} for e in self.CE + ("sp",)}
        self.ndma = 0
        self.ndma_q = {}
        self.slots = set()
        self.final = []
        self.dma_toks = []
        self.pfx = ""
        self.excl = set()
        self.bank = {}

    def _need(self, eng, tok, waits):
        if tok is None:
            return
        k = self.know[eng]
        key = (tok[0], tok[1])
        if k.get(key, 0) >= tok[2]:
            return
        waits.append(tok)
        for kk, v in self.clk[tok].items():
            if k.get(kk, 0) < v:
                k[kk] = v

    def _dep(self, eng, tok, waits):
        if tok is None:
            return
        if tok[0] == "c" and tok[1] == "pe" and eng == "pe":
            return
        self._need(eng, tok, waits)

    def op(self, eng, fn, r=(), w=(), dma=False, final=False):
        waits = []
        for key in list(r) + list(w):
            if key in self.excl:
                for f, t in self.bank.setdefault(key, {}).items():
                    if f != eng:
                        self._need(eng, t, waits)
        for key in r:
            ent = self.res.get(key)
            if ent is not None:
                self._dep(eng, ent[0], waits)
        for key in w:
            ent = self.res.get(key)
            if ent is not None:
                self._dep(eng, ent[0], waits)
                for t in ent[1]:
                    self._dep(eng, t, waits)
        if dma:
            nslot, base = (NDMA, 0) if eng == "sp" else (NDMA_SW, 100)
            n = self.ndma_q.get(eng, 0)
            self.ndma_q[eng] = n + 1
            slot = base + n % nslot
            val = 16 * (n // nslot + 1)
            if val > 16:
                self._need(eng, ("d", slot, val - 16), waits)
            self.ndma += 1
            self.slots.add(slot)
            tok = ("d", slot, val)
            self.dma_toks.append(tok)
        else:
            self.cnt[eng] += 1
            tok = ("c", eng, self.cnt[eng])
        clk = dict(self.know[eng])
        clk[(tok[0], tok[1])] = tok[2]
        self.clk[tok] = clk
        self.ops[eng].append((waits, fn, tok))
        for key in r:
            ent = self.res.setdefault(key, [None, []])
            ent[1].append(tok)
        for key in w:
            self.res[key] = [tok, []]
        for key in list(r) + list(w):
            if key in self.excl:
                self.bank[key][eng] = tok
        if final:
            self.final.append(tok)
        return tok

    def cc(self, fn):
        self.cnt["pool"] += 1
        tok = ("c", "pool", self.cnt["pool"])
        clk = dict(self.know["pool"])
        clk[("c", "pool")] = tok[2]
        self.clk[tok] = clk
        self.ops["pool"].append(([], fn, tok))
        w = []
        self._need("pool", tok, w)
        self.ops["pool"].append((w, None, None))
        return tok

    def barrier(self):
        toks = [("c", e, self.cnt[e]) for e in self.CE if self.cnt[e] > 0]
        last = {}
        for t in self.dma_toks:
            last[t[1]] = t
        toks += list(last.values())
        for eng in self.CE + ("sp",):
            waits = []
            for t in toks:
                self._need(eng, t, waits)
            if waits:
                self.ops[eng].append((waits, None, None))
        self.dma_toks = []

    def emit(self):
        nc = self.nc
        fw = []
        for t in self.final:
            self._need("sp", t, fw)
        with contextlib.ExitStack() as es:
            sems = {}
            for e in self.CE:
                n = (self.cnt[e] + CH - 1) // CH
                for i in range(max(n, 1)):
                    sems[(e, i)] = es.enter_context(nc.semaphore(f"{self.pfx}s_{e}{i}"))
            for i in sorted(self.slots):
                sems[("d", i)] = es.enter_context(nc.semaphore(f"{self.pfx}s_d{i}"))
            block = es.enter_context(nc.Block())

            def semval(tok):
                if tok[0] == "c":
                    n = tok[2] - 1
                    return sems[(tok[1], n // CH)], n % CH + 1
                return sems[("d", tok[1])], tok[2]

            def run(ename, eng):
                for waits, fn, tok in self.ops[ename]:
                    for t in waits:
                        s, v = semval(t)
                        eng.wait_ge(s, v)
                    if fn is None:
                        continue
                    ins = fn(eng)
                    s, v = semval(tok)
                    ins.then_inc(s, 1 if tok[0] == "c" else 16)
                if ename == "sp":
                    for t in fw:
                        s, v = semval(t)
                        eng.wait_ge(s, v)

            @block.tensor
            def _(e):
                run("pe", e)

            @block.scalar
            def _(e):
                run("act", e)

            @block.vector
            def _(e):
                run("dve", e)

            @block.gpsimd
            def _(e):
                run("pool", e)

            @block.sync
            def _(e):
                run("sp", e)


def _bucket(d):
    n = np.maximum(d, 0)
    nf = np.maximum(n, 1).astype(np.float32)
    large = 16 + (np.log(nf / np.float32(16)) / np.float32(math.log(8.0)) * np.float32(16)).astype(np.int32)
    large = np.minimum(large, 31)
    return np.where(n < 16, n, large)


def _consts():
    s = np.arange(128)
    same = (s[:, None] // 64) == (s[None, :] // 64)
    tri = (same & (s[:, None] <= s[None, :])).astype(np.float32)
    blk = same.astype(np.float32)
    cind = ((s[:, None] // 64) == np.arange(2)[None, :]).astype(np.float32)
    maskbd = np.tile(tri, (1, 4)).astype(ml_dtypes.bfloat16)
    onehot = np.zeros((32, 1024), np.float32)
    m = np.arange(255)
    for typ in range(4):
        d = m - 127 if typ in (0, 2) else m + 1
        if typ in (0, 1):
            valid = (d >= 0) & (d < 128)
        else:
            valid = d >= 0
        b = _bucket(d)
        for mm in range(255):
            if valid[mm]:
                onehot[b[mm], typ * 256 + mm] = 1.0
    return {
        "c_ident": np.eye(128).astype(ml_dtypes.bfloat16),
        "c_tri": tri, "c_blk": blk, "c_cind": cind, "c_maskbd": maskbd,
        "c_onehot": onehot,
        "c_ones1": np.ones((1, 128), np.float32),
    }


def _core_tables(seg):
    own = 8 * seg + np.arange(NT) // 2
    n = np.arange(32)
    valid = (n[None, :] < own[:, None]).astype(np.float32)
    addm = np.where(valid > 0, 0.0, -1e30).astype(np.float32)
    prev = (n[None, :] == (own[:, None] - 1)).astype(np.float32)
    rep = lambda a: np.ascontiguousarray(np.broadcast_to(a.reshape(1, -1), (128, a.size))).astype(np.float32)
    flags = np.zeros(16, np.float32)
    flags[0] = 1.0 if seg > 0 else 0.0
    for sp in range(4):
        flags[4 + sp] = 1.0 if sp < seg else 0.0
        flags[8 + sp] = 1.0 if sp == seg - 1 else 0.0
    return {"t_valid": rep(valid), "t_addm": rep(addm), "t_prev": rep(prev), "t_flags": rep(flags)}


class Ctx:
    pass


_UID = [0]


def _mk(nc, es, dcache=None):
    C = Ctx()
    C.nc = nc
    C.S = Sched(nc)
    _UID[0] += 1
    C.pfx = f"u{_UID[0]}_"
    C.S.pfx = C.pfx
    dcache = {} if dcache is None else dcache
    C.stack = [es]
    C.sb = lambda n, s, d=F32: C.stack[-1].enter_context(nc.sbuf_tensor(C.pfx + "S_" + n, list(s), d))

    @contextlib.contextmanager
    def scope():
        with contextlib.ExitStack() as es2:
            C.stack.append(es2)
            try:
                yield
            finally:
                C.S.barrier()
                C.stack.pop()
    C.scope = scope
    C.ps = lambda n, s, d=F32: C.stack[-1].enter_context(nc.psum_tensor(C.pfx + "P_" + n, list(s), d))
    def din(n, s, d=F32):
        if n not in dcache:
            dcache[n] = nc.dram_tensor(n, list(s), d, kind="ExternalInput").ap()
        return dcache[n]
    C.din = din
    C.dout = lambda n, s, d=F32: nc.dram_tensor(n, list(s), d, kind="ExternalOutput").ap()
    return C


def _load(C, dst, src, key, eng="sp"):
    return C.S.op(eng, lambda e: e.dma_start(out=dst, in_=src), w=[key], dma=True)


def _load_consts(C, names):
    shapes = {"c_ident": ([128, 128], BF16), "c_tri": ([128, 128], F32), "c_blk": ([128, 128], F32),
              "c_cind": ([128, 2], F32), "c_maskbd": ([128, 512], BF16), "c_onehot": ([32, 1024], F32),
              "c_ones1": ([1, 128], F32)}
    out = {}
    for n in names:
        shp, dt = shapes[n]
        d = C.din(n, shp, dt)
        t = C.sb("sb_" + n, shp, dt)
        _load(C, t[:], d, n)
        out[n] = t
    return out


def _load_weights(C, io):
    S = C.S
    W = {"wb": C.sb("wE", [128, 8, 2832], BF16), "wq": C.sb("wq", [128, 8, 512], BF16),
         "wo": C.sb("wO", [128, 8, 1024], BF16), "w1": C.sb("w1", [128, 8, 2560], BF16),
         "g0": C.sb("g0rep", [128, 1024]), "g1": C.sb("g1rep", [128, 1024])}
    _load(C, W["g0"][:], io["gain0_rep"], "grep")
    _load(C, W["g1"][:], io["gain1_rep"], "grep")
    first = [("wE", k) for k in range(8)]

    def ld(dst, srcap, key):
        S.op("pool", lambda e: e.dma_start(out=dst, in_=srcap), r=([] if key in first else first), w=[key], dma=True)
    we = io["w_in_even"]
    for k in range(8):
        ld(W["wb"][:, k, 256:1040], we[k * 128:(k + 1) * 128, 256:1040], ("wE", k))
    for k in range(8):
        ld(W["wb"][:, k, 0:256], we[k * 128:(k + 1) * 128, 0:256], ("wE0", k))
        ld(W["wb"][:, k, 1040:1552], we[k * 128:(k + 1) * 128, 1040:1552], ("wE1", k))
    for k in range(8):
        ld(W["wb"][:, k, 1552:2832], we[k * 128:(k + 1) * 128, 1552:2832], ("wE2", k))
        for a in range(4):
            ld(W["wq"][:, k, a * 128:(a + 1) * 128].rearrange("p (g d) -> p g d", g=2),
               we[k * 128:(k + 1) * 128, 1552:2064].rearrange("p (g a d) -> p g a d", g=2, a=4)[:, :, a, :], ("wq", k, a))
    for k in range(8):
        ld(W["wo"][:, k, :], io["w_out_even"][k * 128:(k + 1) * 128, :], ("wO", k))
    for k in range(8):
        ld(W["w1"][:, k, :], io["w_in_odd"][k * 128:(k + 1) * 128, :], ("w1", k))
    return W


def _norm_transpose(C, src, srckey, bufs, ident, tag, grep=None):
    S = C.S
    junk, ss, rstd, xn, pT, xnT = (bufs[k] for k in ("junk", "ss", "rstd", "xn", "pT", "xnT"))
    S.op("act", lambda e: e.activation(out=junk[:], in_=src, func=AF.Square, accum_out=ss[:]),
         r=[srckey], w=[tag + "junk", tag + "ss"])
    S.op("act", lambda e: e.activation(out=ss[:], in_=ss[:], func=AF.Ln, scale=1.0 / 1024, bias=EPS),
         r=[tag + "ss"], w=[tag + "ss"])
    S.op("act", lambda e: e.activation(out=rstd[:], in_=ss[:], func=AF.Exp, scale=-0.5), r=[tag + "ss"], w=[tag + "rstd"])
    if grep is None:
        S.op("dve", lambda e: e.tensor_scalar(out=xn[:], in0=src, scalar1=rstd[:], scalar2=None, op0=ALU.mult),
             r=[srckey, tag + "rstd"], w=[tag + "xn"])
    else:
        S.op("dve", lambda e: e.scalar_tensor_tensor(out=xn[:], in0=src, scalar=rstd[:], in1=grep[:], op0=ALU.mult, op1=ALU.mult),
             r=[srckey, tag + "rstd", "grep"], w=[tag + "xn"])
    for k in range(8):
        S.op("pe", lambda e, k=k: e.transpose(out=pT[:, k, :], in_=xn[:, k * 128:(k + 1) * 128], identity=ident[:]),
             r=[tag + "xn", "c_ident"], w=["pTbank"])
    S.op("act", lambda e: e.copy(out=xnT[:], in_=pT[:]), r=["pTbank"], w=[tag + "xnT"])


def _silu(C, out, pin, pkey, tmp, tkey, okey):
    S = C.S
    S.op("act", lambda e: e.activation(out=tmp, in_=pin, func=AF.Exp, scale=-1.0), r=[pkey], w=[tkey])
    S.op("act", lambda e: e.activation(out=tmp, in_=tmp, func=AF.Ln, bias=1.0), r=[tkey], w=[tkey])
    S.op("act", lambda e: e.activation(out=tmp, in_=tmp, func=AF.Exp, scale=-1.0), r=[tkey], w=[tkey])
    S.op("dve", lambda e: e.tensor_tensor(out=out, in0=pin, in1=tmp, op=ALU.mult), r=[pkey, tkey], w=[okey])


def _mm8(C, out, lhs_fn, rhs_fn, rkeys, wkey):
    for k in range(8):
        C.S.op("pe", lambda e, k=k: e.matmul(out, lhsT=lhs_fn(k), rhs=rhs_fn(k), start=(k == 0), stop=(k == 7)),
               r=rkeys(k), w=[wkey])


class PsPool:
    def __init__(self, C, n, name):
        self.t = [C.ps(f"{name}{i}", [128, 512], F32) for i in range(n)]
        self.free_list = list(range(n))
        self.name = name
        for i in range(n):
            C.S.excl.add((name, i))
        C.S.excl.add("pTbank")
        C.S.excl.add("pT2")

    def alloc(self):
        assert self.free_list, "PSUM pool exhausted (program-order liveness bug)"
        j = self.free_list.pop(0)
        return self.t[j], (self.name, j)

    def free(self, key):
        assert key[1] not in self.free_list
        self.free_list.append(key[1])


def _gla_tile(C, G, ti, pA, pAkey, vsrc, vkey, adT_key, full):
    S = C.S
    K = G["K"]
    pp = G["pp"]
    sfx = G.get("sfx", "")
    kx = lambda n: n + sfx
    pz, pzk = pp.alloc()
    S.op("pe", lambda e: e.matmul(pz[:, 0:256], lhsT=G["adT"][:], rhs=G["wup"][:], start=True, stop=False),
         r=[adT_key, "wup"], w=[pzk])
    S.op("pe", lambda e: e.matmul(pz[:, 0:256], lhsT=K["c_ones1"][:], rhs=G["bup"][:], start=False, stop=True),
         r=["c_ones1", "bup"], w=[pzk])
    el = G["el"]
    S.op("act", lambda e: e.activation(out=el[:], in_=pz[:, 0:256], func=AF.Exp, scale=-1.0), r=[pzk], w=[kx("el")])
    pp.free(pzk)
    S.op("act", lambda e: e.activation(out=el[:], in_=el[:], func=AF.Ln, bias=1.0), r=[kx("el")], w=[kx("el")])
    yield
    pc, pck = pp.alloc()
    S.op("pe", lambda e: e.matmul(pc[:, 0:256], lhsT=K["c_tri"][:], rhs=el[:], start=True, stop=True),
         r=["c_tri", kx("el")], w=[pck])
    S.op("pe", lambda e: e.matmul(pc[:, 256:512], lhsT=K["c_blk"][:], rhs=el[:], start=True, stop=True),
         r=["c_blk", kx("el")], w=[pck])
    pd, pdk = pp.alloc()
    for h in range(4):
        S.op("pe", lambda e, h=h: e.matmul(pd[0:64, 2 * h:2 * h + 2], lhsT=el[:, h * 64:(h + 1) * 64], rhs=K["c_cind"][:],
                                           start=True, stop=True), r=[kx("el"), "c_cind"], w=[pdk])
    dec = G["dec"]
    S.op("act", lambda e: e.activation(out=dec[:], in_=pd[0:64, 0:8], func=AF.Exp, scale=-1.0 / 16), r=[pdk], w=[kx("dec")])
    if not full:
        S.op("dve", lambda e: e.tensor_tensor(out=G["lacc"][:], in0=G["lacc"][:], in1=pd[0:64, 0:8], op=ALU.add),
             r=[pdk, "lacc"], w=["lacc"])
    pp.free(pdk)
    yield
    enb, ebl, edec = G["enb"], G["ebl"], G["edec"]
    S.op("act", lambda e: e.activation(out=enb[:], in_=pc[:, 0:256], func=AF.Exp, scale=1.0 / 16), r=[pck], w=[kx("enb")])
    S.op("act", lambda e: e.activation(out=ebl[:], in_=pc[:, 256:512], func=AF.Exp, scale=-1.0 / 16), r=[pck], w=[kx("ebl")])
    if full:
        eb = G["eb"]
        S.op("act", lambda e: e.activation(out=eb[:], in_=pc[:, 0:256], func=AF.Exp, scale=-1.0 / 16), r=[pck], w=[kx("eb")])
    pp.free(pck)
    yield
    S.op("dve", lambda e: e.tensor_tensor(out=edec[:], in0=ebl[:], in1=enb[:], op=ALU.mult), r=[kx("enb"), kx("ebl")], w=[kx("edec")])
    kdec = G["kdec"]
    S.op("dve", lambda e: e.tensor_tensor(out=kdec[:], in0=pA[:, 256:512], in1=edec[:], op=ALU.mult),
         r=[pAkey, kx("edec")], w=[kx("kdec")])
    if full:
        qke = G["qke"]
        S.op("dve", lambda e: e.scalar_tensor_tensor(out=qke[:, 0:256], in0=pA[:, 0:256], scalar=0.125, in1=eb[:],
                                                     op0=ALU.mult, op1=ALU.mult), r=[pAkey, kx("eb")], w=[kx("qke")])
        S.op("dve", lambda e: e.tensor_tensor(out=qke[:, 256:512], in0=pA[:, 256:512], in1=enb[:], op=ALU.mult),
             r=[pAkey, kx("enb")], w=[kx("qke")])
        pT2, qkT = G["pT2"], G["qkT"]
        for j in range(8):
            S.op("pe", lambda e, j=j: e.transpose(out=pT2[0:64, j, :], in_=qke[:, j * 64:(j + 1) * 64], identity=K["c_ident"][:]),
                 r=[kx("qke"), "c_ident"], w=["pT2"])
        S.op("act", lambda e: e.copy(out=qkT[:], in_=pT2[0:64, :, :]), r=["pT2"], w=[kx("qkT")])
        yield
        patt, pattk = pp.alloc()
        for h in range(4):
            S.op("pe", lambda e, h=h: e.matmul(patt[:, h * 128:(h + 1) * 128], lhsT=qkT[:, 4 + h, :], rhs=qkT[:, h, :],
                                               start=True, stop=True), r=[kx("qkT")], w=[pattk])
        attm = G["attm"]
        S.op("dve", lambda e: e.tensor_tensor(out=attm[:], in0=patt[:], in1=K["c_maskbd"][:], op=ALU.mult),
             r=[pattk, "c_maskbd"], w=[kx("attm")])
        pp.free(pattk)
        yield
    S32 = G["S32"]

    def kv_mm(c):
        pkv, pkvk = pp.alloc()
        for h in range(4):
            S.op("pe", lambda e, h=h: e.matmul(pkv[0:64, h * 128:(h + 1) * 128],
                                               lhsT=kdec[c * 64:(c + 1) * 64, h * 64:(h + 1) * 64],
                                               rhs=vsrc[c * 64:(c + 1) * 64, h * 128:(h + 1) * 128],
                                               start=True, stop=True), r=[kx("kdec"), vkey], w=[pkvk])
        return pkv, pkvk

    def upd(c, pkv, pkvk):
        g = 2 * ti + c
        for h in range(4):
            S.op("dve", lambda e, h=h: e.scalar_tensor_tensor(
                out=S32[:, h * 128:(h + 1) * 128], in0=S32[:, h * 128:(h + 1) * 128],
                scalar=dec[:, 2 * h + c:2 * h + c + 1], in1=pkv[0:64, h * 128:(h + 1) * 128],
                op0=ALU.mult, op1=ALU.add), r=["S32", kx("dec"), pkvk], w=["S32"])
        pp.free(pkvk)
        if full:
            nb = G["Sbf"][(g + 1) % 2]
            S.op("pool", lambda e: e.tensor_copy(out=nb[:], in_=S32[:]), r=["S32"], w=[("Sbf", (g + 1) % 2)])

    k0 = kv_mm(0)
    upd(0, *k0)
    yield
    k1 = kv_mm(1)
    if full:
        yield
    po = pok = None
    if full:
        po, pok = pp.alloc()
        for h in range(4):
            S.op("pe", lambda e, h=h: e.matmul(po[:, h * 128:(h + 1) * 128], lhsT=attm[:, h * 128:(h + 1) * 128],
                                               rhs=vsrc[:, h * 128:(h + 1) * 128], start=True, stop=False),
                 r=[kx("attm"), vkey], w=[pok])
            for c in range(2):
                g = 2 * ti + c
                sb_ = G["Sbf"][g % 2]
                S.op("pe", lambda e, h=h, c=c, sb_=sb_: e.matmul(po[c * 64:(c + 1) * 64, h * 128:(h + 1) * 128],
                                                                 lhsT=qkT[:, h, c * 64:(c + 1) * 64],
                                                                 rhs=sb_[:, h * 128:(h + 1) * 128],
                                                                 start=False, stop=True),
                     r=[kx("qkT"), ("Sbf", g % 2)], w=[pok])
    if full:
        yield
    upd(1, *k1)
    return po, pok


def _gla_bufs(C, K, pp, full):
    G = {"K": K, "pp": pp}
    G["adT"] = C.sb("g_adT", [16, 128])
    G["wup"] = C.sb("g_wup", [16, 256])
    G["bup"] = C.sb("g_bup", [1, 256])
    G["el"] = C.sb("g_el", [128, 256])
    G["dec"] = C.sb("g_dec", [64, 8])
    G["enb"] = C.sb("g_enb", [128, 256])
    G["ebl"] = C.sb("g_ebl", [128, 256])
    G["edec"] = C.sb("g_edec", [128, 256])
    G["kdec"] = C.sb("g_kdec", [128, 256], BF16)
    G["S32"] = C.sb("g_S32", [64, 512])
    if full:
        G["eb"] = C.sb("g_eb", [128, 256])
        G["qke"] = C.sb("g_qke", [128, 512], BF16)
        G["qkT"] = C.sb("g_qkT", [64, 8, 128], BF16)
        G["attm"] = C.sb("g_attm", [128, 512], BF16)
        G["Sbf"] = [C.sb("g_Sbf0", [64, 512], BF16), C.sb("g_Sbf1", [64, 512], BF16)]
    else:
        G["lacc"] = C.sb("g_lacc", [64, 8])
    return G


def phase_a(C, io, W, T):
    S = C.S
    K = _load_consts(C, ["c_ident", "c_tri", "c_blk", "c_cind", "c_ones1", "c_onehot"])
    wb = W["wb"]
    pp = PsPool(C, 6, "pp")
    xt = [C.sb(f"xt{i}", [128, 1024]) for i in range(3)]
    _load(C, xt[0][:], io["x"][0:128, :], ("xt", 0))
    ecrep, Dc = T["ecrep"], T["Dc"]
    _load(C, ecrep[:], io["rb31_rep"], "ecrep")
    S.op("act", lambda e: e.activation(out=ecrep[:], in_=ecrep[:], func=AF.Exp), r=["ecrep"], w=["ecrep"])

    def per_head(h, t32, tk):
        S.op("dve", lambda e: e.tensor_scalar(out=Dc[:, h, :], in0=t32[:, 3, :], scalar1=ecrep[:, h:h + 1], scalar2=None,
                                              op0=ALU.subtract), r=[tk, "ecrep"], w=["Dc"])
    tables = _bias_tables(C, K, io, pp, T["EB"], per_head)
    G0 = _gla_bufs(C, K, pp, False)
    _load(C, G0["wup"][:], io["gla_w_up"], "wup")
    _load(C, G0["bup"][:], io["gla_b_up"], "bup")
    S.op("dve", lambda e: e.memset(G0["S32"][:], 0.0), w=["S32"])
    S.op("dve", lambda e: e.memset(G0["lacc"][:], 0.0), w=["lacc"])
    G1 = dict(G0)
    for n, shp, dt in [("adT", [16, 128], F32), ("el", [128, 256], F32), ("dec", [64, 8], F32), ("enb", [128, 256], F32),
                       ("ebl", [128, 256], F32), ("edec", [128, 256], F32), ("kdec", [128, 256], BF16)]:
        G1[n] = C.sb("g1_" + n, shp, dt)
    G0["sfx"], G1["sfx"] = "_0", "_1"
    G = G0
    Gs = [G0, G1]
    pT = C.ps("pT", [128, 8, 128], BF16)
    nbs = [{"junk": C.sb(f"junk{p}", [128, 1024], BF16), "ss": C.sb(f"ss{p}", [128, 1]), "rstd": C.sb(f"rstd{p}", [128, 1]),
            "xn": C.sb(f"xn{p}", [128, 1024], BF16), "pT": pT, "xnT": C.sb(f"xnT{p}", [128, 8, 128], BF16)} for p in range(2)]
    vbfs = [C.sb(f"vbf{p}", [128, 512], BF16) for p in range(2)]

    def A(ti):
        par = ti % 2
        nb, vbf, Gp = nbs[par], vbfs[par], Gs[par]
        tag = f"a{par}"
        if ti + 1 < NT:
            _load(C, xt[(ti + 1) % 3][:], io["x"][(ti + 1) * 128:(ti + 2) * 128, :], ("xt", (ti + 1) % 3))
        _norm_transpose(C, xt[ti % 3][:], ("xt", ti % 3), nb, K["c_ident"], tag, grep=W["g0"])
        yield
        xnT = nb["xnT"]
        rk = lambda k: [tag + "xnT", ("wE", k)]
        pA, pAk = pp.alloc()
        _mm8(C, pA[:, 256:512], lambda k: xnT[:, k, :], lambda k: wb[:, k, 256:512], rk, pAk)
        yield
        pV, pVk = pp.alloc()
        _mm8(C, pV[:, 0:512], lambda k: xnT[:, k, :], lambda k: wb[:, k, 512:1024], rk, pVk)
        S.op("act", lambda e: e.copy(out=vbf[:], in_=pV[:, 0:512]), r=[pVk], w=["vbf" + tag])
        pp.free(pVk)
        yield
        pM, pMk = pp.alloc()
        _mm8(C, pM[0:16, 0:128], lambda k: wb[:, k, 1024:1040], lambda k: xnT[:, k, :], rk, pMk)
        S.op("dve", lambda e: e.tensor_copy(out=Gp["adT"][:], in_=pM[0:16, 0:128]), r=[pMk], w=["adT" + tag])
        pp.free(pMk)
        yield
        yield from _gla_tile(C, Gp, ti, pA, pAk, vbf, "vbf" + tag, "adT" + tag, False)
        pp.free(pAk)

    active, nxt = [A(0), tables], 1
    while active:
        for g in list(active):
            try:
                next(g)
            except StopIteration:
                active.remove(g)
        if sum(1 for g in active if g is not tables) < 2 and nxt < NT:
            active.append(A(nxt))
            nxt += 1
    S.op("sp", lambda e: e.dma_start(out=io["o_summ"][:, 0:512], in_=G["S32"][:]), r=["S32"], dma=True, final=True)
    S.op("sp", lambda e: e.dma_start(out=io["o_summ"][:, 512:520], in_=G["lacc"][:]), r=["lacc"], dma=True, final=True)


def _bcast_mid(ap, n):
    pat = [list(p) for p in ap.ap]
    return bass.AP(ap.tensor, ap.offset, [pat[0], [0, n]] + pat[1:])


def _bias_tables(C, K, io, pp, EB, per_head=None):
    S = C.S
    rb = C.sb("rb", [32, 8])
    _load(C, rb[:], io["rel_bias"], "rb")
    S.op("act", lambda e: e.activation(out=rb[:], in_=rb[:], func=AF.Exp), r=["rb"], w=["rb"])
    lh = [C.sb(f"lh{i}", [32, 128]) for i in range(2)]
    frep = [C.sb(f"frep{i}", [128, 1024]) for i in range(2)]
    T32 = [C.sb(f"T32_{i}", [128, 4, 128]) for i in range(2)]
    zt = io["zscr"]

    def produce(h):
        l = lh[h % 2]
        lk = ("lh", h % 2)
        S.op("dve", lambda e: e.tensor_copy(out=l[:], in_=rb[:, h:h + 1].to_broadcast([32, 128])), r=["rb"], w=[lk])
        fr = frep[h % 2]
        fk = ("frep", h % 2)
        for half in range(2):
            pf, pfk = pp.alloc()
            S.op("pe", lambda e, half=half, pf=pf: e.matmul(pf[:, 0:512], lhsT=l[:], rhs=K["c_onehot"][:, half * 512:(half + 1) * 512],
                                                            start=True, stop=True), r=[lk, "c_onehot"], w=[pfk])
            S.op("act", lambda e, half=half, pf=pf: e.copy(out=fr[:, half * 512:(half + 1) * 512], in_=pf[:, 0:512]), r=[pfk], w=[fk])
            pp.free(pfk)
        S.op("sp", lambda e: e.dma_start(out=zt[h], in_=fr[:]), r=[fk], w=[("z", h)], dma=True)

    def consume(h):
        src_ap = bass.AP(zt.tensor, zt.offset + h * 128 * 1024 + 127, [[1023, 128], [256, 4], [1, 128]])
        t32 = T32[h % 2]
        tk = ("T32", h % 2)
        S.op("sp", lambda e: e.dma_start(out=t32[:], in_=src_ap), r=[("z", h)], w=[tk], dma=True)
        S.op("dve", lambda e: e.tensor_copy(out=EB[:, :, h, :], in_=t32[:]), r=[tk], w=["EB"])
        if per_head is not None:
            per_head(h, t32, tk)

    for h in range(8):
        produce(h)
        yield
        if h >= 1:
            consume(h - 1)
            yield
    consume(7)
    yield


def phase_b(C, io, W, T):
    S = C.S
    K = _load_consts(C, ["c_ident", "c_tri", "c_blk", "c_cind", "c_ones1", "c_maskbd"])
    ident = K["c_ident"]
    pp = PsPool(C, 6, "pp")
    flags = C.sb("flags", [128, 16])
    _load(C, flags[:], io["t_flags"], "flags")
    wb, wo, w1, wq = W["wb"], W["wo"], W["w1"], W["wq"]
    EB = T["EB"]
    EBfirst = C.sb("EBfirst", [128, 8, 128], BF16)
    sinkexp = C.sb("sinkexp", [128, 8])
    ggain = C.sb("ggain", [128, 512])
    G = _gla_bufs(C, K, pp, True)
    G["pT2"] = C.ps("pT2", [128, 8, 128], BF16)
    S32 = G["S32"]
    if True:
        S.op("dve", lambda e: e.tensor_scalar(out=EBfirst[:], in0=EB[:, 1, :, :], scalar1=flags[:, 0:1], scalar2=None, op0=ALU.mult),
             r=["EB", "flags"], w=["EBfirst"])
        _load(C, sinkexp[:], io["sinks_rep"], "sinkexp")
        S.op("act", lambda e: e.activation(out=sinkexp[:], in_=sinkexp[:], func=AF.Exp), r=["sinkexp"], w=["sinkexp"])
        _load(C, ggain[:], io["gla_gain_rep"], "ggain")
        _load(C, G["wup"][:], io["gla_w_up"], "wup")
        _load(C, G["bup"][:], io["gla_b_up"], "bup")
        S.op("dve", lambda e: e.memset(S32[:], 0.0), w=["S32"])
        lall = C.sb("lall", [64, 4, 8])
        h1t = C.sb("h1t", [128, 1024])
        s1 = h1t[0:64, 0:512]
        S.op("sp", lambda e: e.dma_start(out=lall[:], in_=io["summ_all"][:, :, 512:520].rearrange("s p c -> p s c")),
             r=["cc1"], w=["lall"], dma=True)
        S.op("act", lambda e: e.activation(out=lall[:], in_=lall[:], func=AF.Exp, scale=-1.0 / 16), r=["lall"], w=["lall"])
        aseg = C.sb("aseg", [64, 4, 4])
        la4 = lall[:].rearrange("p s (h c) -> p s h c", c=2)
        S.op("dve", lambda e: e.tensor_tensor(out=aseg[:], in0=la4[:, :, :, 0], in1=la4[:, :, :, 1], op=ALU.mult), r=["lall"], w=["aseg"])
        for sp in range(4):
            m = flags[0:64, 4 + sp:5 + sp]
            S.op("sp", lambda e, sp=sp: e.dma_start(out=s1[:], in_=io["summ_all"][sp][:, 0:512]), r=["cc1"], w=["s1"], dma=True)
            S.op("dve", lambda e, sp=sp, m=m: e.tensor_scalar(out=aseg[:, sp, :], in0=aseg[:, sp, :], scalar1=-1.0, scalar2=m,
                                                              op0=ALU.add, op1=ALU.mult), r=["aseg", "flags"], w=["aseg"])
            S.op("dve", lambda e, sp=sp: e.tensor_scalar(out=aseg[:, sp, :], in0=aseg[:, sp, :], scalar1=1.0, scalar2=None, op0=ALU.add),
                 r=["aseg"], w=["aseg"])
            S.op("dve", lambda e, m=m: e.tensor_scalar(out=s1[:], in0=s1[:], scalar1=m, scalar2=None, op0=ALU.mult),
                 r=["s1", "flags"], w=["s1"])
            for h in range(4):
                S.op("dve", lambda e, sp=sp, h=h: e.scalar_tensor_tensor(
                    out=S32[:, h * 128:(h + 1) * 128], in0=S32[:, h * 128:(h + 1) * 128], scalar=aseg[:, sp, h:h + 1],
                    in1=s1[:, h * 128:(h + 1) * 128], op0=ALU.mult, op1=ALU.add), r=["S32", "aseg", "s1"], w=["S32"])
        S.op("pool", lambda e: e.tensor_copy(out=G["Sbf"][0][:], in_=S32[:]), r=["S32"], w=[("Sbf", 0)])

    xt = [C.sb(f"xt{i}", [128, 1024]) for i in range(3)]
    nb = {"junk": C.sb("junk", [128, 1024], BF16), "ss": C.sb("ss", [128, 1]), "rstd": C.sb("rstd", [128, 1]),
          "xn": C.sb("xn", [128, 1024], BF16), "pT": C.ps("pT", [128, 8, 128], BF16), "xnT": C.sb("xnT", [128, 8, 128], BF16)}
    nb1 = dict(nb)
    nb1["junk"] = C.sb("junk1", [128, 1024], BF16)
    stmp1 = C.sb("stmp1", [128, 512])
    stmp2 = C.sb("stmp2", [128, 512])
    nb1["ss"] = C.sb("ss1", [128, 1]); nb1["rstd"] = C.sb("rstd1", [128, 1])
    nb1["xn"] = C.sb("xn1", [128, 1024], BF16); nb1["xnT"] = C.sb("xn1T", [128, 8, 128], BF16)
    vbf = C.sb("vbf", [128, 512], BF16)
    KT = [C.sb(f"KT{i}", [128, 128], BF16) for i in range(2)]
    VA = [C.sb(f"VA{i}", [128, 2, 66], BF16) for i in range(2)]
    for i in range(2):
        S.op("pool", lambda e, i=i: e.memset(VA[i][:], 1.0), w=[("VA", i)])
    QT = C.sb("QT", [128, 4, 128], BF16)
    sgB = C.sb("sgB", [128, 512])
    stmp = C.sb("stmp", [128, 512])
    sgA = C.sb("sgA", [128, 512])
    gsg = C.sb("gsg", [128, 512])
    Ee = C.sb("Ee", [128, 512], BF16)
    Pt = [C.sb(f"Pt{i}", [128, 512], BF16) for i in range(2)]
    den = C.sb("den", [128, 4]); rden = C.sb("rden", [128, 4])
    og = [C.sb(f"og{i}", [128, 1024], BF16) for i in range(2)]
    ogT = C.sb("ogT", [128, 8, 128], BF16)
    ssA = C.sb("ssA", [128, 4]); rsA = C.sb("rsA", [128, 4])
    junkA = C.sb("junkA", [128, 128], BF16)
    QT1 = C.sb("L1QT", [128, 8, 128], BF16)
    KT1 = C.sb("L1KT", [128, 2, 128], BF16)
    V1 = C.sb("L1V", [128, 2, 130], BF16)
    S.op("pool", lambda e: e.memset(V1[:], 1.0), w=["V1"])
    sg1 = C.sb("sg1", [128, 1024], BF16)
    kmacc = C.sb("kmacc", [128, 2, 8])
    kmt = C.sb("kmt", [128, 2])
    pT = nb["pT"]

    def xsrc(i):
        return io["x_halo"] if i == 0 else io["x"][(i - 1) * 128:i * 128, :]

    def X(i):
        ti = i - 1
        if i + 1 <= NT:
            _load(C, xt[(i + 1) % 3][:], xsrc(i + 1), ("xt", (i + 1) % 3))
        xk = ("xt", i % 3)
        xti = xt[i % 3]
        _norm_transpose(C, xti[:], xk, nb, ident, "n0", grep=W["g0"])
        yield
        xnT = nb["xnT"]
        rk = lambda k: ["n0xnT", ("wE", k), ("wE0", k), ("wE1", k), ("wE2", k)]
        ogc = og[ti % 2] if ti >= 0 else None
        oka, okb = ("og", ti % 2, "a"), ("og", ti % 2, "b")

        def chain_s():
            pM, pMk = pp.alloc()
            _mm8(C, pM[:, 0:128], lambda k: xnT[:, k, :], lambda k: wb[:, k, 2192:2320], rk, pMk)
            _mm8(C, pM[:, 128:256], lambda k: wb[:, k, 2064:2192], lambda k: xnT[:, k, :], rk, pMk)
            va, kt_ = VA[i % 2], KT[i % 2]
            S.op("dve", lambda e: e.tensor_copy(out=va[:, :, 0:64], in_=pM[:, 0:128].rearrange("p (k d) -> p k d", k=2)),
                 r=[pMk], w=[("VA", i % 2)])
            S.op("dve", lambda e: e.tensor_copy(out=kt_[:], in_=pM[:, 128:256]), r=[pMk], w=[("KT", i % 2)])
            pp.free(pMk)
            yield
            if ti < 0:
                return
            pQ, pQk = pp.alloc()
            for a in range(4):
                _mm8(C, pQ[:, a * 128:(a + 1) * 128], lambda k, a=a: wq[:, k, a * 128:(a + 1) * 128],
                     lambda k: xnT[:, k, :], lambda k, a=a: ["n0xnT", ("wq", k, a)], pQk)
            S.op("act", lambda e: e.activation(out=QT[:], in_=pQ[:, 0:512].rearrange("p (a t) -> p a t", a=4), func=AF.Copy, scale=0.125),
                 r=[pQk], w=["QT"])
            pp.free(pQk)
            yield
            pG, pGk = pp.alloc()
            _mm8(C, pG[:, 0:512], lambda k: xnT[:, k, :], lambda k: wb[:, k, 2320:2832], rk, pGk)
            _silu(C, sgB[:], pG[:, 0:512], pGk, stmp[:], "stmp", "sgB")
            pp.free(pGk)
            yield
            for kv in range(2):
                for kt in range(2):
                    ktile = KT[(i - 1 + kt) % 2]
                    pS, pSk = pp.alloc()
                    S.op("pe", lambda e, ktile=ktile, kv=kv, pS=pS: e.matmul(pS[:, 0:512], lhsT=ktile[kv * 64:(kv + 1) * 64, :],
                                                                            rhs=QT[kv * 64:(kv + 1) * 64, :, :], start=True, stop=True),
                         r=[("KT", (i - 1 + kt) % 2), "QT"], w=[pSk])
                    S.op("act", lambda e, pS=pS: e.activation(out=Ee[:], in_=pS[:, 0:512], func=AF.Exp), r=[pSk], w=["Ee"])
                    pp.free(pSk)
                    if kt == 0:
                        tab = EBfirst[:, kv * 4:(kv + 1) * 4, :] if ti == 0 else EB[:, 1, kv * 4:(kv + 1) * 4, :]
                        tk = "EBfirst" if ti == 0 else "EB"
                    else:
                        tab = EB[:, 0, kv * 4:(kv + 1) * 4, :]
                        tk = "EB"
                    S.op("dve", lambda e, kt=kt, tab=tab: e.tensor_tensor(out=Pt[kt][:].rearrange("p (a t) -> p a t", a=4),
                                                                          in0=Ee[:].rearrange("p (a t) -> p a t", a=4), in1=tab, op=ALU.mult),
                         r=["Ee", tk], w=[("Pt", kt)])
                    yield
                pO, pOk = pp.alloc()
                for a in range(4):
                    for kt in range(2):
                        vat = VA[(i - 1 + kt) % 2]
                        S.op("pe", lambda e, a=a, kt=kt, vat=vat, pO=pO, kv=kv: e.matmul(
                            pO[:, a * 65:(a + 1) * 65], lhsT=Pt[kt][:, a * 128:(a + 1) * 128], rhs=vat[:, kv, 0:65],
                            start=(kt == 0), stop=(kt == 1)), r=[("Pt", kt), ("VA", (i - 1 + kt) % 2)], w=[pOk])
                pO3 = pO[:, 0:260].rearrange("p (a d) -> p a d", a=4)
                S.op("dve", lambda e, pO3=pO3, kv=kv: e.tensor_tensor(out=den[:], in0=pO3[:, :, 64], in1=sinkexp[:, kv * 4:(kv + 1) * 4], op=ALU.add),
                     r=[pOk, "sinkexp"], w=["den"])
                S.op("dve", lambda e: e.reciprocal(out=rden[:], in_=den[:]), r=["den"], w=["rden"])
                for a in range(4):
                    h = kv * 4 + a
                    S.op("dve", lambda e, a=a, h=h, pO3=pO3: e.scalar_tensor_tensor(
                        out=ogc[:, 512 + h * 64:512 + (h + 1) * 64], in0=pO3[:, a, 0:64], scalar=rden[:, a:a + 1],
                        in1=sgB[:, h * 64:(h + 1) * 64], op0=ALU.mult, op1=ALU.mult), r=[pOk, "rden", "sgB"], w=[okb])
                pp.free(pOk)
                yield

        def chain_g():
            if ti < 0:
                return
            pD, pDk = pp.alloc()
            _mm8(C, pD[0:16, 0:128], lambda k: wb[:, k, 1024:1040], lambda k: xnT[:, k, :], rk, pDk)
            S.op("dve", lambda e: e.tensor_copy(out=G["adT"][:], in_=pD[0:16, 0:128]), r=[pDk], w=["adT"])
            pp.free(pDk)
            pV, pVk = pp.alloc()
            _mm8(C, pV[:, 0:512], lambda k: xnT[:, k, :], lambda k: wb[:, k, 512:1024], rk, pVk)
            S.op("act", lambda e: e.copy(out=vbf[:], in_=pV[:, 0:512]), r=[pVk], w=["vbf"])
            pp.free(pVk)
            yield
            pA, pAk = pp.alloc()
            _mm8(C, pA[:, 0:512], lambda k: xnT[:, k, :], lambda k: wb[:, k, 0:512], rk, pAk)
            yield
            gla = _gla_tile(C, G, ti, pA, pAk, vbf, "vbf", "adT", True)
            res = None
            step = 0
            while True:
                try:
                    next(gla)
                except StopIteration as stop:
                    res = stop.value
                    break
                step += 1
                if step == 2:
                    pG2, pG2k = pp.alloc()
                    _mm8(C, pG2[:, 0:512], lambda k: xnT[:, k, :], lambda k: wb[:, k, 1040:1552], rk, pG2k)
                    _silu(C, sgA[:], pG2[:, 0:512], pG2k, stmp2[:], "stmp2", "sgA")
                    pp.free(pG2k)
                    S.op("pool", lambda e: e.tensor_tensor(out=gsg[:], in0=sgA[:], in1=ggain[:], op=ALU.mult), r=["sgA", "ggain"], w=["gsg"])
                yield
            po, pok = res
            pp.free(pAk)
            for h in range(4):
                S.op("act", lambda e, h=h: e.activation(out=junkA[:], in_=po[:, h * 128:(h + 1) * 128], func=AF.Square,
                                                        accum_out=ssA[:, h:h + 1]), r=[pok], w=["junkA", "ssA"])
            S.op("act", lambda e: e.activation(out=ssA[:], in_=ssA[:], func=AF.Ln, scale=1.0 / 128, bias=EPS), r=["ssA"], w=["ssA"])
            S.op("act", lambda e: e.activation(out=rsA[:], in_=ssA[:], func=AF.Exp, scale=-0.5), r=["ssA"], w=["rsA"])
            for h in range(4):
                S.op("dve", lambda e, h=h: e.scalar_tensor_tensor(
                    out=ogc[:, h * 128:(h + 1) * 128], in0=po[:, h * 128:(h + 1) * 128], scalar=rsA[:, h:h + 1],
                    in1=gsg[:, h * 128:(h + 1) * 128], op0=ALU.mult, op1=ALU.mult), r=[pok, "rsA", "gsg"], w=[oka])
            pp.free(pok)
            yield

        gens = [chain_g(), chain_s()]
        while gens:
            for g in list(gens):
                try:
                    next(g)
                except StopIteration:
                    gens.remove(g)
                yield

    def Y(ti):
        i = ti + 1
        xk = ("xt", i % 3)
        xti = xt[i % 3]
        ogc = og[ti % 2]
        oka, okb = ("og", ti % 2, "a"), ("og", ti % 2, "b")
        for k in range(8):
            S.op("pe", lambda e, k=k: e.transpose(out=pT[:, k, :], in_=ogc[:, k * 128:(k + 1) * 128], identity=ident[:]),
                 r=[oka, okb, "c_ident"], w=["pTbank"])
        S.op("act", lambda e: e.copy(out=ogT[:], in_=pT[:]), r=["pTbank"], w=["ogT"])
        yield
        for nbk in range(2):
            pH, pHk = pp.alloc()
            _mm8(C, pH[:, 0:512], lambda k: ogT[:, k, :], lambda k, nbk=nbk: wo[:, k, nbk * 512:(nbk + 1) * 512],
                 lambda k: ["ogT", ("wO", k)], pHk)
            S.op("dve", lambda e, nbk=nbk, pH=pH: e.tensor_tensor(out=h1t[:, nbk * 512:(nbk + 1) * 512], in0=pH[:, 0:512],
                                                                  in1=xti[:, nbk * 512:(nbk + 1) * 512], op=ALU.add),
                 r=[pHk, xk], w=["h1t"])
            pp.free(pHk)
            yield
        S.op("sp", lambda e: e.dma_start(out=io["o_h1"][ti * 128:(ti + 1) * 128, :], in_=h1t[:]), r=["h1t"], dma=True, final=True)
        _norm_transpose(C, h1t[:], "h1t", nb1, ident, "n1", grep=W["g1"])
        yield
        x1T = nb1["xnT"]
        rk1 = lambda k: ["n1xnT", ("w1", k)]
        for half in range(2):
            pq, pqk = pp.alloc()
            for a in range(4):
                h = half * 4 + a
                _mm8(C, pq[:, a * 128:(a + 1) * 128], lambda k, h=h: w1[:, k, h * 128:(h + 1) * 128], lambda k: x1T[:, k, :], rk1, pqk)
                yield
            S.op("act", lambda e, half=half, pq=pq: e.activation(out=QT1[:, half * 4:(half + 1) * 4, :],
                                                                 in_=pq[:, 0:512].rearrange("p (a t) -> p a t", a=4),
                                                                 func=AF.Copy, scale=128.0 ** -0.5), r=[pqk], w=["QT1"])
            pp.free(pqk)
        S.op("sp", lambda e: e.dma_start(out=io["o_qT"][ti], in_=QT1[:].rearrange("p h t -> p (h t)")), r=["QT1"], dma=True, final=True)
        pk, pkk = pp.alloc()
        for kv in range(2):
            _mm8(C, pk[:, kv * 128:(kv + 1) * 128], lambda k, kv=kv: w1[:, k, 1024 + kv * 128:1024 + (kv + 1) * 128],
                 lambda k: x1T[:, k, :], rk1, pkk)
        _mm8(C, pk[:, 256:512], lambda k: x1T[:, k, :], lambda k: w1[:, k, 1280:1536], rk1, pkk)
        S.op("dve", lambda e: e.tensor_copy(out=KT1[:], in_=pk[:, 0:256].rearrange("p (k t) -> p k t", k=2)), r=[pkk], w=["KT1"])
        S.op("dve", lambda e: e.tensor_copy(out=V1[:, :, 0:128], in_=pk[:, 256:512].rearrange("p (k d) -> p k d", k=2)),
             r=[pkk], w=["V1"])
        pp.free(pkk)
        yield
        S.op("sp", lambda e: e.dma_start(out=io["o_K"][:, :, ti * 128:(ti + 1) * 128], in_=KT1[:]), r=["KT1"], dma=True, final=True)
        S.op("sp", lambda e: e.dma_start(out=io["o_V"][ti * 128:(ti + 1) * 128, :].rearrange("p (k d) -> p k d", k=2), in_=V1[:, :, 0:128]),
             r=["V1"], dma=True, final=True)
        for half in range(2):
            pg, pgk = pp.alloc()
            _mm8(C, pg[:, 0:512], lambda k: x1T[:, k, :], lambda k, half=half: w1[:, k, 1536 + half * 512:1536 + (half + 1) * 512], rk1, pgk)
            _silu(C, sg1[:, half * 512:(half + 1) * 512], pg[:, 0:512], pgk, stmp1[:], "stmp1", "sg1")
            pp.free(pgk)
            yield
        S.op("sp", lambda e: e.dma_start(out=io["o_sg"][ti * 128:(ti + 1) * 128, :], in_=sg1[:]), r=["sg1"], dma=True, final=True)

    def interleave(*gens):
        gens = [g for g in gens if g is not None]
        while gens:
            for g in list(gens):
                try:
                    next(g)
                except StopIteration:
                    gens.remove(g)

    _load(C, xt[0][:], xsrc(0), ("xt", 0))
    interleave(X(0))
    interleave(X(1))
    def spread(y, x, every=2):
        if x is None:
            interleave(y)
            return
        ydone = xdone = False
        while not (ydone and xdone):
            if not ydone:
                try:
                    next(y)
                except StopIteration:
                    ydone = True
            for _ in range(every):
                if not xdone:
                    try:
                        next(x)
                    except StopIteration:
                        xdone = True

    for ti in range(NT):
        spread(Y(ti), X(ti + 2) if ti + 2 <= NT else None)


def phase_c(C, io, T, gather2):
    S = C.S
    K = _load_consts(C, ["c_ident"])
    ident = K["c_ident"]
    pp = PsPool(C, 7, "pp")
    flags = C.sb("flags", [128, 16])
    _load(C, flags[:], io["t_flags"], "flags")
    tval = C.sb("tval", [128, 512]); tadd = C.sb("tadd", [128, 512]); tprev = C.sb("tprev", [128, 512])
    _load(C, tval[:], io["t_valid"], "tval")
    _load(C, tadd[:], io["t_addm"], "tadd")
    _load(C, tprev[:], io["t_prev"], "tprev")
    fgain = C.sb("fgain", [128, 1024])
    _load(C, fgain[:], io["fgain_rep"], "fgain")
    ecrep, EB, Dc = T["ecrep"], T["EB"], T["Dc"]
    wo = C.sb("wO2", [128, 8, 1024], BF16)
    KTall = C.sb("KTall", [128, 2, 8192], BF16)
    Vall = C.sb("Vall", [128, 64, 260], BF16)
    KTloc = C.sb("KTloc", [128, 2, 2048], BF16)
    Vloc = C.sb("Vloc", [128, 16, 260], BF16)
    kmT = C.sb("kmT", [128, 2, 32], BF16)
    Kprev = C.sb("Kprev", [128, 2, 128], BF16)
    Vprev = C.sb("Vprev", [128, 2, 130], BF16)
    S.op("dve", lambda e: e.memset(Vall[:], 1.0), w=[("Vall", sp, kv) for sp in range(4) for kv in range(2)])
    S.op("dve", lambda e: e.memset(Vloc[:], 1.0), w=[("Vloc", 0), ("Vloc", 1)])
    _load(C, KTloc[:], io["k_loc"], "KTloc")
    gather2()
    for sp in range(4):
        S.op("sp", lambda e, sp=sp: e.dma_start(out=KTall[:, :, sp * 2048:(sp + 1) * 2048], in_=io["k_all"][sp]),
             r=["ccK"], w=[("KTall", sp)], dma=True)
    for sp in range(4):
        for kv in range(2):
            C.S.op("sp", lambda e, sp=sp, kv=kv: e.dma_start(out=Vall[:, sp * 16:(sp + 1) * 16, kv * 130:kv * 130 + 128],
                                                             in_=io["v_all"][sp][:, kv * 128:(kv + 1) * 128].rearrange("(t p) d -> p t d", p=128)),
                   r=["ccV"], w=[("Vall", sp, kv)], dma=True)
    for kv in range(2):
        C.S.op("sp", lambda e, kv=kv: e.dma_start(out=Vloc[:, :, kv * 130:kv * 130 + 128],
                                                  in_=io["v_loc"][:, kv * 128:(kv + 1) * 128].rearrange("(t p) d -> p t d", p=128)),
               w=[("Vloc", kv)], dma=True)
    if True:
        for k in range(8):
            S.op("pool", lambda e, k=k: e.dma_start(out=wo[:, k, :], in_=io["w_out_odd"][k * 128:(k + 1) * 128, :]), w=[("wO2", k)], dma=True)

        kmf = C.sb("kmf", [128, 2, 32])
        for sp in range(4):
            for kv in range(2):
                S.op("dve", lambda e, sp=sp, kv=kv: e.tensor_reduce(
                    out=kmf[:, kv, sp * 8:(sp + 1) * 8], in_=KTall[:, kv, sp * 2048:(sp + 1) * 2048].rearrange("p (b t) -> p b t", b=8),
                    axis=AX.X, op=ALU.add), r=[("KTall", sp)], w=[("kmf", sp, kv)])
        S.op("dve", lambda e: e.tensor_scalar(out=kmT[:], in0=kmf[:], scalar1=1.0 / 256, scalar2=None, op0=ALU.mult),
             r=[("kmf", sp, kv) for sp in range(4) for kv in range(2)], w=["kmT"])
        for sp in range(3):
            ksrc = KTall[:, :, (16 * sp + 15) * 128:(16 * sp + 16) * 128]
            vsrc = Vall[:, 16 * sp + 15, :].rearrange("p (k d) -> p k d", k=2)
            m = flags[:, 8 + sp:9 + sp]
            if sp == 0:
                S.op("dve", lambda e, ksrc=ksrc, m=m: e.tensor_scalar(out=Kprev[:], in0=ksrc, scalar1=m, scalar2=None, op0=ALU.mult),
                     r=[("KTall", sp), "flags"], w=["Kprev"])
                S.op("dve", lambda e, vsrc=vsrc, m=m: e.tensor_scalar(out=Vprev[:], in0=vsrc, scalar1=m, scalar2=None, op0=ALU.mult),
                     r=[("Vall", sp, 0), ("Vall", sp, 1), "flags"], w=["Vprev"])
            else:
                S.op("dve", lambda e, ksrc=ksrc, m=m: e.scalar_tensor_tensor(out=Kprev[:], in0=ksrc, scalar=m, in1=Kprev[:],
                                                                            op0=ALU.mult, op1=ALU.add), r=[("KTall", sp), "flags", "Kprev"], w=["Kprev"])
                S.op("dve", lambda e, vsrc=vsrc, m=m: e.scalar_tensor_tensor(out=Vprev[:], in0=vsrc, scalar=m, in1=Vprev[:],
                                                                            op0=ALU.mult, op1=ALU.add), r=[("Vall", sp, 0), ("Vall", sp, 1), "flags", "Vprev"], w=["Vprev"])

    QT1 = [C.sb(f"QT1_{i}", [128, 8, 128], BF16) for i in range(3)]
    sgt = [C.sb(f"sgt{i}", [128, 1024], BF16) for i in range(4)]
    h1t = [C.sb(f"h1t{i}", [128, 1024]) for i in range(4)]
    accs = [C.sb(f"acc{i}", [128, 8, 130]) for i in range(2)]
    Pt = [C.sb(f"Pt{i}", [128, 512], BF16) for i in range(4)]
    Ee = [C.sb(f"Ee{i}", [128, 512], BF16) for i in range(2)]
    gms = [C.sb(f"gm{i}", [128, 8, 32]) for i in range(2)]
    top8s = [C.sb(f"top8{i}", [128, 8, 8]) for i in range(2)]
    sels = [C.sb(f"sel{i}", [128, 8, 32]) for i in range(2)]
    selws = [C.sb(f"selw{i}", [128, 8, 32]) for i in range(2)]
    tmps = [C.sb(f"tmpsel{i}", [128, 8, 32]) for i in range(2)]
    selps = [C.sb(f"selp{i}", [128, 8]) for i in range(2)]
    rden = C.sb("rden", [128, 8])
    og = C.sb("og", [128, 1024], BF16); ogT = C.sb("ogT", [128, 8, 128], BF16)
    pT = C.ps("pT", [128, 8, 128], BF16)
    h2 = C.sb("h2", [128, 1024]); junk = C.sb("junk", [128, 1024], BF16)
    ss = C.sb("ss", [128, 1]); rstd = C.sb("rstd", [128, 1]); outt = C.sb("outt", [128, 1024])
    cnt = {"pt": 0, "ee": 0}

    def ld(t):
        _load(C, QT1[t % 3][:].rearrange("p h t -> p (h t)"), io["q_t"][t], ("QT1", t % 3))
        _load(C, sgt[t % 4][:], io["sg"][t * 128:(t + 1) * 128, :], ("sgt", t % 4))
        _load(C, h1t[t % 4][:], io["h1"][t * 128:(t + 1) * 128, :], ("h1t", t % 4))

    def tile(t):
        if t + 1 < NT:
            ld(t + 1)
        q = QT1[t % 3]
        qk = ("QT1", t % 3)
        p = t % 2
        acc, gm, top8, sel, selw, tmp, selp = accs[p], gms[p], top8s[p], sels[p], selws[p], tmps[p], selps[p]
        kq = lambda n: (n, p)
        pg, pgk = pp.alloc()
        for h in range(8):
            S.op("pe", lambda e, h=h: e.matmul(pg[:, h * 32:(h + 1) * 32], lhsT=q[:, h, :], rhs=kmT[:, h // 4, :], start=True, stop=True),
                 r=[qk, "kmT"], w=[pgk])
        S.op("dve", lambda e: e.tensor_tensor(out=gm[:], in0=pg[:, 0:256].rearrange("p (h n) -> p h n", h=8),
                                              in1=_bcast_mid(tadd[:, t * 32:(t + 1) * 32], 8), op=ALU.add), r=[pgk, "tadd"], w=[kq("gm")])
        pp.free(pgk)
        for h in range(8):
            S.op("dve", lambda e, h=h: e.max(out=top8[:, h, :], in_=gm[:, h, :]), r=[kq("gm")], w=[("top8", p, h)])
            S.op("dve", lambda e, h=h: e.tensor_scalar(out=sel[:, h, :], in0=gm[:, h, :], scalar1=top8[:, h, 2:3], scalar2=None, op0=ALU.is_ge),
                 r=[kq("gm"), ("top8", p, h)], w=[("sel", p, h)])
        selk = [("sel", p, h) for h in range(8)]
        S.op("dve", lambda e: e.tensor_tensor(out=sel[:], in0=sel[:], in1=_bcast_mid(tval[:, t * 32:(t + 1) * 32], 8), op=ALU.mult),
             r=selk + ["tval"], w=selk)
        S.op("dve", lambda e: e.tensor_tensor(out=tmp[:], in0=sel[:], in1=_bcast_mid(tprev[:, t * 32:(t + 1) * 32], 8), op=ALU.mult),
             r=selk + ["tprev"], w=[kq("tmpsel")])
        S.op("dve", lambda e: e.tensor_reduce(out=selp[:], in_=tmp[:], axis=AX.X, op=ALU.add), r=[kq("tmpsel")], w=[kq("selp")])
        for h in range(8):
            S.op("dve", lambda e, h=h: e.tensor_scalar(out=selw[:, h, :], in0=sel[:, h, :], scalar1=ecrep[:, h:h + 1], scalar2=None, op0=ALU.mult),
                 r=selk + ["ecrep"], w=[kq("selw")])
        S.op("pool", lambda e: e.memset(acc[:], 0.0), w=[("acc", p, h) for h in range(8)])
        yield "G"

        def stage1(u):
            keys, table = u["keys"], u["table"]
            kv = u["kv"]
            u["pt"] = []
            for j, (kap, kkey, _, _) in enumerate(keys):
                pS, pSk = pp.alloc()
                S.op("pe", lambda e, kap=kap, pS=pS: e.matmul(pS[:, 0:512], lhsT=kap, rhs=q[:, kv * 4:(kv + 1) * 4, :], start=True, stop=True),
                     r=[kkey, qk], w=[pSk])
                pi = cnt["pt"] % 4
                cnt["pt"] += 1
                if table is None:
                    S.op("act", lambda e, pS=pS, pi=pi: e.activation(out=Pt[pi][:], in_=pS[:, 0:512], func=AF.Exp), r=[pSk], w=[("Pt", pi)])
                else:
                    ei = cnt["ee"] % 2
                    cnt["ee"] += 1
                    S.op("act", lambda e, pS=pS, ei=ei: e.activation(out=Ee[ei][:], in_=pS[:, 0:512], func=AF.Exp), r=[pSk], w=[("Ee", ei)])
                    tab, tkey = table[j]
                    S.op("dve", lambda e, pi=pi, ei=ei, tab=tab: e.tensor_tensor(out=Pt[pi][:].rearrange("p (a t) -> p a t", a=4),
                                                                                in0=Ee[ei][:].rearrange("p (a t) -> p a t", a=4), in1=tab, op=ALU.mult),
                         r=[("Ee", ei), tkey], w=[("Pt", pi)])
                pp.free(pSk)
                u["pt"].append(pi)

        def stage2(u):
            kv, keys, wf = u["kv"], u["keys"], u["w"]
            banks = [pp.alloc(), pp.alloc()]
            for a in range(4):
                pO, pOk = banks[a // 2]
                col = (a % 2) * 129
                for j, (_, _, vap, vkey) in enumerate(keys):
                    pi = u["pt"][j]
                    S.op("pe", lambda e, pO=pO, col=col, pi=pi, vap=vap, a=a, j=j: e.matmul(
                        pO[:, col:col + 129], lhsT=Pt[pi][:, a * 128:(a + 1) * 128], rhs=vap[:, kv, 0:129],
                        start=(j == 0), stop=(j == len(keys) - 1)), r=[("Pt", pi), vkey], w=[pOk])
            for a in range(4):
                pO, pOk = banks[a // 2]
                col = (a % 2) * 129
                h = kv * 4 + a
                sc, sck = wf(h)
                S.op("dve", lambda e, pO=pO, col=col, h=h, sc=sc: e.scalar_tensor_tensor(
                    out=acc[:, h, 0:129], in0=pO[:, col:col + 129], scalar=sc, in1=acc[:, h, 0:129], op0=ALU.mult, op1=ALU.add),
                    r=[pOk, ("acc", p, h)] + sck, w=[("acc", p, h)])
            pp.free(banks[0][1])
            pp.free(banks[1][1])

        units = []
        for n in range(min(32, 24 + t // 2)):
            sp = n // 8
            for kv in range(2):
                keys = []
                for kt in range(2):
                    g = 2 * n + kt
                    keys.append((KTall[:, kv, g * 128:(g + 1) * 128], ("KTall", sp),
                                 Vall[:, g, :].rearrange("p (k d) -> p k d", k=2), ("Vall", sp, kv)))
                units.append({"kv": kv, "keys": keys, "table": None,
                              "w": (lambda h, n=n: (selw[:, h, n:n + 1], [kq("selw")]))})
        for kv in range(2):
            keys, table = [], []
            if t % 2 == 1:
                keys.append((KTloc[:, kv, (t - 1) * 128:t * 128], "KTloc", Vloc[:, t - 1, :].rearrange("p (k d) -> p k d", k=2), ("Vloc", kv)))
                table.append((EB[:, 3, kv * 4:(kv + 1) * 4, :], "EB"))
            keys.append((KTloc[:, kv, t * 128:(t + 1) * 128], "KTloc", Vloc[:, t, :].rearrange("p (k d) -> p k d", k=2), ("Vloc", kv)))
            table.append((EB[:, 2, kv * 4:(kv + 1) * 4, :], "EB"))
            units.append({"kv": kv, "keys": keys, "table": table, "w": (lambda h: (1.0, []))})
        if t % 2 == 0:
            for kv in range(2):
                if t == 0:
                    keys = [(Kprev[:, kv, :], "Kprev", Vprev, "Vprev")]
                else:
                    keys = [(KTloc[:, kv, (t - 1) * 128:t * 128], "KTloc", Vloc[:, t - 1, :].rearrange("p (k d) -> p k d", k=2), ("Vloc", kv))]
                units.append({"kv": kv, "keys": keys, "table": [(Dc[:, kv * 4:(kv + 1) * 4, :], "Dc")],
                              "w": (lambda h: (selp[:, h:h + 1], [kq("selp")]))})
        stage1(units[0])
        for ui in range(len(units)):
            if ui + 1 < len(units):
                stage1(units[ui + 1])
            stage2(units[ui])
            yield "U"
        yield "UE"

        S.op("dve", lambda e: e.reciprocal(out=rden[:], in_=acc[:, :, 128]), r=[("acc", p, h) for h in range(8)], w=["rden"])
        sg_ = sgt[t % 4]
        for h in range(8):
            S.op("dve", lambda e, h=h: e.scalar_tensor_tensor(out=og[:, h * 128:(h + 1) * 128], in0=acc[:, h, 0:128], scalar=rden[:, h:h + 1],
                                                              in1=sg_[:, h * 128:(h + 1) * 128], op0=ALU.mult, op1=ALU.mult),
                 r=[("acc", p, h), "rden", ("sgt", t % 4)], w=[("og", h)])
        yield "E"
        for k in range(8):
            S.op("pe", lambda e, k=k: e.transpose(out=pT[:, k, :], in_=og[:, k * 128:(k + 1) * 128], identity=ident[:]),
                 r=[("og", k), "c_ident"], w=["pTbank"])
        S.op("act", lambda e: e.copy(out=ogT[:], in_=pT[:]), r=["pTbank"], w=["ogT"])
        yield "E"
        hh = h1t[t % 4]
        for nbk in range(2):
            pH, pHk = pp.alloc()
            _mm8(C, pH[:, 0:512], lambda k: ogT[:, k, :], lambda k, nbk=nbk: wo[:, k, nbk * 512:(nbk + 1) * 512],
                 lambda k: ["ogT", ("wO2", k)], pHk)
            S.op("dve", lambda e, nbk=nbk, pH=pH: e.tensor_tensor(out=h2[:, nbk * 512:(nbk + 1) * 512], in0=pH[:, 0:512],
                                                                  in1=hh[:, nbk * 512:(nbk + 1) * 512], op=ALU.add),
                 r=[pHk, ("h1t", t % 4)], w=["h2"])
            pp.free(pHk)
            yield "E"
        S.op("act", lambda e: e.activation(out=junk[:], in_=h2[:], func=AF.Square, accum_out=ss[:]), r=["h2"], w=["junk", "ss"])
        S.op("act", lambda e: e.activation(out=ss[:], in_=ss[:], func=AF.Ln, scale=1.0 / 1024, bias=EPS), r=["ss"], w=["ss"])
        S.op("act", lambda e: e.activation(out=rstd[:], in_=ss[:], func=AF.Exp, scale=-0.5), r=["ss"], w=["rstd"])
        S.op("dve", lambda e: e.scalar_tensor_tensor(out=outt[:], in0=h2[:], scalar=rstd[:], in1=fgain[:], op0=ALU.mult, op1=ALU.mult),
             r=["h2", "rstd", "fgain"], w=["outt"])
        S.op("sp", lambda e, t=t: e.dma_start(out=io["out"][t * 128:(t + 1) * 128, :], in_=outt[:]), r=["outt"], dma=True, final=True)

    ld(0)
    gens = [tile(t) for t in range(NT)]
    done_gate = set()

    def run_until(g, marks):
        while True:
            try:
                r = next(g)
            except StopIteration:
                return None
            if r in marks:
                return r

    run_until(gens[0], ("G",))
    done_gate.add(0)
    for t in range(NT):
        side = []
        if t >= 1:
            side.append(gens[t - 1])
        ui = 0
        while True:
            r = next(gens[t])
            if r == "UE":
                break
            ui += 1
            if side:
                try:
                    next(side[0])
                except StopIteration:
                    side = []
            elif t + 1 < NT and (t + 1) not in done_gate:
                run_until(gens[t + 1], ("G",))
                done_gate.add(t + 1)
        for g in side:
            for _ in g:
                pass
        if t + 1 < NT and (t + 1) not in done_gate:
            run_until(gens[t + 1], ("G",))
            done_gate.add(t + 1)
    for _ in gens[NT - 1]:
        pass


RG = [[0, 1, 2, 3], [4, 5, 6, 7]]


def build_fused():
    nc = bass.Bass("TRN2", target_bir_lowering=False)
    dc = {}
    D = lambda n, s, d=F32: nc.dram_tensor(n, list(s), d).ap()
    DL = lambda n, s, d=F32: nc.dram_tensor(n, list(s), d, addr_space="Local").ap()
    summ, summ_all = D("i_summ", [64, 520]), DL("i_summ_all", [256, 520])
    h1, qT, sg = D("i_h1", [SEG, 1024]), D("i_qT", [NT, 128, 1024], BF16), D("i_sg", [SEG, 1024], BF16)
    Kl, Vl = D("i_K", [128, 2 * SEG], BF16), D("i_V", [SEG, 256], BF16)
    k_all, v_all = DL("i_k_all", [512, 2 * SEG], BF16), DL("i_v_all", [4 * SEG, 256], BF16)
    zscr = D("zscr", [8, 128, 1024])

    with contextlib.ExitStack() as es:
        C = _mk(nc, es, dc)
        S = C.S
        io = {"x": C.din("x", [SEG, 1024]), "x_halo": C.din("x_halo", [128, 1024]),
              "gain0_rep": C.din("gain0_rep", [128, 1024]), "gain1_rep": C.din("gain1_rep", [128, 1024]),
              "w_in_even": C.din("w_in_even", [1024, 2832]), "w_out_even": C.din("w_out_even", [1024, 1024]),
              "w_in_odd": C.din("w_in_odd", [1024, 2560]), "w_out_odd": C.din("w_out_odd", [1024, 1024]),
              "gla_w_up": C.din("gla_w_up", [16, 256]), "gla_b_up": C.din("gla_b_up", [1, 256]),
              "gla_gain_rep": C.din("gla_gain_rep", [128, 512]), "sinks_rep": C.din("sinks_rep", [128, 8]),
              "rel_bias": C.din("rel_bias", [32, 8]), "rb31_rep": C.din("rb31_rep", [128, 8]),
              "fgain_rep": C.din("fgain_rep", [128, 1024]), "t_flags": C.din("t_flags", [128, 16]),
              "t_valid": C.din("t_valid", [128, 512]), "t_addm": C.din("t_addm", [128, 512]), "t_prev": C.din("t_prev", [128, 512]),
              "zscr": zscr,
              "o_summ": summ, "summ_all": summ_all.rearrange("(s p) c -> s p c", s=4),
              "o_h1": h1, "o_qT": qT, "o_sg": sg, "o_K": Kl.rearrange("p (k t) -> p k t", k=2), "o_V": Vl,
              "q_t": qT, "sg": sg, "h1": h1,
              "k_all": k_all.rearrange("(s p) (k t) -> s p k t", s=4, k=2), "v_all": v_all.rearrange("(s t) d -> s t d", s=4),
              "k_loc": Kl.rearrange("p (k t) -> p k t", k=2), "v_loc": Vl,
              "out": nc.dram_tensor("out", [SEG, 1024], F32, kind="ExternalOutput").ap()}

        def gather(pairs):
            for a, b in pairs:
                tok = S.cc(lambda e, a=a, b=b: e.collective_compute("AllGather", ALU.bypass, replica_groups=RG, ins=[a], outs=[b]))
                S.res["cc1"] = [tok, []]

        C.pfx = "T_"
        T = {"EB": C.sb("EBall", [128, 4, 8, 128], BF16), "Dc": C.sb("Dcorr", [128, 8, 128], BF16), "ecrep": C.sb("ecrep", [128, 8])}
        C.pfx = "W_"
        with C.scope():
            W = _load_weights(C, io)
            C.pfx = "A_"
            with C.scope():
                phase_a(C, io, W, T)
            gather([(summ, summ_all)])
            C.pfx = "B_"
            with C.scope():
                phase_b(C, io, W, T)
        def gather2():
            for key, a, b in (("ccK", Kl, k_all), ("ccV", Vl, v_all)):
                tok = S.cc(lambda e, a=a, b=b: e.collective_compute("AllGather", ALU.bypass, replica_groups=RG, ins=[a], outs=[b]))
                S.res[key] = [tok, []]

        C.pfx = "C_"
        with C.scope():
            phase_c(C, io, T, gather2)
        S.emit()
    return nc


_PROGS = {}


def _prog(name, fn):
    if name not in _PROGS:
        _PROGS[name] = fn()
    return _PROGS[name]


def _rep(a, n=128):
    a = np.asarray(a, np.float32)
    return np.ascontiguousarray(np.broadcast_to(a.reshape(1, -1), (n, a.size)))


def kernel(x, norm_gain, final_gain, rel_bias, w_in_even, gla_w_up, gla_b_up, gla_norm_gain,
           swa_sinks, w_out_even, w_in_odd, w_out_odd):
    f = lambda a: np.ascontiguousarray(np.asarray(a, dtype=np.float32))
    x = f(x)
    cores = list(range(NCORES))
    cn = _consts()
    rb = f(rel_bias)
    shared = dict(cn, gain0_rep=_rep(f(norm_gain)[0]), gain1_rep=_rep(f(norm_gain)[1]), w_in_even=f(w_in_even)[0], w_out_even=f(w_out_even)[0], w_in_odd=f(w_in_odd)[0],
                  w_out_odd=f(w_out_odd)[0], gla_w_up=f(gla_w_up)[0], gla_b_up=f(gla_b_up)[0].reshape(1, 256),
                  gla_gain_rep=_rep(np.tile(f(gla_norm_gain)[0], 4)), sinks_rep=_rep(f(swa_sinks)[0]), rel_bias=rb,
                  rb31_rep=_rep(rb[31]), fgain_rep=_rep(f(final_gain)))
    in_maps = []
    for c in cores:
        b, s = c // 4, c % 4
        halo = x[b, s * SEG - 128:s * SEG] if s > 0 else np.zeros((128, 1024), np.float32)
        in_maps.append(dict(shared, x=np.ascontiguousarray(x[b, s * SEG:(s + 1) * SEG]), x_halo=np.ascontiguousarray(halo),
                            **_core_tables(s)))
    res = run_bass_kernel_spmd(_prog("fused", build_fused), in_maps, core_ids=cores).results
    out = np.empty((2, 4 * SEG, 1024), np.float32)
    for c in cores:
        out[c // 4, (c % 4) * SEG:(c % 4 + 1) * SEG] = np.asarray(res[c]["out"])
    return out
```

```python
import contextlib
import math
import numpy as np
import ml_dtypes
import concourse.bass as bass
import concourse.mybir as mybir
from concourse.bass_utils import run_bass_kernel_spmd

F32 = mybir.dt.float32
BF16 = mybir.dt.bfloat16
AF = mybir.ActivationFunctionType
ALU = mybir.AluOpType
AX = mybir.AxisListType

NCORES = 8
SEG = 2048
NT = SEG // 128
EPS = 1e-6
CH = 30000
NDMA = 40
NDMA_SW = 12


class Sched:
    CE = ("pe", "act", "dve", "pool")

    def __init__(self, nc):
        self.nc = nc
        self.ops = {e: [] for e in self.CE + ("sp",)}
        self.cnt = {e: 0 for e in self.CE}
        self.res = {}
        self.clk = {}
        self.know = {e: {} for e in self.CE + ("sp",)}
        self.ndma = 0
        self.ndma_q = {}
        self.slots = set()
        self.final = []
        self.dma_toks = []
        self.pfx = ""
        self.excl = set()
        self.bank = {}

    def _need(self, eng, tok, waits):
        if tok is None:
            return
        k = self.know[eng]
        key = (tok[0], tok[1])
        if k.get(key, 0) >= tok[2]:
            return
        waits.append(tok)
        for kk, v in self.clk[tok].items():
            if k.get(kk, 0) < v:
                k[kk] = v

    def _dep(self, eng, tok, waits):
        if tok is None:
            return
        if tok[0] == "c" and tok[1] == "pe" and eng == "pe":
            return
        self._need(eng, tok, waits)

    def op(self, eng, fn, r=(), w=(), dma=False, final=False):
        waits = []
        for key in list(r) + list(w):
            if key in self.excl:
                for f, t in self.bank.setdefault(key, {}).items():
                    if f != eng:
                        self._need(eng, t, waits)
        for key in r:
            ent = self.res.get(key)
            if ent is not None:
                self._dep(eng, ent[0], waits)
        for key in w:
            ent = self.res.get(key)
            if ent is not None:
                self._dep(eng, ent[0], waits)
                for t in ent[1]:
                    self._dep(eng, t, waits)
        if dma:
            nslot, base = (NDMA, 0) if eng == "sp" else (NDMA_SW, 100)
            n = self.ndma_q.get(eng, 0)
            self.ndma_q[eng] = n + 1
            slot = base + n % nslot
            val = 16 * (n // nslot + 1)
            if val > 16:
                self._need(eng, ("d", slot, val - 16), waits)
            self.ndma += 1
            self.slots.add(slot)
            tok = ("d", slot, val)
            self.dma_toks.append(tok)
        else:
            self.cnt[eng] += 1
            tok = ("c", eng, self.cnt[eng])
        clk = dict(self.know[eng])
        clk[(tok[0], tok[1])] = tok[2]
        self.clk[tok] = clk
        self.ops[eng].append((waits, fn, tok))
        for key in r:
            ent = self.res.setdefault(key, [None, []])
            ent[1].append(tok)
        for key in w:
            self.res[key] = [tok, []]
        for key in list(r) + list(w):
            if key in self.excl:
                self.bank[key][eng] = tok
        if final:
            self.final.append(tok)
        return tok

    def cc(self, fn):
        self.cnt["pool"] += 1
        tok = ("c", "pool", self.cnt["pool"])
        clk = dict(self.know["pool"])
        clk[("c", "pool")] = tok[2]
        self.clk[tok] = clk
        self.ops["pool"].append(([], fn, tok))
        w = []
        self._need("pool", tok, w)
        self.ops["pool"].append((w, None, None))
        return tok

    def barrier(self):
        toks = [("c", e, self.cnt[e]) for e in self.CE if self.cnt[e] > 0]
        last = {}
        for t in self.dma_toks:
            last[t[1]] = t
        toks += list(last.values())
        for eng in self.CE + ("sp",):
            waits = []
            for t in toks:
                self._need(eng, t, waits)
            if waits:
                self.ops[eng].append((waits, None, None))
        self.dma_toks = []

    def emit(self):
        nc = self.nc
        fw = []
        for t in self.final:
            self._need("sp", t, fw)
        with contextlib.ExitStack() as es:
            sems = {}
            for e in self.CE:
                n = (self.cnt[e] + CH - 1) // CH
                for i in range(max(n, 1)):
                    sems[(e, i)] = es.enter_context(nc.semaphore(f"{self.pfx}s_{e}{i}"))
            for i in sorted(self.slots):
                sems[("d", i)] = es.enter_context(nc.semaphore(f"{self.pfx}s_d{i}"))
            block = es.enter_context(nc.Block())

            def semval(tok):
                if tok[0] == "c":
                    n = tok[2] - 1
                    return sems[(tok[1], n // CH)], n % CH + 1
                return sems[("d", tok[1])], tok[2]

            def run(ename, eng):
                for waits, fn, tok in self.ops[ename]:
                    for t in waits:
                        s, v = semval(t)
                        eng.wait_ge(s, v)
                    if fn is None:
                        continue
                    ins = fn(eng)
                    s, v = semval(tok)
                    ins.then_inc(s, 1 if tok[0] == "c" else 16)
                if ename == "sp":
                    for t in fw:
                        s, v = semval(t)
                        eng.wait_ge(s, v)

            @block.tensor
            def _(e):
                run("pe", e)

            @block.scalar
            def _(e):
                run("act", e)

            @block.vector
            def _(e):
                run("dve", e)

            @block.gpsimd
            def _(e):
                run("pool", e)

            @block.sync
            def _(e):
                run("sp", e)


def _bucket(d):
    n = np.maximum(d, 0)
    nf = np.maximum(n, 1).astype(np.float32)
    large = 16 + (np.log(nf / np.float32(16)) / np.float32(math.log(8.0)) * np.float32(16)).astype(np.int32)
    large = np.minimum(large, 31)
    return np.where(n < 16, n, large)


def _consts():
    s = np.arange(128)
    same = (s[:, None] // 64) == (s[None, :] // 64)
    tri = (same & (s[:, None] <= s[None, :])).astype(np.float32)
    blk = same.astype(np.float32)
    cind = ((s[:, None] // 64) == np.arange(2)[None, :]).astype(np.float32)
    maskbd = np.tile(tri, (1, 4)).astype(ml_dtypes.bfloat16)
    onehot = np.zeros((32, 1024), np.float32)
    m = np.arange(255)
    for typ in range(4):
        d = m - 127 if typ in (0, 2) else m + 1
        if typ in (0, 1):
            valid = (d >= 0) & (d < 128)
        else:
            valid = d >= 0
        b = _bucket(d)
        for mm in range(255):
            if valid[mm]:
                onehot[b[mm], typ * 256 + mm] = 1.0
    return {
        "c_ident": np.eye(128).astype(ml_dtypes.bfloat16),
        "c_tri": tri, "c_blk": blk, "c_cind": cind, "c_maskbd": maskbd,
        "c_onehot": onehot,
        "c_ones1": np.ones((1, 128), np.float32),
    }


def _core_tables(seg):
    own = 8 * seg + np.arange(NT) // 2
    n = np.arange(32)
    valid = (n[None, :] < own[:, None]).astype(np.float32)
    addm = np.where(valid > 0, 0.0, -1e30).astype(np.float32)
    prev = (n[None, :] == (own[:, None] - 1)).astype(np.float32)
    rep = lambda a: np.ascontiguousarray(np.broadcast_to(a.reshape(1, -1), (128, a.size))).astype(np.float32)
    flags = np.zeros(16, np.float32)
    flags[0] = 1.0 if seg > 0 else 0.0
    for sp in range(4):
        flags[4 + sp] = 1.0 if sp < seg else 0.0
        flags[8 + sp] = 1.0 if sp == seg - 1 else 0.0
    return {"t_valid": rep(valid), "t_addm": rep(addm), "t_prev": rep(prev), "t_flags": rep(flags)}


class Ctx:
    pass


_UID = [0]


def _mk(nc, es, dcache=None):
    C = Ctx()
    C.nc = nc
    C.S = Sched(nc)
    _UID[0] += 1
    C.pfx = f"u{_UID[0]}_"
    C.S.pfx = C.pfx
    dcache = {} if dcache is None else dcache
    C.stack = [es]
    C.sb = lambda n, s, d=F32: C.stack[-1].enter_context(nc.sbuf_tensor(C.pfx + "S_" + n, list(s), d))

    @contextlib.contextmanager
    def scope():
        with contextlib.ExitStack() as es2:
            C.stack.append(es2)
            try:
                yield
            finally:
                C.S.barrier()
                C.stack.pop()
    C.scope = scope
    C.ps = lambda n, s, d=F32: C.stack[-1].enter_context(nc.psum_tensor(C.pfx + "P_" + n, list(s), d))
    def din(n, s, d=F32):
        if n not in dcache:
            dcache[n] = nc.dram_tensor(n, list(s), d, kind="ExternalInput").ap()
        return dcache[n]
    C.din = din
    C.dout = lambda n, s, d=F32: nc.dram_tensor(n, list(s), d, kind="ExternalOutput").ap()
    return C


def _load(C, dst, src, key, eng="sp"):
    return C.S.op(eng, lambda e: e.dma_start(out=dst, in_=src), w=[key], dma=True)


def _load_consts(C, names):
    shapes = {"c_ident": ([128, 128], BF16), "c_tri": ([128, 128], F32), "c_blk": ([128, 128], F32),
              "c_cind": ([128, 2], F32), "c_maskbd": ([128, 512], BF16), "c_onehot": ([32, 1024], F32),
              "c_ones1": ([1, 128], F32)}
    out = {}
    for n in names:
        shp, dt = shapes[n]
        d = C.din(n, shp, dt)
        t = C.sb("sb_" + n, shp, dt)
        _load(C, t[:], d, n)
        out[n] = t
    return out


def _load_weights(C, io):
    S = C.S
    W = {"wb": C.sb("wE", [128, 8, 2832], BF16), "wq": C.sb("wq", [128, 8, 512], BF16),
         "wo": C.sb("wO", [128, 8, 1024], BF16), "w1": C.sb("w1", [128, 8, 2560], BF16),
         "g0": C.sb("g0rep", [128, 1024]), "g1": C.sb("g1rep", [128, 1024])}
    _load(C, W["g0"][:], io["gain0_rep"], "grep")
    _load(C, W["g1"][:], io["gain1_rep"], "grep")
    first = [("wE", k) for k in range(8)]

    def ld(dst, srcap, key):
        S.op("pool", lambda e: e.dma_start(out=dst, in_=srcap), r=([] if key in first else first), w=[key], dma=True)
    we = io["w_in_even"]
    for k in range(8):
        ld(W["wb"][:, k, 256:1040], we[k * 128:(k + 1) * 128, 256:1040], ("wE", k))
    for k in range(8):
        ld(W["wb"][:, k, 0:256], we[k * 128:(k + 1) * 128, 0:256], ("wE0", k))
        ld(W["wb"][:, k, 1040:1552], we[k * 128:(k + 1) * 128, 1040:1552], ("wE1", k))
    for k in range(8):
        ld(W["wb"][:, k, 1552:2832], we[k * 128:(k + 1) * 128, 1552:2832], ("wE2", k))
        for a in range(4):
            ld(W["wq"][:, k, a * 128:(a + 1) * 128].rearrange("p (g d) -> p g d", g=2),
               we[k * 128:(k + 1) * 128, 1552:2064].rearrange("p (g a d) -> p g a d", g=2, a=4)[:, :, a, :], ("wq", k, a))
    for k in range(8):
        ld(W["wo"][:, k, :], io["w_out_even"][k * 128:(k + 1) * 128, :], ("wO", k))
    for k in range(8):
        ld(W["w1"][:, k, :], io["w_in_odd"][k * 128:(k + 1) * 128, :], ("w1", k))
    return W


def _norm_transpose(C, src, srckey, bufs, ident, tag, grep=None):
    S = C.S
    junk, ss, rstd, xn, pT, xnT = (bufs[k] for k in ("junk", "ss", "rstd", "xn", "pT", "xnT"))
    S.op("act", lambda e: e.activation(out=junk[:], in_=src, func=AF.Square, accum_out=ss[:]),
         r=[srckey], w=[tag + "junk", tag + "ss"])
    S.op("act", lambda e: e.activation(out=ss[:], in_=ss[:], func=AF.Ln, scale=1.0 / 1024, bias=EPS),
         r=[tag + "ss"], w=[tag + "ss"])
    S.op("act", lambda e: e.activation(out=rstd[:], in_=ss[:], func=AF.Exp, scale=-0.5), r=[tag + "ss"], w=[tag + "rstd"])
    if grep is None:
        S.op("dve", lambda e: e.tensor_scalar(out=xn[:], in0=src, scalar1=rstd[:], scalar2=None, op0=ALU.mult),
             r=[srckey, tag + "rstd"], w=[tag + "xn"])
    else:
        S.op("dve", lambda e: e.scalar_tensor_tensor(out=xn[:], in0=src, scalar=rstd[:], in1=grep[:], op0=ALU.mult, op1=ALU.mult),
             r=[srckey, tag + "rstd", "grep"], w=[tag + "xn"])
    for k in range(8):
        S.op("pe", lambda e, k=k: e.transpose(out=pT[:, k, :], in_=xn[:, k * 128:(k + 1) * 128], identity=ident[:]),
             r=[tag + "xn", "c_ident"], w=["pTbank"])
    S.op("act", lambda e: e.copy(out=xnT[:], in_=pT[:]), r=["pTbank"], w=[tag + "xnT"])


def _silu(C, out, pin, pkey, tmp, tkey, okey):
    S = C.S
    S.op("act", lambda e: e.activation(out=tmp, in_=pin, func=AF.Exp, scale=-1.0), r=[pkey], w=[tkey])
    S.op("act", lambda e: e.activation(out=tmp, in_=tmp, func=AF.Ln, bias=1.0), r=[tkey], w=[tkey])
    S.op("act", lambda e: e.activation(out=tmp, in_=tmp, func=AF.Exp, scale=-1.0), r=[tkey], w=[tkey])
    S.op("dve", lambda e: e.tensor_tensor(out=out, in0=pin, in1=tmp, op=ALU.mult), r=[pkey, tkey], w=[okey])


def _mm8(C, out, lhs_fn, rhs_fn, rkeys, wkey):
    for k in range(8):
        C.S.op("pe", lambda e, k=k: e.matmul(out, lhsT=lhs_fn(k), rhs=rhs_fn(k), start=(k == 0), stop=(k == 7)),
               r=rkeys(k), w=[wkey])


class PsPool:
    def __init__(self, C, n, name):
        self.t = [C.ps(f"{name}{i}", [128, 512], F32) for i in range(n)]
        self.free_list = list(range(n))
        self.name = name
        for i in range(n):
            C.S.excl.add((name, i))
        C.S.excl.add("pTbank")
        C.S.excl.add("pT2")

    def alloc(self):
        assert self.free_list, "PSUM pool exhausted (program-order liveness bug)"
        j = self.free_list.pop(0)
        return self.t[j], (self.name, j)

    def free(self, key):
        assert key[1] not in self.free_list
        self.free_list.append(key[1])


def _gla_tile(C, G, ti, pA, pAkey, vsrc, vkey, adT_key, full):
    S = C.S
    K = G["K"]
    pp = G["pp"]
    sfx = G.get("sfx", "")
    kx = lambda n: n + sfx
    pz, pzk = pp.alloc()
    S.op("pe", lambda e: e.matmul(pz[:, 0:256], lhsT=G["adT"][:], rhs=G["wup"][:], start=True, stop=False),
         r=[adT_key, "wup"], w=[pzk])
    S.op("pe", lambda e: e.matmul(pz[:, 0:256], lhsT=K["c_ones1"][:], rhs=G["bup"][:], start=False, stop=True),
         r=["c_ones1", "bup"], w=[pzk])
    el = G["el"]
    S.op("act", lambda e: e.activation(out=el[:], in_=pz[:, 0:256], func=AF.Exp, scale=-1.0), r=[pzk], w=[kx("el")])
    pp.free(pzk)
    S.op("act", lambda e: e.activation(out=el[:], in_=el[:], func=AF.Ln, bias=1.0), r=[kx("el")], w=[kx("el")])
    yield
    pc, pck = pp.alloc()
    S.op("pe", lambda e: e.matmul(pc[:, 0:256], lhsT=K["c_tri"][:], rhs=el[:], start=True, stop=True),
         r=["c_tri", kx("el")], w=[pck])
    S.op("pe", lambda e: e.matmul(pc[:, 256:512], lhsT=K["c_blk"][:], rhs=el[:], start=True, stop=True),
         r=["c_blk", kx("el")], w=[pck])
    pd, pdk = pp.alloc()
    for h in range(4):
        S.op("pe", lambda e, h=h: e.matmul(pd[0:64, 2 * h:2 * h + 2], lhsT=el[:, h * 64:(h + 1) * 64], rhs=K["c_cind"][:],
                                           start=True, stop=True), r=[kx("el"), "c_cind"], w=[pdk])
    dec = G["dec"]
    S.op("act", lambda e: e.activation(out=dec[:], in_=pd[0:64, 0:8], func=AF.Exp, scale=-1.0 / 16), r=[pdk], w=[kx("dec")])
    if not full:
        S.op("dve", lambda e: e.tensor_tensor(out=G["lacc"][:], in0=G["lacc"][:], in1=pd[0:64, 0:8], op=ALU.add),
             r=[pdk, "lacc"], w=["lacc"])
    pp.free(pdk)
    yield
    enb, ebl, edec = G["enb"], G["ebl"], G["edec"]
    S.op("act", lambda e: e.activation(out=enb[:], in_=pc[:, 0:256], func=AF.Exp, scale=1.0 / 16), r=[pck], w=[kx("enb")])
    S.op("act", lambda e: e.activation(out=ebl[:], in_=pc[:, 256:512], func=AF.Exp, scale=-1.0 / 16), r=[pck], w=[kx("ebl")])
    if full:
        eb = G["eb"]
        S.op("act", lambda e: e.activation(out=eb[:], in_=pc[:, 0:256], func=AF.Exp, scale=-1.0 / 16), r=[pck], w=[kx("eb")])
    pp.free(pck)
    yield
    S.op("dve", lambda e: e.tensor_tensor(out=edec[:], in0=ebl[:], in1=enb[:], op=ALU.mult), r=[kx("enb"), kx("ebl")], w=[kx("edec")])
    kdec = G["kdec"]
    S.op("dve", lambda e: e.tensor_tensor(out=kdec[:], in0=pA[:, 256:512], in1=edec[:], op=ALU.mult),
         r=[pAkey, kx("edec")], w=[kx("kdec")])
    if full:
        qke = G["qke"]
        S.op("dve", lambda e: e.scalar_tensor_tensor(out=qke[:, 0:256], in0=pA[:, 0:256], scalar=0.125, in1=eb[:],
                                                     op0=ALU.mult, op1=ALU.mult), r=[pAkey, kx("eb")], w=[kx("qke")])
        S.op("dve", lambda e: e.tensor_tensor(out=qke[:, 256:512], in0=pA[:, 256:512], in1=enb[:], op=ALU.mult),
             r=[pAkey, kx("enb")], w=[kx("qke")])
        pT2, qkT = G["pT2"], G["qkT"]
        for j in range(8):
            S.op("pe", lambda e, j=j: e.transpose(out=pT2[0:64, j, :], in_=qke[:, j * 64:(j + 1) * 64], identity=K["c_ident"][:]),
                 r=[kx("qke"), "c_ident"], w=["pT2"])
        S.op("act", lambda e: e.copy(out=qkT[:], in_=pT2[0:64, :, :]), r=["pT2"], w=[kx("qkT")])
        yield
        patt, pattk = pp.alloc()
        for h in range(4):
            S.op("pe", lambda e, h=h: e.matmul(patt[:, h * 128:(h + 1) * 128], lhsT=qkT[:, 4 + h, :], rhs=qkT[:, h, :],
                                               start=True, stop=True), r=[kx("qkT")], w=[pattk])
        attm = G["attm"]
        S.op("dve", lambda e: e.tensor_tensor(out=attm[:], in0=patt[:], in1=K["c_maskbd"][:], op=ALU.mult),
             r=[pattk, "c_maskbd"], w=[kx("attm")])
        pp.free(pattk)
        yield
    S32 = G["S32"]

    def kv_mm(c):
        pkv, pkvk = pp.alloc()
        for h in range(4):
            S.op("pe", lambda e, h=h: e.matmul(pkv[0:64, h * 128:(h + 1) * 128],
                                               lhsT=kdec[c * 64:(c + 1) * 64, h * 64:(h + 1) * 64],
                                               rhs=vsrc[c * 64:(c + 1) * 64, h * 128:(h + 1) * 128],
                                               start=True, stop=True), r=[kx("kdec"), vkey], w=[pkvk])
        return pkv, pkvk

    def upd(c, pkv, pkvk):
        g = 2 * ti + c
        for h in range(4):
            S.op("dve", lambda e, h=h: e.scalar_tensor_tensor(
                out=S32[:, h * 128:(h + 1) * 128], in0=S32[:, h * 128:(h + 1) * 128],
                scalar=dec[:, 2 * h + c:2 * h + c + 1], in1=pkv[0:64, h * 128:(h + 1) * 128],
                op0=ALU.mult, op1=ALU.add), r=["S32", kx("dec"), pkvk], w=["S32"])
        pp.free(pkvk)
        if full:
            nb = G["Sbf"][(g + 1) % 2]
            S.op("pool", lambda e: e.tensor_copy(out=nb[:], in_=S32[:]), r=["S32"], w=[("Sbf", (g + 1) % 2)])

    k0 = kv_mm(0)
    upd(0, *k0)
    yield
    k1 = kv_mm(1)
    if full:
        yield
    po = pok = None
    if full:
        po, pok = pp.alloc()
        for h in range(4):
            S.op("pe", lambda e, h=h: e.matmul(po[:, h * 128:(h + 1) * 128], lhsT=attm[:, h * 128:(h + 1) * 128],
                                               rhs=vsrc[:, h * 128:(h + 1) * 128], start=True, stop=False),
                 r=[kx("attm"), vkey], w=[pok])
            for c in range(2):
                g = 2 * ti + c
                sb_ = G["Sbf"][g % 2]
                S.op("pe", lambda e, h=h, c=c, sb_=sb_: e.matmul(po[c * 64:(c + 1) * 64, h * 128:(h + 1) * 128],
                                                                 lhsT=qkT[:, h, c * 64:(c + 1) * 64],
                                                                 rhs=sb_[:, h * 128:(h + 1) * 128],
                                                                 start=False, stop=True),
                     r=[kx("qkT"), ("Sbf", g % 2)], w=[pok])
    if full:
        yield
    upd(1, *k1)
    return po, pok


def _gla_bufs(C, K, pp, full):
    G = {"K": K, "pp": pp}
    G["adT"] = C.sb("g_adT", [16, 128])
    G["wup"] = C.sb("g_wup", [16, 256])
    G["bup"] = C.sb("g_bup", [1, 256])
    G["el"] = C.sb("g_el", [128, 256])
    G["dec"] = C.sb("g_dec", [64, 8])
    G["enb"] = C.sb("g_enb", [128, 256])
    G["ebl"] = C.sb("g_ebl", [128, 256])
    G["edec"] = C.sb("g_edec", [128, 256])
    G["kdec"] = C.sb("g_kdec", [128, 256], BF16)
    G["S32"] = C.sb("g_S32", [64, 512])
    if full:
        G["eb"] = C.sb("g_eb", [128, 256])
        G["qke"] = C.sb("g_qke", [128, 512], BF16)
        G["qkT"] = C.sb("g_qkT", [64, 8, 128], BF16)
        G["attm"] = C.sb("g_attm", [128, 512], BF16)
        G["Sbf"] = [C.sb("g_Sbf0", [64, 512], BF16), C.sb("g_Sbf1", [64, 512], BF16)]
    else:
        G["lacc"] = C.sb("g_lacc", [64, 8])
    return G


def phase_a(C, io, W, T):
    S = C.S
    K = _load_consts(C, ["c_ident", "c_tri", "c_blk", "c_cind", "c_ones1", "c_onehot"])
    wb = W["wb"]
    pp = PsPool(C, 6, "pp")
    xt = [C.sb(f"xt{i}", [128, 1024]) for i in range(3)]
    _load(C, xt[0][:], io["x"][0:128, :], ("xt", 0))
    ecrep, Dc = T["ecrep"], T["Dc"]
    _load(C, ecrep[:], io["rb31_rep"], "ecrep")
    S.op("act", lambda e: e.activation(out=ecrep[:], in_=ecrep[:], func=AF.Exp), r=["ecrep"], w=["ecrep"])

    def per_head(h, t32, tk):
        S.op("dve", lambda e: e.tensor_scalar(out=Dc[:, h, :], in0=t32[:, 3, :], scalar1=ecrep[:, h:h + 1], scalar2=None,
                                              op0=ALU.subtract), r=[tk, "ecrep"], w=["Dc"])
    tables = _bias_tables(C, K, io, pp, T["EB"], per_head)
    G0 = _gla_bufs(C, K, pp, False)
    _load(C, G0["wup"][:], io["gla_w_up"], "wup")
    _load(C, G0["bup"][:], io["gla_b_up"], "bup")
    S.op("dve", lambda e: e.memset(G0["S32"][:], 0.0), w=["S32"])
    S.op("dve", lambda e: e.memset(G0["lacc"][:], 0.0), w=["lacc"])
    G1 = dict(G0)
    for n, shp, dt in [("adT", [16, 128], F32), ("el", [128, 256], F32), ("dec", [64, 8], F32), ("enb", [128, 256], F32),
                       ("ebl", [128, 256], F32), ("edec", [128, 256], F32), ("kdec", [128, 256], BF16)]:
        G1[n] = C.sb("g1_" + n, shp, dt)
    G0["sfx"], G1["sfx"] = "_0", "_1"
    G = G0
    Gs = [G0, G1]
    pT = C.ps("pT", [128, 8, 128], BF16)
    nbs = [{"junk": C.sb(f"junk{p}", [128, 1024], BF16), "ss": C.sb(f"ss{p}", [128, 1]), "rstd": C.sb(f"rstd{p}", [128, 1]),
            "xn": C.sb(f"xn{p}", [128, 1024], BF16), "pT": pT, "xnT": C.sb(f"xnT{p}", [128, 8, 128], BF16)} for p in range(2)]
    vbfs = [C.sb(f"vbf{p}", [128, 512], BF16) for p in range(2)]

    def A(ti):
        par = ti % 2
        nb, vbf, Gp = nbs[par], vbfs[par], Gs[par]
        tag = f"a{par}"
        if ti + 1 < NT:
            _load(C, xt[(ti + 1) % 3][:], io["x"][(ti + 1) * 128:(ti + 2) * 128, :], ("xt", (ti + 1) % 3))
        _norm_transpose(C, xt[ti % 3][:], ("xt", ti % 3), nb, K["c_ident"], tag, grep=W["g0"])
        yield
        xnT = nb["xnT"]
        rk = lambda k: [tag + "xnT", ("wE", k)]
        pA, pAk = pp.alloc()
        _mm8(C, pA[:, 256:512], lambda k: xnT[:, k, :], lambda k: wb[:, k, 256:512], rk, pAk)
        yield
        pV, pVk = pp.alloc()
        _mm8(C, pV[:, 0:512], lambda k: xnT[:, k, :], lambda k: wb[:, k, 512:1024], rk, pVk)
        S.op("act", lambda e: e.copy(out=vbf[:], in_=pV[:, 0:512]), r=[pVk], w=["vbf" + tag])
        pp.free(pVk)
        yield
        pM, pMk = pp.alloc()
        _mm8(C, pM[0:16, 0:128], lambda k: wb[:, k, 1024:1040], lambda k: xnT[:, k, :], rk, pMk)
        S.op("dve", lambda e: e.tensor_copy(out=Gp["adT"][:], in_=pM[0:16, 0:128]), r=[pMk], w=["adT" + tag])
        pp.free(pMk)
        yield
        yield from _gla_tile(C, Gp, ti, pA, pAk, vbf, "vbf" + tag, "adT" + tag, False)
        pp.free(pAk)

    active, nxt = [A(0), tables], 1
    while active:
        for g in list(active):
            try:
                next(g)
            except StopIteration:
                active.remove(g)
        if sum(1 for g in active if g is not tables) < 2 and nxt < NT:
            active.append(A(nxt))
            nxt += 1
    S.op("sp", lambda e: e.dma_start(out=io["o_summ"][:, 0:512], in_=G["S32"][:]), r=["S32"], dma=True, final=True)
    S.op("sp", lambda e: e.dma_start(out=io["o_summ"][:, 512:520], in_=G["lacc"][:]), r=["lacc"], dma=True, final=True)


def _bcast_mid(ap, n):
    pat = [list(p) for p in ap.ap]
    return bass.AP(ap.tensor, ap.offset, [pat[0], [0, n]] + pat[1:])


def _bias_tables(C, K, io, pp, EB, per_head=None):
    S = C.S
    rb = C.sb("rb", [32, 8])
    _load(C, rb[:], io["rel_bias"], "rb")
    S.op("act", lambda e: e.activation(out=rb[:], in_=rb[:], func=AF.Exp), r=["rb"], w=["rb"])
    lh = [C.sb(f"lh{i}", [32, 128]) for i in range(2)]
    frep = [C.sb(f"frep{i}", [128, 1024]) for i in range(2)]
    T32 = [C.sb(f"T32_{i}", [128, 4, 128]) for i in range(2)]
    zt = io["zscr"]

    def produce(h):
        l = lh[h % 2]
        lk = ("lh", h % 2)
        S.op("dve", lambda e: e.tensor_copy(out=l[:], in_=rb[:, h:h + 1].to_broadcast([32, 128])), r=["rb"], w=[lk])
        fr = frep[h % 2]
        fk = ("frep", h % 2)
        for half in range(2):
            pf, pfk = pp.alloc()
            S.op("pe", lambda e, half=half, pf=pf: e.matmul(pf[:, 0:512], lhsT=l[:], rhs=K["c_onehot"][:, half * 512:(half + 1) * 512],
                                                            start=True, stop=True), r=[lk, "c_onehot"], w=[pfk])
            S.op("act", lambda e, half=half, pf=pf: e.copy(out=fr[:, half * 512:(half + 1) * 512], in_=pf[:, 0:512]), r=[pfk], w=[fk])
            pp.free(pfk)
        S.op("sp", lambda e: e.dma_start(out=zt[h], in_=fr[:]), r=[fk], w=[("z", h)], dma=True)

    def consume(h):
        src_ap = bass.AP(zt.tensor, zt.offset + h * 128 * 1024 + 127, [[1023, 128], [256, 4], [1, 128]])
        t32 = T32[h % 2]
        tk = ("T32", h % 2)
        S.op("sp", lambda e: e.dma_start(out=t32[:], in_=src_ap), r=[("z", h)], w=[tk], dma=True)
        S.op("dve", lambda e: e.tensor_copy(out=EB[:, :, h, :], in_=t32[:]), r=[tk], w=["EB"])
        if per_head is not None:
            per_head(h, t32, tk)

    for h in range(8):
        produce(h)
        yield
        if h >= 1:
            consume(h - 1)
            yield
    consume(7)
    yield


def phase_b(C, io, W, T):
    S = C.S
    K = _load_consts(C, ["c_ident", "c_tri", "c_blk", "c_cind", "c_ones1", "c_maskbd"])
    ident = K["c_ident"]
    pp = PsPool(C, 6, "pp")
    flags = C.sb("flags", [128, 16])
    _load(C, flags[:], io["t_flags"], "flags")
    wb, wo, w1, wq = W["wb"], W["wo"], W["w1"], W["wq"]
    EB = T["EB"]
    EBfirst = C.sb("EBfirst", [128, 8, 128], BF16)
    sinkexp = C.sb("sinkexp", [128, 8])
    ggain = C.sb("ggain", [128, 512])
    G = _gla_bufs(C, K, pp, True)
    G["pT2"] = C.ps("pT2", [128, 8, 128], BF16)
    S32 = G["S32"]
    if True:
        S.op("dve", lambda e: e.tensor_scalar(out=EBfirst[:], in0=EB[:, 1, :, :], scalar1=flags[:, 0:1], scalar2=None, op0=ALU.mult),
             r=["EB", "flags"], w=["EBfirst"])
        _load(C, sinkexp[:], io["sinks_rep"], "sinkexp")
        S.op("act", lambda e: e.activation(out=sinkexp[:], in_=sinkexp[:], func=AF.Exp), r=["sinkexp"], w=["sinkexp"])
        _load(C, ggain[:], io["gla_gain_rep"], "ggain")
        _load(C, G["wup"][:], io["gla_w_up"], "wup")
        _load(C, G["bup"][:], io["gla_b_up"], "bup")
        S.op("dve", lambda e: e.memset(S32[:], 0.0), w=["S32"])
        lall = C.sb("lall", [64, 4, 8])
        h1t = C.sb("h1t", [128, 1024])
        s1 = h1t[0:64, 0:512]
        S.op("sp", lambda e: e.dma_start(out=lall[:], in_=io["summ_all"][:, :, 512:520].rearrange("s p c -> p s c")),
             r=["cc1"], w=["lall"], dma=True)
        S.op("act", lambda e: e.activation(out=lall[:], in_=lall[:], func=AF.Exp, scale=-1.0 / 16), r=["lall"], w=["lall"])
        aseg = C.sb("aseg", [64, 4, 4])
        la4 = lall[:].rearrange("p s (h c) -> p s h c", c=2)
        S.op("dve", lambda e: e.tensor_tensor(out=aseg[:], in0=la4[:, :, :, 0], in1=la4[:, :, :, 1], op=ALU.mult), r=["lall"], w=["aseg"])
        for sp in range(4):
            m = flags[0:64, 4 + sp:5 + sp]
            S.op("sp", lambda e, sp=sp: e.dma_start(out=s1[:], in_=io["summ_all"][sp][:, 0:512]), r=["cc1"], w=["s1"], dma=True)
            S.op("dve", lambda e, sp=sp, m=m: e.tensor_scalar(out=aseg[:, sp, :], in0=aseg[:, sp, :], scalar1=-1.0, scalar2=m,
                                                              op0=ALU.add, op1=ALU.mult), r=["aseg", "flags"], w=["aseg"])
            S.op("dve", lambda e, sp=sp: e.tensor_scalar(out=aseg[:, sp, :], in0=aseg[:, sp, :], scalar1=1.0, scalar2=None, op0=ALU.add),
                 r=["aseg"], w=["aseg"])
            S.op("dve", lambda e, m=m: e.tensor_scalar(out=s1[:], in0=s1[:], scalar1=m, scalar2=None, op0=ALU.mult),
                 r=["s1", "flags"], w=["s1"])
            for h in range(4):
                S.op("dve", lambda e, sp=sp, h=h: e.scalar_tensor_tensor(
                    out=S32[:, h * 128:(h + 1) * 128], in0=S32[:, h * 128:(h + 1) * 128], scalar=aseg[:, sp, h:h + 1],
                    in1=s1[:, h * 128:(h + 1) * 128], op0=ALU.mult, op1=ALU.add), r=["S32", "aseg", "s1"], w=["S32"])
        S.op("pool", lambda e: e.tensor_copy(out=G["Sbf"][0][:], in_=S32[:]), r=["S32"], w=[("Sbf", 0)])

    xt = [C.sb(f"xt{i}", [128, 1024]) for i in range(3)]
    nb = {"junk": C.sb("junk", [128, 1024], BF16), "ss": C.sb("ss", [128, 1]), "rstd": C.sb("rstd", [128, 1]),
          "xn": C.sb("xn", [128, 1024], BF16), "pT": C.ps("pT", [128, 8, 128], BF16), "xnT": C.sb("xnT", [128, 8, 128], BF16)}
    nb1 = dict(nb)
    nb1["junk"] = C.sb("junk1", [128, 1024], BF16)
    stmp1 = C.sb("stmp1", [128, 512])
    stmp2 = C.sb("stmp2", [128, 512])
    nb1["ss"] = C.sb("ss1", [128, 1]); nb1["rstd"] = C.sb("rstd1", [128, 1])
    nb1["xn"] = C.sb("xn1", [128, 1024], BF16); nb1["xnT"] = C.sb("xn1T", [128, 8, 128], BF16)
    vbf = C.sb("vbf", [128, 512], BF16)
    KT = [C.sb(f"KT{i}", [128, 128], BF16) for i in range(2)]
    VA = [C.sb(f"VA{i}", [128, 2, 66], BF16) for i in range(2)]
    for i in range(2):
        S.op("pool", lambda e, i=i: e.memset(VA[i][:], 1.0), w=[("VA", i)])
    QT = C.sb("QT", [128, 4, 128], BF16)
    sgB = C.sb("sgB", [128, 512])
    stmp = C.sb("stmp", [128, 512])
    sgA = C.sb("sgA", [128, 512])
    gsg = C.sb("gsg", [128, 512])
    Ee = C.sb("Ee", [128, 512], BF16)
    Pt = [C.sb(f"Pt{i}", [128, 512], BF16) for i in range(2)]
    den = C.sb("den", [128, 4]); rden = C.sb("rden", [128, 4])
    og = [C.sb(f"og{i}", [128, 1024], BF16) for i in range(2)]
    ogT = C.sb("ogT", [128, 8, 128], BF16)
    ssA = C.sb("ssA", [128, 4]); rsA = C.sb("rsA", [128, 4])
    junkA = C.sb("junkA", [128, 128], BF16)
    QT1 = C.sb("L1QT", [128, 8, 128], BF16)
    KT1 = C.sb("L1KT", [128, 2, 128], BF16)
    V1 = C.sb("L1V", [128, 2, 130], BF16)
    S.op("pool", lambda e: e.memset(V1[:], 1.0), w=["V1"])
    sg1 = C.sb("sg1", [128, 1024], BF16)
    kmacc = C.sb("kmacc", [128, 2, 8])
    kmt = C.sb("kmt", [128, 2])
    pT = nb["pT"]

    def xsrc(i):
        return io["x_halo"] if i == 0 else io["x"][(i - 1) * 128:i * 128, :]

    def X(i):
        ti = i - 1
        if i + 1 <= NT:
            _load(C, xt[(i + 1) % 3][:], xsrc(i + 1), ("xt", (i + 1) % 3))
        xk = ("xt", i % 3)
        xti = xt[i % 3]
        _norm_transpose(C, xti[:], xk, nb, ident, "n0", grep=W["g0"])
        yield
        xnT = nb["xnT"]
        rk = lambda k: ["n0xnT", ("wE", k), ("wE0", k), ("wE1", k), ("wE2", k)]
        ogc = og[ti % 2] if ti >= 0 else None
        oka, okb = ("og", ti % 2, "a"), ("og", ti % 2, "b")

        def chain_s():
            pM, pMk = pp.alloc()
            _mm8(C, pM[:, 0:128], lambda k: xnT[:, k, :], lambda k: wb[:, k, 2192:2320], rk, pMk)
            _mm8(C, pM[:, 128:256], lambda k: wb[:, k, 2064:2192], lambda k: xnT[:, k, :], rk, pMk)
            va, kt_ = VA[i % 2], KT[i % 2]
            S.op("dve", lambda e: e.tensor_copy(out=va[:, :, 0:64], in_=pM[:, 0:128].rearrange("p (k d) -> p k d", k=2)),
                 r=[pMk], w=[("VA", i % 2)])
            S.op("dve", lambda e: e.tensor_copy(out=kt_[:], in_=pM[:, 128:256]), r=[pMk], w=[("KT", i % 2)])
            pp.free(pMk)
            yield
            if ti < 0:
                return
            pQ, pQk = pp.alloc()
            for a in range(4):
                _mm8(C, pQ[:, a * 128:(a + 1) * 128], lambda k, a=a: wq[:, k, a * 128:(a + 1) * 128],
                     lambda k: xnT[:, k, :], lambda k, a=a: ["n0xnT", ("wq", k, a)], pQk)
            S.op("act", lambda e: e.activation(out=QT[:], in_=pQ[:, 0:512].rearrange("p (a t) -> p a t", a=4), func=AF.Copy, scale=0.125),
                 r=[pQk], w=["QT"])
            pp.free(pQk)
            yield
            pG, pGk = pp.alloc()
            _mm8(C, pG[:, 0:512], lambda k: xnT[:, k, :], lambda k: wb[:, k, 2320:2832], rk, pGk)
            _silu(C, sgB[:], pG[:, 0:512], pGk, stmp[:], "stmp", "sgB")
            pp.free(pGk)
            yield
            for kv in range(2):
                for kt in range(2):
                    ktile = KT[(i - 1 + kt) % 2]
                    pS, pSk = pp.alloc()
                    S.op("pe", lambda e, ktile=ktile, kv=kv, pS=pS: e.matmul(pS[:, 0:512], lhsT=ktile[kv * 64:(kv + 1) * 64, :],
                                                                            rhs=QT[kv * 64:(kv + 1) * 64, :, :], start=True, stop=True),
                         r=[("KT", (i - 1 + kt) % 2), "QT"], w=[pSk])
                    S.op("act", lambda e, pS=pS: e.activation(out=Ee[:], in_=pS[:, 0:512], func=AF.Exp), r=[pSk], w=["Ee"])
                    pp.free(pSk)
                    if kt == 0:
                        tab = EBfirst[:, kv * 4:(kv + 1) * 4, :] if ti == 0 else EB[:, 1, kv * 4:(kv + 1) * 4, :]
                        tk = "EBfirst" if ti == 0 else "EB"
                    else:
                        tab = EB[:, 0, kv * 4:(kv + 1) * 4, :]
                        tk = "EB"
                    S.op("dve", lambda e, kt=kt, tab=tab: e.tensor_tensor(out=Pt[kt][:].rearrange("p (a t) -> p a t", a=4),
                                                                          in0=Ee[:].rearrange("p (a t) -> p a t", a=4), in1=tab, op=ALU.mult),
                         r=["Ee", tk], w=[("Pt", kt)])
                    yield
                pO, pOk = pp.alloc()
                for a in range(4):
                    for kt in range(2):
                        vat = VA[(i - 1 + kt) % 2]
                        S.op("pe", lambda e, a=a, kt=kt, vat=vat, pO=pO, kv=kv: e.matmul(
                            pO[:, a * 65:(a + 1) * 65], lhsT=Pt[kt][:, a * 128:(a + 1) * 128], rhs=vat[:, kv, 0:65],
                            start=(kt == 0), stop=(kt == 1)), r=[("Pt", kt), ("VA", (i - 1 + kt) % 2)], w=[pOk])
                pO3 = pO[:, 0:260].rearrange("p (a d) -> p a d", a=4)
                S.op("dve", lambda e, pO3=pO3, kv=kv: e.tensor_tensor(out=den[:], in0=pO3[:, :, 64], in1=sinkexp[:, kv * 4:(kv + 1) * 4], op=ALU.add),
                     r=[pOk, "sinkexp"], w=["den"])
                S.op("dve", lambda e: e.reciprocal(out=rden[:], in_=den[:]), r=["den"], w=["rden"])
                for a in range(4):
                    h = kv * 4 + a
                    S.op("dve", lambda e, a=a, h=h, pO3=pO3: e.scalar_tensor_tensor(
                        out=ogc[:, 512 + h * 64:512 + (h + 1) * 64], in0=pO3[:, a, 0:64], scalar=rden[:, a:a + 1],
                        in1=sgB[:, h * 64:(h + 1) * 64], op0=ALU.mult, op1=ALU.mult), r=[pOk, "rden", "sgB"], w=[okb])
                pp.free(pOk)
                yield

        def chain_g():
            if ti < 0:
                return
            pD, pDk = pp.alloc()
            _mm8(C, pD[0:16, 0:128], lambda k: wb[:, k, 1024:1040], lambda k: xnT[:, k, :], rk, pDk)
            S.op("dve", lambda e: e.tensor_copy(out=G["adT"][:], in_=pD[0:16, 0:128]), r=[pDk], w=["adT"])
            pp.free(pDk)
            pV, pVk = pp.alloc()
            _mm8(C, pV[:, 0:512], lambda k: xnT[:, k, :], lambda k: wb[:, k, 512:1024], rk, pVk)
            S.op("act", lambda e: e.copy(out=vbf[:], in_=pV[:, 0:512]), r=[pVk], w=["vbf"])
            pp.free(pVk)
            yield
            pA, pAk = pp.alloc()
            _mm8(C, pA[:, 0:512], lambda k: xnT[:, k, :], lambda k: wb[:, k, 0:512], rk, pAk)
            yield
            gla = _gla_tile(C, G, ti, pA, pAk, vbf, "vbf", "adT", True)
            res = None
            step = 0
            while True:
                try:
                    next(gla)
                except StopIteration as stop:
                    res = stop.value
                    break
                step += 1
                if step == 2:
                    pG2, pG2k = pp.alloc()
                    _mm8(C, pG2[:, 0:512], lambda k: xnT[:, k, :], lambda k: wb[:, k, 1040:1552], rk, pG2k)
                    _silu(C, sgA[:], pG2[:, 0:512], pG2k, stmp2[:], "stmp2", "sgA")
                    pp.free(pG2k)
                    S.op("pool", lambda e: e.tensor_tensor(out=gsg[:], in0=sgA[:], in1=ggain[:], op=ALU.mult), r=["sgA", "ggain"], w=["gsg"])
                yield
            po, pok = res
            pp.free(pAk)
            for h in range(4):
                S.op("act", lambda e, h=h: e.activation(out=junkA[:], in_=po[:, h * 128:(h + 1) * 128], func=AF.Square,
                                                        accum_out=ssA[:, h:h + 1]), r=[pok], w=["junkA", "ssA"])
            S.op("act", lambda e: e.activation(out=ssA[:], in_=ssA[:], func=AF.Ln, scale=1.0 / 128, bias=EPS), r=["ssA"], w=["ssA"])
            S.op("act", lambda e: e.activation(out=rsA[:], in_=ssA[:], func=AF.Exp, scale=-0.5), r=["ssA"], w=["rsA"])
            for h in range(4):
                S.op("dve", lambda e, h=h: e.scalar_tensor_tensor(
                    out=ogc[:, h * 128:(h + 1) * 128], in0=po[:, h * 128:(h + 1) * 128], scalar=rsA[:, h:h + 1],
                    in1=gsg[:, h * 128:(h + 1) * 128], op0=ALU.mult, op1=ALU.mult), r=[pok, "rsA", "gsg"], w=[oka])
            pp.free(pok)
            yield

        gens = [chain_g(), chain_s()]
        while gens:
            for g in list(gens):
                try:
                    next(g)
                except StopIteration:
                    gens.remove(g)
                yield

    def Y(ti):
        i = ti + 1
        xk = ("xt", i % 3)
        xti = xt[i % 3]
        ogc = og[ti % 2]
        oka, okb = ("og", ti % 2, "a"), ("og", ti % 2, "b")
        for k in range(8):
            S.op("pe", lambda e, k=k: e.transpose(out=pT[:, k, :], in_=ogc[:, k * 128:(k + 1) * 128], identity=ident[:]),
                 r=[oka, okb, "c_ident"], w=["pTbank"])
        S.op("act", lambda e: e.copy(out=ogT[:], in_=pT[:]), r=["pTbank"], w=["ogT"])
        yield
        for nbk in range(2):
            pH, pHk = pp.alloc()
            _mm8(C, pH[:, 0:512], lambda k: ogT[:, k, :], lambda k, nbk=nbk: wo[:, k, nbk * 512:(nbk + 1) * 512],
                 lambda k: ["ogT", ("wO", k)], pHk)
            S.op("dve", lambda e, nbk=nbk, pH=pH: e.tensor_tensor(out=h1t[:, nbk * 512:(nbk + 1) * 512], in0=pH[:, 0:512],
                                                                  in1=xti[:, nbk * 512:(nbk + 1) * 512], op=ALU.add),
                 r=[pHk, xk], w=["h1t"])
            pp.free(pHk)
            yield
        S.op("sp", lambda e: e.dma_start(out=io["o_h1"][ti * 128:(ti + 1) * 128, :], in_=h1t[:]), r=["h1t"], dma=True, final=True)
        _norm_transpose(C, h1t[:], "h1t", nb1, ident, "n1", grep=W["g1"])
        yield
        x1T = nb1["xnT"]
        rk1 = lambda k: ["n1xnT", ("w1", k)]
        for half in range(2):
            pq, pqk = pp.alloc()
            for a in range(4):
                h = half * 4 + a
                _mm8(C, pq[:, a * 128:(a + 1) * 128], lambda k, h=h: w1[:, k, h * 128:(h + 1) * 128], lambda k: x1T[:, k, :], rk1, pqk)
                yield
            S.op("act", lambda e, half=half, pq=pq: e.activation(out=QT1[:, half * 4:(half + 1) * 4, :],
                                                                 in_=pq[:, 0:512].rearrange("p (a t) -> p a t", a=4),
                                                                 func=AF.Copy, scale=128.0 ** -0.5), r=[pqk], w=["QT1"])
            pp.free(pqk)
        S.op("sp", lambda e: e.dma_start(out=io["o_qT"][ti], in_=QT1[:].rearrange("p h t -> p (h t)")), r=["QT1"], dma=True, final=True)
        pk, pkk = pp.alloc()
        for kv in range(2):
            _mm8(C, pk[:, kv * 128:(kv + 1) * 128], lambda k, kv=kv: w1[:, k, 1024 + kv * 128:1024 + (kv + 1) * 128],
                 lambda k: x1T[:, k, :], rk1, pkk)
        _mm8(C, pk[:, 256:512], lambda k: x1T[:, k, :], lambda k: w1[:, k, 1280:1536], rk1, pkk)
        S.op("dve", lambda e: e.tensor_copy(out=KT1[:], in_=pk[:, 0:256].rearrange("p (k t) -> p k t", k=2)), r=[pkk], w=["KT1"])
        S.op("dve", lambda e: e.tensor_copy(out=V1[:, :, 0:128], in_=pk[:, 256:512].rearrange("p (k d) -> p k d", k=2)),
             r=[pkk], w=["V1"])
        pp.free(pkk)
        yield
        S.op("sp", lambda e: e.dma_start(out=io["o_K"][:, :, ti * 128:(ti + 1) * 128], in_=KT1[:]), r=["KT1"], dma=True, final=True)
        S.op("sp", lambda e: e.dma_start(out=io["o_V"][ti * 128:(ti + 1) * 128, :].rearrange("p (k d) -> p k d", k=2), in_=V1[:, :, 0:128]),
             r=["V1"], dma=True, final=True)
        for half in range(2):
            pg, pgk = pp.alloc()
            _mm8(C, pg[:, 0:512], lambda k: x1T[:, k, :], lambda k, half=half: w1[:, k, 1536 + half * 512:1536 + (half + 1) * 512], rk1, pgk)
            _silu(C, sg1[:, half * 512:(half + 1) * 512], pg[:, 0:512], pgk, stmp1[:], "stmp1", "sg1")
            pp.free(pgk)
            yield
        S.op("sp", lambda e: e.dma_start(out=io["o_sg"][ti * 128:(ti + 1) * 128, :], in_=sg1[:]), r=["sg1"], dma=True, final=True)

    def interleave(*gens):
        gens = [g for g in gens if g is not None]
        while gens:
            for g in list(gens):
                try:
                    next(g)
                except StopIteration:
                    gens.remove(g)

    _load(C, xt[0][:], xsrc(0), ("xt", 0))
    interleave(X(0))
    interleave(X(1))
    def spread(y, x, every=2):
        if x is None:
            interleave(y)
            return
        ydone = xdone = False
        while not (ydone and xdone):
            for _ in range(every):
                if not xdone:
                    try:
                        next(x)
                    except StopIteration:
                        xdone = True
            if not ydone:
                try:
                    next(y)
                except StopIteration:
                    ydone = True

    for ti in range(NT):
        spread(Y(ti), X(ti + 2) if ti + 2 <= NT else None)


def phase_c(C, io, T, gather2):
    S = C.S
    K = _load_consts(C, ["c_ident"])
    ident = K["c_ident"]
    pp = PsPool(C, 7, "pp")
    flags = C.sb("flags", [128, 16])
    _load(C, flags[:], io["t_flags"], "flags")
    tval = C.sb("tval", [128, 512]); tadd = C.sb("tadd", [128, 512]); tprev = C.sb("tprev", [128, 512])
    _load(C, tval[:], io["t_valid"], "tval")
    _load(C, tadd[:], io["t_addm"], "tadd")
    _load(C, tprev[:], io["t_prev"], "tprev")
    fgain = C.sb("fgain", [128, 1024])
    _load(C, fgain[:], io["fgain_rep"], "fgain")
    ecrep, EB, Dc = T["ecrep"], T["EB"], T["Dc"]
    wo = C.sb("wO2", [128, 8, 1024], BF16)
    KTall = C.sb("KTall", [128, 2, 8192], BF16)
    Vall = C.sb("Vall", [128, 64, 260], BF16)
    KTloc = C.sb("KTloc", [128, 2, 2048], BF16)
    Vloc = C.sb("Vloc", [128, 16, 260], BF16)
    kmT = C.sb("kmT", [128, 2, 32], BF16)
    Kprev = C.sb("Kprev", [128, 2, 128], BF16)
    Vprev = C.sb("Vprev", [128, 2, 130], BF16)
    S.op("dve", lambda e: e.memset(Vall[:], 1.0), w=[("Vall", sp, kv) for sp in range(4) for kv in range(2)])
    S.op("dve", lambda e: e.memset(Vloc[:], 1.0), w=[("Vloc", 0), ("Vloc", 1)])
    _load(C, KTloc[:], io["k_loc"], "KTloc")
    gather2()
    for sp in range(4):
        S.op("sp", lambda e, sp=sp: e.dma_start(out=KTall[:, :, sp * 2048:(sp + 1) * 2048], in_=io["k_all"][sp]),
             r=["ccK"], w=[("KTall", sp)], dma=True)
    for sp in range(4):
        for kv in range(2):
            C.S.op("sp", lambda e, sp=sp, kv=kv: e.dma_start(out=Vall[:, sp * 16:(sp + 1) * 16, kv * 130:kv * 130 + 128],
                                                             in_=io["v_all"][sp][:, kv * 128:(kv + 1) * 128].rearrange("(t p) d -> p t d", p=128)),
                   r=["ccV"], w=[("Vall", sp, kv)], dma=True)
    for kv in range(2):
        C.S.op("sp", lambda e, kv=kv: e.dma_start(out=Vloc[:, :, kv * 130:kv * 130 + 128],
                                                  in_=io["v_loc"][:, kv * 128:(kv + 1) * 128].rearrange("(t p) d -> p t d", p=128)),
               w=[("Vloc", kv)], dma=True)
    if True:
        for k in range(8):
            S.op("pool", lambda e, k=k: e.dma_start(out=wo[:, k, :], in_=io["w_out_odd"][k * 128:(k + 1) * 128, :]), w=[("wO2", k)], dma=True)

        kmf = C.sb("kmf", [128, 2, 32])
        for sp in range(4):
            for kv in range(2):
                S.op("dve", lambda e, sp=sp, kv=kv: e.tensor_reduce(
                    out=kmf[:, kv, sp * 8:(sp + 1) * 8], in_=KTall[:, kv, sp * 2048:(sp + 1) * 2048].rearrange("p (b t) -> p b t", b=8),
                    axis=AX.X, op=ALU.add), r=[("KTall", sp)], w=[("kmf", sp, kv)])
        S.op("dve", lambda e: e.tensor_scalar(out=kmT[:], in0=kmf[:], scalar1=1.0 / 256, scalar2=None, op0=ALU.mult),
             r=[("kmf", sp, kv) for sp in range(4) for kv in range(2)], w=["kmT"])
        for sp in range(3):
            ksrc = KTall[:, :, (16 * sp + 15) * 128:(16 * sp + 16) * 128]
            vsrc = Vall[:, 16 * sp + 15, :].rearrange("p (k d) -> p k d", k=2)
            m = flags[:, 8 + sp:9 + sp]
            if sp == 0:
                S.op("dve", lambda e, ksrc=ksrc, m=m: e.tensor_scalar(out=Kprev[:], in0=ksrc, scalar1=m, scalar2=None, op0=ALU.mult),
                     r=[("KTall", sp), "flags"], w=["Kprev"])
                S.op("dve", lambda e, vsrc=vsrc, m=m: e.tensor_scalar(out=Vprev[:], in0=vsrc, scalar1=m, scalar2=None, op0=ALU.mult),
                     r=[("Vall", sp, 0), ("Vall", sp, 1), "flags"], w=["Vprev"])
            else:
                S.op("dve", lambda e, ksrc=ksrc, m=m: e.scalar_tensor_tensor(out=Kprev[:], in0=ksrc, scalar=m, in1=Kprev[:],
                                                                            op0=ALU.mult, op1=ALU.add), r=[("KTall", sp), "flags", "Kprev"], w=["Kprev"])
                S.op("dve", lambda e, vsrc=vsrc, m=m: e.scalar_tensor_tensor(out=Vprev[:], in0=vsrc, scalar=m, in1=Vprev[:],
                                                                            op0=ALU.mult, op1=ALU.add), r=[("Vall", sp, 0), ("Vall", sp, 1), "flags", "Vprev"], w=["Vprev"])

    QT1 = [C.sb(f"QT1_{i}", [128, 8, 128], BF16) for i in range(3)]
    sgt = [C.sb(f"sgt{i}", [128, 1024], BF16) for i in range(4)]
    h1t = [C.sb(f"h1t{i}", [128, 1024]) for i in range(4)]
    accs = [C.sb(f"acc{i}", [128, 8, 130]) for i in range(2)]
    Pt = [C.sb(f"Pt{i}", [128, 512], BF16) for i in range(6)]
    Ee = [C.sb(f"Ee{i}", [128, 512], BF16) for i in range(2)]
    gms = [C.sb(f"gm{i}", [128, 8, 32]) for i in range(2)]
    top8s = [C.sb(f"top8{i}", [128, 8, 8]) for i in range(2)]
    sels = [C.sb(f"sel{i}", [128, 8, 32]) for i in range(2)]
    selws = [C.sb(f"selw{i}", [128, 8, 32]) for i in range(2)]
    tmps = [C.sb(f"tmpsel{i}", [128, 8, 32]) for i in range(2)]
    selps = [C.sb(f"selp{i}", [128, 8]) for i in range(2)]
    rden = C.sb("rden", [128, 8])
    og = C.sb("og", [128, 1024], BF16); ogT = C.sb("ogT", [128, 8, 128], BF16)
    pT = C.ps("pT", [128, 8, 128], BF16)
    h2 = C.sb("h2", [128, 1024]); junk = C.sb("junk", [128, 1024], BF16)
    ss = C.sb("ss", [128, 1]); rstd = C.sb("rstd", [128, 1]); outt = C.sb("outt", [128, 1024])
    cnt = {"pt": 0, "ee": 0}

    def ld(t):
        _load(C, QT1[t % 3][:].rearrange("p h t -> p (h t)"), io["q_t"][t], ("QT1", t % 3))
        _load(C, sgt[t % 4][:], io["sg"][t * 128:(t + 1) * 128, :], ("sgt", t % 4))
        _load(C, h1t[t % 4][:], io["h1"][t * 128:(t + 1) * 128, :], ("h1t", t % 4))

    def tile(t):
        if t + 1 < NT:
            ld(t + 1)
        q = QT1[t % 3]
        qk = ("QT1", t % 3)
        p = t % 2
        acc, gm, top8, sel, selw, tmp, selp = accs[p], gms[p], top8s[p], sels[p], selws[p], tmps[p], selps[p]
        kq = lambda n: (n, p)
        pg, pgk = pp.alloc()
        for h in range(8):
            S.op("pe", lambda e, h=h: e.matmul(pg[:, h * 32:(h + 1) * 32], lhsT=q[:, h, :], rhs=kmT[:, h // 4, :], start=True, stop=True),
                 r=[qk, "kmT"], w=[pgk])
        S.op("dve", lambda e: e.tensor_tensor(out=gm[:], in0=pg[:, 0:256].rearrange("p (h n) -> p h n", h=8),
                                              in1=_bcast_mid(tadd[:, t * 32:(t + 1) * 32], 8), op=ALU.add), r=[pgk, "tadd"], w=[kq("gm")])
        pp.free(pgk)
        for h in range(8):
            S.op("dve", lambda e, h=h: e.max(out=top8[:, h, :], in_=gm[:, h, :]), r=[kq("gm")], w=[("top8", p, h)])
            S.op("dve", lambda e, h=h: e.tensor_scalar(out=sel[:, h, :], in0=gm[:, h, :], scalar1=top8[:, h, 2:3], scalar2=None, op0=ALU.is_ge),
                 r=[kq("gm"), ("top8", p, h)], w=[("sel", p, h)])
        selk = [("sel", p, h) for h in range(8)]
        S.op("dve", lambda e: e.tensor_tensor(out=sel[:], in0=sel[:], in1=_bcast_mid(tval[:, t * 32:(t + 1) * 32], 8), op=ALU.mult),
             r=selk + ["tval"], w=selk)
        S.op("dve", lambda e: e.tensor_tensor(out=tmp[:], in0=sel[:], in1=_bcast_mid(tprev[:, t * 32:(t + 1) * 32], 8), op=ALU.mult),
             r=selk + ["tprev"], w=[kq("tmpsel")])
        S.op("dve", lambda e: e.tensor_reduce(out=selp[:], in_=tmp[:], axis=AX.X, op=ALU.add), r=[kq("tmpsel")], w=[kq("selp")])
        for h in range(8):
            S.op("dve", lambda e, h=h: e.tensor_scalar(out=selw[:, h, :], in0=sel[:, h, :], scalar1=ecrep[:, h:h + 1], scalar2=None, op0=ALU.mult),
                 r=selk + ["ecrep"], w=[kq("selw")])
        S.op("pool", lambda e: e.memset(acc[:], 0.0), w=[("acc", p, h) for h in range(8)])
        yield "G"

        def stage1(u):
            keys, table = u["keys"], u["table"]
            kv = u["kv"]
            u["pt"] = []
            for j, (kap, kkey, _, _) in enumerate(keys):
                pS, pSk = pp.alloc()
                S.op("pe", lambda e, kap=kap, pS=pS: e.matmul(pS[:, 0:512], lhsT=kap, rhs=q[:, kv * 4:(kv + 1) * 4, :], start=True, stop=True),
                     r=[kkey, qk], w=[pSk])
                pi = cnt["pt"] % 6
                cnt["pt"] += 1
                if table is None:
                    S.op("act", lambda e, pS=pS, pi=pi: e.activation(out=Pt[pi][:], in_=pS[:, 0:512], func=AF.Exp), r=[pSk], w=[("Pt", pi)])
                else:
                    ei = cnt["ee"] % 2
                    cnt["ee"] += 1
                    S.op("act", lambda e, pS=pS, ei=ei: e.activation(out=Ee[ei][:], in_=pS[:, 0:512], func=AF.Exp), r=[pSk], w=[("Ee", ei)])
                    tab, tkey = table[j]
                    S.op("dve", lambda e, pi=pi, ei=ei, tab=tab: e.tensor_tensor(out=Pt[pi][:].rearrange("p (a t) -> p a t", a=4),
                                                                                in0=Ee[ei][:].rearrange("p (a t) -> p a t", a=4), in1=tab, op=ALU.mult),
                         r=[("Ee", ei), tkey], w=[("Pt", pi)])
                pp.free(pSk)
                u["pt"].append(pi)

        def stage2(u):
            kv, keys, wf = u["kv"], u["keys"], u["w"]
            banks = [pp.alloc(), pp.alloc()]
            for a in range(4):
                pO, pOk = banks[a // 2]
                col = (a % 2) * 129
                for j, (_, _, vap, vkey) in enumerate(keys):
                    pi = u["pt"][j]
                    S.op("pe", lambda e, pO=pO, col=col, pi=pi, vap=vap, a=a, j=j: e.matmul(
                        pO[:, col:col + 129], lhsT=Pt[pi][:, a * 128:(a + 1) * 128], rhs=vap[:, kv, 0:129],
                        start=(j == 0), stop=(j == len(keys) - 1)), r=[("Pt", pi), vkey], w=[pOk])
            for a in range(4):
                pO, pOk = banks[a // 2]
                col = (a % 2) * 129
                h = kv * 4 + a
                sc, sck = wf(h)
                S.op("dve", lambda e, pO=pO, col=col, h=h, sc=sc: e.scalar_tensor_tensor(
                    out=acc[:, h, 0:129], in0=pO[:, col:col + 129], scalar=sc, in1=acc[:, h, 0:129], op0=ALU.mult, op1=ALU.add),
                    r=[pOk, ("acc", p, h)] + sck, w=[("acc", p, h)])
            pp.free(banks[0][1])
            pp.free(banks[1][1])

        units = []
        for n in range(min(32, 24 + t // 2)):
            sp = n // 8
            for kv in range(2):
                keys = []
                for kt in range(2):
                    g = 2 * n + kt
                    keys.append((KTall[:, kv, g * 128:(g + 1) * 128], ("KTall", sp),
                                 Vall[:, g, :].rearrange("p (k d) -> p k d", k=2), ("Vall", sp, kv)))
                units.append({"kv": kv, "keys": keys, "table": None,
                              "w": (lambda h, n=n: (selw[:, h, n:n + 1], [kq("selw")]))})
        for kv in range(2):
            keys, table = [], []
            if t % 2 == 1:
                keys.append((KTloc[:, kv, (t - 1) * 128:t * 128], "KTloc", Vloc[:, t - 1, :].rearrange("p (k d) -> p k d", k=2), ("Vloc", kv)))
                table.append((EB[:, 3, kv * 4:(kv + 1) * 4, :], "EB"))
            keys.append((KTloc[:, kv, t * 128:(t + 1) * 128], "KTloc", Vloc[:, t, :].rearrange("p (k d) -> p k d", k=2), ("Vloc", kv)))
            table.append((EB[:, 2, kv * 4:(kv + 1) * 4, :], "EB"))
            units.append({"kv": kv, "keys": keys, "table": table, "w": (lambda h: (1.0, []))})
        if t % 2 == 0:
            for kv in range(2):
                if t == 0:
                    keys = [(Kprev[:, kv, :], "Kprev", Vprev, "Vprev")]
                else:
                    keys = [(KTloc[:, kv, (t - 1) * 128:t * 128], "KTloc", Vloc[:, t - 1, :].rearrange("p (k d) -> p k d", k=2), ("Vloc", kv))]
                units.append({"kv": kv, "keys": keys, "table": [(Dc[:, kv * 4:(kv + 1) * 4, :], "Dc")],
                              "w": (lambda h: (selp[:, h:h + 1], [kq("selp")]))})
        stage1(units[0])
        stage1(units[1])
        for ui in range(len(units)):
            if ui + 2 < len(units):
                stage1(units[ui + 2])
            stage2(units[ui])
            yield "U"
        yield "UE"

        S.op("dve", lambda e: e.reciprocal(out=rden[:], in_=acc[:, :, 128]), r=[("acc", p, h) for h in range(8)], w=["rden"])
        sg_ = sgt[t % 4]
        for h in range(8):
            S.op("dve", lambda e, h=h: e.scalar_tensor_tensor(out=og[:, h * 128:(h + 1) * 128], in0=acc[:, h, 0:128], scalar=rden[:, h:h + 1],
                                                              in1=sg_[:, h * 128:(h + 1) * 128], op0=ALU.mult, op1=ALU.mult),
                 r=[("acc", p, h), "rden", ("sgt", t % 4)], w=[("og", h)])
        yield "E"
        for k in range(8):
            S.op("pe", lambda e, k=k: e.transpose(out=pT[:, k, :], in_=og[:, k * 128:(k + 1) * 128], identity=ident[:]),
                 r=[("og", k), "c_ident"], w=["pTbank"])
        S.op("act", lambda e: e.copy(out=ogT[:], in_=pT[:]), r=["pTbank"], w=["ogT"])
        yield "E"
        hh = h1t[t % 4]
        for nbk in range(2):
            pH, pHk = pp.alloc()
            _mm8(C, pH[:, 0:512], lambda k: ogT[:, k, :], lambda k, nbk=nbk: wo[:, k, nbk * 512:(nbk + 1) * 512],
                 lambda k: ["ogT", ("wO2", k)], pHk)
            S.op("dve", lambda e, nbk=nbk, pH=pH: e.tensor_tensor(out=h2[:, nbk * 512:(nbk + 1) * 512], in0=pH[:, 0:512],
                                                                  in1=hh[:, nbk * 512:(nbk + 1) * 512], op=ALU.add),
                 r=[pHk, ("h1t", t % 4)], w=["h2"])
            pp.free(pHk)
            yield "E"
        S.op("act", lambda e: e.activation(out=junk[:], in_=h2[:], func=AF.Square, accum_out=ss[:]), r=["h2"], w=["junk", "ss"])
        S.op("act", lambda e: e.activation(out=ss[:], in_=ss[:], func=AF.Ln, scale=1.0 / 1024, bias=EPS), r=["ss"], w=["ss"])
        S.op("act", lambda e: e.activation(out=rstd[:], in_=ss[:], func=AF.Exp, scale=-0.5), r=["ss"], w=["rstd"])
        S.op("dve", lambda e: e.scalar_tensor_tensor(out=outt[:], in0=h2[:], scalar=rstd[:], in1=fgain[:], op0=ALU.mult, op1=ALU.mult),
             r=["h2", "rstd", "fgain"], w=["outt"])
        S.op("sp", lambda e, t=t: e.dma_start(out=io["out"][t * 128:(t + 1) * 128, :], in_=outt[:]), r=["outt"], dma=True, final=True)

    ld(0)
    gens = [tile(t) for t in range(NT)]
    done_gate = set()

    def run_until(g, marks):
        while True:
            try:
                r = next(g)
            except StopIteration:
                return None
            if r in marks:
                return r

    run_until(gens[0], ("G",))
    done_gate.add(0)
    for t in range(NT):
        side = []
        if t >= 1:
            side.append(gens[t - 1])
        ui = 0
        while True:
            r = next(gens[t])
            if r == "UE":
                break
            ui += 1
            if side:
                try:
                    next(side[0])
                except StopIteration:
                    side = []
            elif t + 1 < NT and (t + 1) not in done_gate:
                run_until(gens[t + 1], ("G",))
                done_gate.add(t + 1)
        for g in side:
            for _ in g:
                pass
        if t + 1 < NT and (t + 1) not in done_gate:
            run_until(gens[t + 1], ("G",))
            done_gate.add(t + 1)
    for _ in gens[NT - 1]:
        pass


RG = [[0, 1, 2, 3], [4, 5, 6, 7]]


def build_fused():
    nc = bass.Bass("TRN2", target_bir_lowering=False)
    dc = {}
    D = lambda n, s, d=F32: nc.dram_tensor(n, list(s), d).ap()
    DL = lambda n, s, d=F32: nc.dram_tensor(n, list(s), d, addr_space="Local").ap()
    summ, summ_all = D("i_summ", [64, 520]), DL("i_summ_all", [256, 520])
    h1, qT, sg = D("i_h1", [SEG, 1024]), D("i_qT", [NT, 128, 1024], BF16), D("i_sg", [SEG, 1024], BF16)
    Kl, Vl = D("i_K", [128, 2 * SEG], BF16), D("i_V", [SEG, 256], BF16)
    k_all, v_all = DL("i_k_all", [512, 2 * SEG], BF16), DL("i_v_all", [4 * SEG, 256], BF16)
    zscr = D("zscr", [8, 128, 1024])

    with contextlib.ExitStack() as es:
        C = _mk(nc, es, dc)
        S = C.S
        io = {"x": C.din("x", [SEG, 1024]), "x_halo": C.din("x_halo", [128, 1024]),
              "gain0_rep": C.din("gain0_rep", [128, 1024]), "gain1_rep": C.din("gain1_rep", [128, 1024]),
              "w_in_even": C.din("w_in_even", [1024, 2832]), "w_out_even": C.din("w_out_even", [1024, 1024]),
              "w_in_odd": C.din("w_in_odd", [1024, 2560]), "w_out_odd": C.din("w_out_odd", [1024, 1024]),
              "gla_w_up": C.din("gla_w_up", [16, 256]), "gla_b_up": C.din("gla_b_up", [1, 256]),
              "gla_gain_rep": C.din("gla_gain_rep", [128, 512]), "sinks_rep": C.din("sinks_rep", [128, 8]),
              "rel_bias": C.din("rel_bias", [32, 8]), "rb31_rep": C.din("rb31_rep", [128, 8]),
              "fgain_rep": C.din("fgain_rep", [128, 1024]), "t_flags": C.din("t_flags", [128, 16]),
              "t_valid": C.din("t_valid", [128, 512]), "t_addm": C.din("t_addm", [128, 512]), "t_prev": C.din("t_prev", [128, 512]),
              "zscr": zscr,
              "o_summ": summ, "summ_all": summ_all.rearrange("(s p) c -> s p c", s=4),
              "o_h1": h1, "o_qT": qT, "o_sg": sg, "o_K": Kl.rearrange("p (k t) -> p k t", k=2), "o_V": Vl,
              "q_t": qT, "sg": sg, "h1": h1,
              "k_all": k_all.rearrange("(s p) (k t) -> s p k t", s=4, k=2), "v_all": v_all.rearrange("(s t) d -> s t d", s=4),
              "k_loc": Kl.rearrange("p (k t) -> p k t", k=2), "v_loc": Vl,
              "out": nc.dram_tensor("out", [SEG, 1024], F32, kind="ExternalOutput").ap()}

        def gather(pairs):
            for a, b in pairs:
                tok = S.cc(lambda e, a=a, b=b: e.collective_compute("AllGather", ALU.bypass, replica_groups=RG, ins=[a], outs=[b]))
                S.res["cc1"] = [tok, []]

        C.pfx = "T_"
        T = {"EB": C.sb("EBall", [128, 4, 8, 128], BF16), "Dc": C.sb("Dcorr", [128, 8, 128], BF16), "ecrep": C.sb("ecrep", [128, 8])}
        C.pfx = "W_"
        with C.scope():
            W = _load_weights(C, io)
            C.pfx = "A_"
            with C.scope():
                phase_a(C, io, W, T)
            gather([(summ, summ_all)])
            C.pfx = "B_"
            with C.scope():
                phase_b(C, io, W, T)
        def gather2():
            for key, a, b in (("ccK", Kl, k_all), ("ccV", Vl, v_all)):
                tok = S.cc(lambda e, a=a, b=b: e.collective_compute("AllGather", ALU.bypass, replica_groups=RG, ins=[a], outs=[b]))
                S.res[key] = [tok, []]

        C.pfx = "C_"
        with C.scope():
            phase_c(C, io, T, gather2)
        S.emit()
    return nc


_PROGS = {}


def _prog(name, fn):
    if name not in _PROGS:
        _PROGS[name] = fn()
    return _PROGS[name]


def _rep(a, n=128):
    a = np.asarray(a, np.float32)
    return np.ascontiguousarray(np.broadcast_to(a.reshape(1, -1), (n, a.size)))


def kernel(x, norm_gain, final_gain, rel_bias, w_in_even, gla_w_up, gla_b_up, gla_norm_gain,
           swa_sinks, w_out_even, w_in_odd, w_out_odd):
    f = lambda a: np.ascontiguousarray(np.asarray(a, dtype=np.float32))
    x = f(x)
    cores = list(range(NCORES))
    cn = _consts()
    rb = f(rel_bias)
    shared = dict(cn, gain0_rep=_rep(f(norm_gain)[0]), gain1_rep=_rep(f(norm_gain)[1]), w_in_even=f(w_in_even)[0], w_out_even=f(w_out_even)[0], w_in_odd=f(w_in_odd)[0],
                  w_out_odd=f(w_out_odd)[0], gla_w_up=f(gla_w_up)[0], gla_b_up=f(gla_b_up)[0].reshape(1, 256),
                  gla_gain_rep=_rep(np.tile(f(gla_norm_gain)[0], 4)), sinks_rep=_rep(f(swa_sinks)[0]), rel_bias=rb,
                  rb31_rep=_rep(rb[31]), fgain_rep=_rep(f(final_gain)))
    in_maps = []
    for c in cores:
        b, s = c // 4, c % 4
        halo = x[b, s * SEG - 128:s * SEG] if s > 0 else np.zeros((128, 1024), np.float32)
        in_maps.append(dict(shared, x=np.ascontiguousarray(x[b, s * SEG:(s + 1) * SEG]), x_halo=np.ascontiguousarray(halo),
                            **_core_tables(s)))
    res = run_bass_kernel_spmd(_prog("fused", build_fused), in_maps, core_ids=cores).results
    out = np.empty((2, 4 * SEG, 1024), np.float32)
    for c in cores:
        out[c // 4, (c % 4) * SEG:(c % 4 + 1) * SEG] = np.asarray(res[c]["out"])
    return out
```
